# Optimizing a Trainium2 kernel written in Bass

```python
import math
import jax, jax.numpy as jnp
from jax import lax
import numpy as np

D_MODEL = 1024
BATCH = 4
SEQ = 8192
DEPTH = 1

CHUNK = 64
Q_BLOCK = 128
NORM_EPS = 1e-6

DA_HEADS = 4
DA_QK_DIM = 64
DA_V_DIM = 2 * DA_QK_DIM
DA_WIDTH = DA_HEADS * DA_V_DIM
ROPE_THETA = 500000.0
ROT_DIM = DA_QK_DIM // 4

ML_HEADS = 4
ML_DIM = 128
ML_WIDTH = ML_HEADS * ML_DIM
CONV_WIDTH = 4

MIX_WIDTH = DA_WIDTH + ML_WIDTH
D_FF = ((8 * D_MODEL // 3 + 255) // 256) * 256

DA_QK_COLS = DA_HEADS * 2 * DA_QK_DIM
IN_SIZES = (DA_QK_COLS, DA_QK_COLS, DA_WIDTH,
            ML_WIDTH, ML_WIDTH, ML_WIDTH, ML_WIDTH,
            ML_HEADS, ML_HEADS)
IN_WIDTH = sum(IN_SIZES)

kernel_name = "hybrid_diffattn_mlstm_block"


def rms_norm(x, g, eps=NORM_EPS):
    xf = x.astype(jnp.float32)
    y = xf * lax.rsqrt(jnp.mean(xf * xf, axis=-1, keepdims=True) + eps)
    return (y * g.astype(jnp.float32)).astype(x.dtype)


def rope_cos_sin(positions):
    inv_freq = ROPE_THETA ** (-jnp.arange(0, ROT_DIM, 2, dtype=jnp.float32) / ROT_DIM)
    ang = positions.astype(jnp.float32)[..., None] * inv_freq
    return jnp.cos(ang), jnp.sin(ang)


def apply_partial_rope(x, cos, sin):
    half = ROT_DIM // 2
    c = cos[:, :, None, None, :]
    s = sin[:, :, None, None, :]
    x1 = x[..., :half]
    x2 = x[..., half:ROT_DIM]
    return jnp.concatenate([x1 * c - x2 * s, x2 * c + x1 * s, x[..., ROT_DIM:]], axis=-1)


def diff_attention(q, k, v, lam):
    B, S = q.shape[0], q.shape[1]
    nb = S // Q_BLOCK
    scale = DA_QK_DIM ** -0.5
    qb = q.reshape(B, nb, Q_BLOCK, DA_HEADS, 2, DA_QK_DIM).transpose(1, 0, 2, 3, 4, 5)
    key_chunk = jnp.arange(S) // CHUNK

    def block(args):
        q_blk, blk = args
        s = jnp.einsum('bqhmd,bkhmd->bhmqk', q_blk, k) * scale
        q_chunk = (blk * Q_BLOCK + jnp.arange(Q_BLOCK)) // CHUNK
        mask = key_chunk[None, :] <= q_chunk[:, None]
        s = jnp.where(mask, s, -jnp.inf)
        p = jax.nn.softmax(s, axis=-1)
        a = p[:, :, 0] - lam * p[:, :, 1]
        return jnp.einsum('bhqk,bkhe->bqhe', a, v)

    o = lax.map(block, (qb, jnp.arange(nb)))
    return o.transpose(1, 0, 2, 3, 4).reshape(B, S, DA_HEADS, DA_V_DIM)


def mlstm_chunkwise(q, k, v, log_i, log_f):
    B, H, S, D = q.shape
    nc = S // CHUNK
    L = CHUNK

    def to_chunks(a):
        return jnp.moveaxis(a.reshape(a.shape[:2] + (nc, L) + a.shape[3:]), 2, 0)

    xs = (to_chunks(q), to_chunks(k), to_chunks(v), to_chunks(log_i), to_chunks(log_f))
    tril = jnp.tril(jnp.ones((L, L), dtype=bool))

    def step(carry, inp):
        C, n, m = carry
        qc, kc, vc, ic, fc = inp
        b = jnp.cumsum(fc, axis=-1)
        dmat = b[..., :, None] - b[..., None, :] + ic[..., None, :]
        dmat = jnp.where(tril, dmat, -jnp.inf)
        inter = b + m[..., None]
        m_t = jnp.maximum(inter, jnp.max(dmat, axis=-1))
        w_intra = jnp.exp(dmat - m_t[..., None])
        w_inter = jnp.exp(inter - m_t)
        sc = jnp.einsum('bhtd,bhsd->bhts', qc, kc) * w_intra
        num = jnp.einsum('bhts,bhsd->bhtd', sc, vc) \
            + w_inter[..., None] * jnp.einsum('bhtd,bhde->bhte', qc, C)
        nq = jnp.sum(sc, axis=-1) + w_inter * jnp.einsum('bhtd,bhd->bht', qc, n)
        h = num / jnp.maximum(jnp.abs(nq), jnp.exp(-m_t))[..., None]
        bL = b[..., -1]
        g = bL[..., None] - b + ic
        m_new = jnp.maximum(bL + m, jnp.max(g, axis=-1))
        decay = jnp.exp(bL + m - m_new)
        wk = jnp.exp(g - m_new[..., None])
        C_new = decay[..., None, None] * C + jnp.einsum('bhs,bhsd,bhse->bhde', wk, kc, vc)
        n_new = decay[..., None] * n + jnp.einsum('bhs,bhsd->bhd', wk, kc)
        return (C_new, n_new, m_new), h

    init = (jnp.zeros((B, H, D, D), jnp.float32), jnp.zeros((B, H, D), jnp.float32),
            jnp.zeros((B, H), jnp.float32))
    _, hs = lax.scan(step, init, xs)
    return jnp.moveaxis(hs, 0, 2).reshape(B, H, S, D)


def causal_depthwise_conv(x, w, b):
    C = x.shape[-1]
    y = lax.conv_general_dilated(
        x.astype(jnp.float32), w.astype(jnp.float32)[:, None, :],
        window_strides=(1,), padding=[(CONV_WIDTH - 1, 0)],
        dimension_numbers=('NWC', 'WIO', 'NWC'), feature_group_count=C)
    return y + b.astype(jnp.float32)


def setup_inputs(seed: int = 0) -> dict:
    key = jax.random.key(seed)
    ks = jax.random.split(key, 20)
    f32 = jnp.float32
    nrm = lambda k, shape, scale: jax.random.normal(k, shape, f32) * scale
    x = jax.random.normal(ks[0], (BATCH, SEQ, D_MODEL), f32)
    offset = jax.random.randint(ks[1], (BATCH, 1), 0, 64, dtype=jnp.int32) * CHUNK
    positions = offset + jnp.arange(SEQ, dtype=jnp.int32)[None, :]
    gate_b = jnp.stack([
        nrm(ks[2], (DEPTH, ML_HEADS), 0.1),
        jnp.broadcast_to(jnp.linspace(3.0, 6.0, ML_HEADS, dtype=f32), (DEPTH, ML_HEADS))
        + nrm(ks[3], (DEPTH, ML_HEADS), 0.01)], axis=1)
    return {
        "x": x,
        "positions": positions,
        "mix_norm_g": 1.0 + nrm(ks[4], (DEPTH, D_MODEL), 0.02),
        "w_in": nrm(ks[5], (DEPTH, D_MODEL, IN_WIDTH), D_MODEL ** -0.5),
        "da_lambda": nrm(ks[6], (DEPTH, 4, DA_QK_DIM), 0.1),
        "da_subln_g": 1.0 + nrm(ks[7], (DEPTH, DA_V_DIM), 0.02),
        "ml_conv_w": nrm(ks[8], (DEPTH, CONV_WIDTH, 2 * ML_WIDTH), CONV_WIDTH ** -0.5),
        "ml_conv_b": nrm(ks[9], (DEPTH, 2 * ML_WIDTH), 0.01),
        "ml_gate_b": gate_b,
        "ml_norm_g": 1.0 + nrm(ks[10], (DEPTH, ML_WIDTH), 0.02),
        "w_out": nrm(ks[11], (DEPTH, MIX_WIDTH, D_MODEL), MIX_WIDTH ** -0.5),
        "ffn_norm_g": 1.0 + nrm(ks[12], (DEPTH, D_MODEL), 0.02),
        "w_gate": nrm(ks[13], (DEPTH, D_MODEL, D_FF), D_MODEL ** -0.5),
        "w_up": nrm(ks[14], (DEPTH, D_MODEL, D_FF), D_MODEL ** -0.5),
        "w_down": nrm(ks[15], (DEPTH, D_FF, D_MODEL), D_FF ** -0.5),
        "final_norm_g": 1.0 + nrm(ks[16], (D_MODEL,), 0.02),
    }


def reference(x, positions, mix_norm_g, w_in, da_lambda, da_subln_g, ml_conv_w, ml_conv_b,
              ml_gate_b, ml_norm_g, w_out, ffn_norm_g, w_gate, w_up, w_down, final_norm_g):
    B, S, _ = x.shape
    cos, sin = rope_cos_sin(positions)
    split_idx = [sum(IN_SIZES[:i + 1]) for i in range(len(IN_SIZES) - 1)]
    for l in range(DEPTH):
        lam_init = 0.8 - 0.6 * math.exp(-0.3 * l)
        h = rms_norm(x, mix_norm_g[l])
        z = h @ w_in[l]
        (da_q, da_k, da_v, ml_q, ml_k, ml_v, ml_o, ml_i, ml_f) = jnp.split(z, split_idx, axis=-1)

        qa = apply_partial_rope(da_q.astype(jnp.float32).reshape(B, S, DA_HEADS, 2, DA_QK_DIM), cos, sin)
        ka = apply_partial_rope(da_k.astype(jnp.float32).reshape(B, S, DA_HEADS, 2, DA_QK_DIM), cos, sin)
        va = da_v.astype(jnp.float32).reshape(B, S, DA_HEADS, DA_V_DIM)
        lv = da_lambda[l].astype(jnp.float32)
        lam = jnp.exp(jnp.sum(lv[0] * lv[1])) - jnp.exp(jnp.sum(lv[2] * lv[3])) + lam_init
        oa = diff_attention(qa, ka, va, lam)
        oa = rms_norm(oa, da_subln_g[l]) * (1.0 - lam_init)
        attn_out = oa.reshape(B, S, DA_WIDTH).astype(x.dtype)

        qk = jax.nn.silu(causal_depthwise_conv(jnp.concatenate([ml_q, ml_k], axis=-1),
                                               ml_conv_w[l], ml_conv_b[l]))
        to_heads = lambda a: a.reshape(B, S, ML_HEADS, ML_DIM).transpose(0, 2, 1, 3)
        qm = to_heads(qk[..., :ML_WIDTH]) * (ML_DIM ** -0.5)
        km = to_heads(qk[..., ML_WIDTH:])
        vm = to_heads(ml_v.astype(jnp.float32))
        gb = ml_gate_b[l].astype(jnp.float32)
        log_i = (ml_i.astype(jnp.float32) + gb[0]).transpose(0, 2, 1)
        log_f = jax.nn.log_sigmoid(ml_f.astype(jnp.float32) + gb[1]).transpose(0, 2, 1)
        hm = mlstm_chunkwise(qm, km, vm, log_i, log_f)
        hm = rms_norm(hm.transpose(0, 2, 1, 3), ml_norm_g[l].reshape(ML_HEADS, ML_DIM))
        hm = hm * jax.nn.sigmoid(ml_o.astype(jnp.float32)).reshape(B, S, ML_HEADS, ML_DIM)
        mlstm_out = hm.reshape(B, S, ML_WIDTH).astype(x.dtype)

        x = x + jnp.concatenate([attn_out, mlstm_out], axis=-1) @ w_out[l]

        h2 = rms_norm(x, ffn_norm_g[l])
        x = x + (jax.nn.silu(h2 @ w_gate[l]) * (h2 @ w_up[l])) @ w_down[l]
    return rms_norm(x, final_norm_g)
```

```python
import numpy as np
import ml_dtypes
from contextlib import ExitStack
import concourse.bass as bass
import concourse.mybir as mybir
from concourse.bass_utils import run_bass_kernel_spmd

F32 = mybir.dt.float32
BF16 = mybir.dt.bfloat16
I32 = mybir.dt.int32
AF = mybir.ActivationFunctionType
ALU = mybir.AluOpType

ENGS = ("pe", "act", "dve", "pool", "sp")
EPS = 1e-6
DM = 1024
DFF = 2816
INW = 3592
TWO_PI = 2.0 * np.pi


class Buf:
    __slots__ = ("name", "writers", "readers", "psum")

    def __init__(self, name, psum=False):
        self.name = name
        self.writers = {}
        self.readers = []
        self.psum = psum


class Op:
    __slots__ = ("eng", "fn", "deps", "dma", "semkey", "ndma", "signal", "count", "sem", "gidx")

    def __init__(self, eng, fn, dma, semkey, ndma):
        self.eng = eng
        self.fn = fn
        self.deps = set()
        self.dma = dma
        self.semkey = semkey
        self.ndma = ndma
        self.signal = False
        self.count = None
        self.sem = None


class Prog:
    def __init__(self, nc):
        self.nc = nc
        self.ops = []
        self.barrier_deps = set()
        self.last = {}
        self.pending_dma = []
        self.out_dma = []
        self.phase_map = {}

    def add(self, eng, fn, reads=(), writes=(), dma=False, semkey=None, ndma=1, out=False):
        if dma and semkey != "const":
            semkey = "%s%d" % (eng, self.phase_map.setdefault((eng, semkey), len(self.phase_map)))
        op = Op(eng, fn, dma, semkey, ndma)
        op.gidx = len(self.ops)
        deps = set(self.barrier_deps)
        for b in reads:
            deps.update(b.writers.values())
            if b.psum:
                deps.update(r for r in b.readers if r.eng != eng)
        for b in writes:
            deps.update(b.writers.values())
            deps.update(b.readers)
        for b in reads:
            b.readers.append(op)
        for b in writes:
            if b.readers:
                b.writers = {}
                b.readers = []
            b.writers[("dma", op.gidx) if dma else eng] = op
        deps.discard(op)
        op.deps = deps
        self.ops.append(op)
        self.last[eng] = op
        if dma:
            self.pending_dma.append(op)
            if out:
                self.out_dma.append(op)
        return op

    def barrier(self):
        self.barrier_deps = set(self.last.values()) | set(self.pending_dma)
        self.pending_dma = []
        self.phase_map = {}

    @staticmethod
    def needs_sem(op, d):
        if d.dma or op.dma:
            return True
        if d.eng != op.eng:
            return True
        return op.eng != "pe"

    def emit(self, stack):
        nc = self.nc
        for op in self.ops:
            for d in op.deps:
                if self.needs_sem(op, d):
                    d.signal = True
        for op in self.out_dma:
            op.signal = True
        eng_sem = {e: stack.enter_context(nc.semaphore("s_" + e)) for e in ENGS}
        dma_sem, dma_cnt = {}, {}
        eng_cnt = {e: 0 for e in ENGS}
        for op in self.ops:
            if op.dma:
                if op.semkey not in dma_sem:
                    dma_sem[op.semkey] = stack.enter_context(nc.semaphore("d_%s" % (op.semkey,)))
                    dma_cnt[op.semkey] = 0
                dma_cnt[op.semkey] += 16 * op.ndma
                op.sem = dma_sem[op.semkey]
                op.count = dma_cnt[op.semkey]
            elif op.signal:
                eng_cnt[op.eng] += 1
                op.sem = eng_sem[op.eng]
                op.count = eng_cnt[op.eng]
        for op in self.ops:
            if op.dma and op.semkey == "const":
                op.count = dma_cnt["const"]
        self.nsem = len(dma_sem) + len(ENGS)
        block = stack.enter_context(nc.Block())
        by_eng = {e: [o for o in self.ops if o.eng == e] for e in ENGS}
        final_waits = [(o.sem, o.count) for o in self.out_dma]

        def run(engobj, e):
            waited = {}
            for op in by_eng[e]:
                need = {}
                for d in op.deps:
                    if not self.needs_sem(op, d):
                        continue
                    k = id(d.sem)
                    if k not in need or need[k][1] < d.count:
                        need[k] = (d.sem, d.count)
                for k, (s, c) in need.items():
                    if waited.get(k, 0) < c:
                        engobj.wait_ge(s, c)
                        waited[k] = c
                if op.dma:
                    op.fn(engobj, op.sem)
                else:
                    ins = op.fn(engobj)
                    if op.signal:
                        ins.then_inc(op.sem, 1)
            if e == "sp":
                for s, c in final_waits:
                    if waited.get(id(s), 0) < c:
                        engobj.wait_ge(s, c)
                        waited[id(s)] = c

        @block.tensor
        def _(eng):
            run(eng, "pe")

        @block.scalar
        def _(eng):
            run(eng, "act")

        @block.vector
        def _(eng):
            run(eng, "dve")

        @block.gpsimd
        def _(eng):
            run(eng, "pool")

        @block.sync
        def _(eng):
            run(eng, "sp")


class T:
    __slots__ = ("ap", "b")

    def __init__(self, ap, name, psum=False):
        self.ap = ap
        self.b = Buf(name, psum)


class Arena:
    def __init__(self, t):
        self.t = t
        self.off = 0
        self.N = t.shape[1]
        self.n = 0

    def alloc(self, n_elems, dtype, name=None):
        nbytes = n_elems * (2 if dtype == BF16 else 4)
        n32 = (nbytes + 3) // 4
        n32 = (n32 + 7) // 8 * 8
        o = self.off
        self.off += n32
        assert self.off <= self.N, "arena overflow %d > %d (%s)" % (self.off, self.N, name)
        ap = self.t[:, o:o + n32]
        if dtype != F32:
            ap = ap.bitcast(dtype)
        ap = ap[:, 0:n_elems]
        self.n += 1
        return T(ap, name or ("t%d" % self.n))


def build(NGP=8, NGO=8, dbg=False, upto="D"):
    nc = bass.Bass("TRN2", target_bir_lowering=False)
    NTP, NTO = NGP * 4, NGO * 4
    NT = NTP + NTO
    SP, SO = NGP * 512, NGO * 512
    SA = SP + SO
    NKB = NT

    def din(name, shape, dt=F32):
        return nc.dram_tensor(name, list(shape), dt, kind="ExternalInput").ap()

    def dscr(name, shape, dt=BF16):
        kind = "ExternalOutput" if dbg else "Internal"
        return nc.dram_tensor(name, list(shape), dt, kind=kind).ap()

    x_own = din("x_own", [SO + 128, DM])
    x_pre = din("x_pre", [SP, DM])
    pos = din("pos", [1, SA], I32)
    keybias_d = din("keybias", [128, NKB])
    flag_d = din("flag", [128, 1])
    w_in = din("w_in", [DM, INW])
    w_out = din("w_out", [DM, DM])
    w_gate = din("w_gate", [DM, DFF])
    w_up = din("w_up", [DM, DFF])
    w_down = din("w_down", [DFF, DM])
    gmix_d = din("gmix", [128, 8])
    gffn_d = din("gffn", [128, 8])
    lam_d = din("da_lambda", [1, 256])
    subg_d = din("da_subln_g", [1, 128])
    convw_d = din("convw", [128, 8, 4])
    convb_d = din("convb", [128, 8])
    gateb_d = din("ml_gate_b", [1, 8])
    mlg_d = din("ml_norm_g", [1, 512])
    fing_d = din("final_norm_g", [1, DM])
    ident_d = din("ident", [128, 128], BF16)
    rmat_d = din("rmat", [128, 128], BF16)
    causal_d = din("causal", [128, 128], BF16)
    tri_d = din("tri", [128, 128])
    masks_d = din("masks", [128, 4, 1024], BF16)
    invf_d = din("invf", [128, 1])
    sgn_d = din("sgn", [128, 1])
    y_out = nc.dram_tensor("y", [SO, DM], F32, kind="ExternalOutput").ap()

    QT = dscr("QT", [4, 128, SO])
    KT = dscr("KT", [4, 128, SA])
    V1 = dscr("V1", [4, 128, NT, 130])
    MQT = dscr("MQT", [4, 128, SO])
    MKT = dscr("MKT", [4, 128, SO])
    MK = dscr("MK", [128, NT, 512])
    MV1 = dscr("MV1", [128, NT, 4, 130])
    GSO = dscr("GSO", [128, NTO, 512])
    MIXT = dscr("MIXT", [DM, SO])
    QTb, KTb, V1b, MQTb, MKTb, MKb, MV1b, GSOb, MIXTb = [Buf(n) for n in
                                                          "QT KT V1 MQT MKT MK MV1 GSO MIXT".split()]

    st = ExitStack()
    ARENA_N = 53000
    arena_t = st.enter_context(nc.sbuf_tensor("arena", [128, ARENA_N], F32))
    AR = Arena(arena_t)
    psbig = st.enter_context(nc.psum_tensor("psbig", [128, 4096], F32))
    psum = [T(psbig[:, i * 512:(i + 1) * 512], "ps%d" % i, True) for i in range(8)]
    P = Prog(nc)

    def bufs(ts):
        return [t.b if isinstance(t, T) else t for t in ts]

    def mm(out, lhsT, rhs, start, stop, r, w, skip=False):
        P.add("pe", lambda e: e.matmul(out, lhsT=lhsT, rhs=rhs, start=start, stop=stop, skip_group_check=skip),
              reads=bufs(r), writes=bufs(w))

    def tr(out, in_, ident, r, w):
        P.add("pe", lambda e: e.transpose(out=out, in_=in_, identity=ident), reads=bufs(r), writes=bufs(w))

    def act(out, in_, func, r, w, bias=None, scale=None, accum=None, eng="act"):
        kw = {}
        if bias is not None:
            kw["bias"] = bias
        if scale is not None:
            kw["scale"] = scale
        if accum is not None:
            kw["accum_out"] = accum
        P.add("act", lambda e: e.activation(out=out, in_=in_, func=func, **kw), reads=bufs(r), writes=bufs(w))

    def ts(eng, out, in0, s1, s2, op0, op1, r, w):
        if op1 is None:
            P.add(eng, lambda e: e.tensor_scalar(out=out, in0=in0, scalar1=s1, scalar2=None, op0=op0),
                  reads=bufs(r), writes=bufs(w))
        else:
            P.add(eng, lambda e: e.tensor_scalar(out=out, in0=in0, scalar1=s1, scalar2=s2, op0=op0, op1=op1),
                  reads=bufs(r), writes=bufs(w))

    def tt(eng, out, in0, in1, op, r, w):
        P.add(eng, lambda e: e.tensor_tensor(out=out, in0=in0, in1=in1, op=op), reads=bufs(r), writes=bufs(w))

    def stt(out, in0, scalar, in1, op0, op1, r, w, accum=None):
        if accum is None:
            P.add("dve", lambda e: e.scalar_tensor_tensor(out=out, in0=in0, scalar=scalar, in1=in1, op0=op0, op1=op1),
                  reads=bufs(r), writes=bufs(w))
        else:
            P.add("dve", lambda e: e.scalar_tensor_tensor(out=out, in0=in0, scalar=scalar, in1=in1, op0=op0,
                                                          op1=op1, accum_out=accum), reads=bufs(r), writes=bufs(w))

    def cp(eng, out, in_, r, w):
        if eng == "act":
            act(out, in_, AF.Copy, r, w)
        else:
            P.add(eng, lambda e: e.tensor_copy(out=out, in_=in_), reads=bufs(r), writes=bufs(w))

    def recip(out, in_, r, w):
        P.add("dve", lambda e: e.reciprocal(out=out, in_=in_), reads=bufs(r), writes=bufs(w))

    def memset(eng, ap, val, w):
        P.add(eng, lambda e: e.memset(ap, val), writes=bufs(w))

    STQ = "pool"

    def dma(q, out, in_, r, w, key, final=False):
        if q == "pool":
            q = STQ
        P.add(q, lambda e, s: e.dma_start(out=out, in_=in_).then_inc(s, 16), reads=bufs(r), writes=bufs(w),
              dma=True, semkey=key, out=final)

    def dmas(q, pairs, r, w, key, final=False):
        if q == "pool":
            q = STQ
        def fn(e, s):
            for o, i in pairs:
                e.dma_start(out=o, in_=i).then_inc(s, 16)
        P.add(q, fn, reads=bufs(r), writes=bufs(w), dma=True, semkey=key, ndma=len(pairs), out=final)

    def pconst(n, dt, name, src, q="sp"):
        t = AR.alloc(n, dt, name)
        dma(q, t.ap, src, [], [t], "const")
        return t

    ident = pconst(128, BF16, "ident", ident_d[:, :])
    rmat = pconst(128, BF16, "rmat", rmat_d[:, :])
    causal = pconst(128, BF16, "causal", causal_d[:, :])
    tri = pconst(128, F32, "tri", tri_d[:, :])
    invf = pconst(1, F32, "invf", invf_d[:, :])
    sgn = pconst(1, F32, "sgn", sgn_d[:, :])
    keybias = pconst(NKB, F32, "keybias", keybias_d[:, :])
    flag = pconst(1, F32, "flag", flag_d[:, :])
    gmix = pconst(8, F32, "gmix", gmix_d[:, :])
    gffn = pconst(8, F32, "gffn", gffn_d[:, :])
    convw = pconst(32, F32, "convw", convw_d.rearrange("p b j -> p (b j)"))
    convb = pconst(8, F32, "convb", convb_d[:, :])
    gateb = pconst(8, F32, "gateb", gateb_d[0:1, :].partition_broadcast(128))
    mlg = pconst(512, F32, "mlg", mlg_d[0:1, :].partition_broadcast(128))
    fing = pconst(DM, F32, "fing", fing_d[0:1, :].partition_broadcast(128))
    g08 = pconst(128, F32, "g08", subg_d[0:1, :].partition_broadcast(128))
    lamv = pconst(256, F32, "lamv", lam_d[0:1, :].partition_broadcast(128))
    P.barrier()
    ones128 = AR.alloc(128, F32, "ones128")
    memset("dve", ones128.ap, 1.0, [ones128])
    neghalf = AR.alloc(1, F32, "neghalf")
    memset("dve", neghalf.ap, -0.5, [neghalf])
    onecol = AR.alloc(1, F32, "onecol")
    memset("dve", onecol.ap, 1.0, [onecol])
    GP = AR.alloc(NT * 12, F32, "GP")
    GPv = GP.ap.rearrange("p (t c) -> p t c", c=12)
    small = AR.alloc(64, F32, "small")
    lam = T(small.ap[:, 0:1], "lam")
    neglam = T(small.ap[:, 1:2], "neglam")
    s01 = T(small.ap[:, 2:3], "s01")
    s23 = T(small.ap[:, 3:4], "s23")
    junk = AR.alloc(256, F32, "junkc")
    ts("dve", g08.ap, g08.ap, 0.8, None, ALU.mult, None, [g08], [g08])
    stt(junk.ap[:, 0:64], lamv.ap[:, 0:64], 1.0, lamv.ap[:, 64:128], ALU.mult, ALU.mult, [lamv], [junk, s01], accum=s01.ap)
    stt(junk.ap[:, 0:64], lamv.ap[:, 128:192], 1.0, lamv.ap[:, 192:256], ALU.mult, ALU.mult, [lamv], [junk, s23], accum=s23.ap)
    act(s01.ap, s01.ap, AF.Exp, [s01], [s01])
    act(s23.ap, s23.ap, AF.Exp, [s23], [s23])
    tt("dve", lam.ap, s01.ap, s23.ap, ALU.subtract, [s01, s23], [lam])
    ts("dve", lam.ap, lam.ap, 0.2, None, ALU.add, None, [lam], [lam])
    ts("dve", neglam.ap, lam.ap, -1.0, None, ALU.mult, None, [lam], [neglam])
    PERSIST = AR.off
    P.barrier()

    def finish_early():
        dma("sp", y_out[0:128, :], fing.ap, [fing], [], "const", final=True)
        P.emit(st)
        st.close()
        return nc

    if upto == "0":
        return finish_early()
    AR.off = PERSIST
    win = AR.alloc(8 * INW, BF16, "win")
    winv = win.ap.rearrange("p (k c) -> p k c", c=INW)
    TAB = 4096 if max(SP, SO) > 2048 else max(SP, SO)
    Ctab = AR.alloc(max(SP, SO), F32, "Ctab")
    Stab = AR.alloc(max(SP, SO), F32, "Stab")
    ttmp = [AR.alloc(512, F32, "ttmp%d" % i) for i in range(3)]
    tint = AR.alloc(512, I32, "tint")
    A_MARK = AR.off
    wst = [AR.alloc(1796, F32, "wst%d" % i) for i in range(2)]
    n = 0
    for k in range(8):
        for c0 in (0, 1796):
            s = wst[n % 2]
            dma("sp", s.ap, w_in[k * 128:(k + 1) * 128, c0:c0 + 1796], [], [s], s.b.name)
            ts("dve", winv[:, k, c0:c0 + 1796], s.ap, gmix.ap[:, k:k + 1], None, ALU.mult, None,
               [s, gmix], [win])
            n += 1


    def build_tables(t0, ntok):
        for c0 in range(0, ntok, 512):
            cn = min(512, ntok - c0)
            pi_, ang, u = ttmp[0], ttmp[1], ttmp[2]
            dma("sp", tint.ap[:, 0:cn], pos[0:1, t0 + c0:t0 + c0 + cn].partition_broadcast(128), [], [tint], "tint")
            cp("dve", pi_.ap[:, 0:cn], tint.ap[:, 0:cn], [tint], [pi_])
            ts("dve", ang.ap[:, 0:cn], pi_.ap[:, 0:cn], invf.ap, None, ALU.mult, None, [pi_, invf], [ang])
            for tab, shift in ((Stab, 0.0), (Ctab, 0.25)):
                ts("dve", u.ap[:, 0:cn], ang.ap[:, 0:cn], 1.0 / TWO_PI, shift, ALU.mult, ALU.add, [ang], [u])
                cp("dve", tint.ap[:, 0:cn], u.ap[:, 0:cn], [u], [tint])
                cp("dve", pi_.ap[:, 0:cn], tint.ap[:, 0:cn], [tint], [pi_])
                tt("dve", u.ap[:, 0:cn], u.ap[:, 0:cn], pi_.ap[:, 0:cn], ALU.subtract, [u, pi_], [u])
                ts("dve", u.ap[:, 0:cn], u.ap[:, 0:cn], TWO_PI, None, ALU.mult, None, [u], [u])
                ts("dve", u.ap[:, 0:cn], u.ap[:, 0:cn], 3.1415925, -3.1415925, ALU.min, ALU.max, [u], [u])
                act(tab.ap[:, c0:c0 + cn], u.ap[:, 0:cn], AF.Sin, [u], [tab])
            ts("dve", Stab.ap[:, c0:c0 + cn], Stab.ap[:, c0:c0 + cn], sgn.ap, None, ALU.mult, None, [Stab, sgn], [Stab])

    build_tables(0, SP)
    P.barrier()
    if upto == "A0":
        return finish_early()
    AR.off = A_MARK
    xs = [AR.alloc(DM, F32, "xs%d" % i) for i in range(4)]
    hb = [AR.alloc(DM, BF16, "hb%d" % i) for i in range(4)]
    hT = [AR.alloc(8 * 512, BF16, "hT%d" % i) for i in range(2)]
    zc = [AR.alloc(515, F32, "zc%d" % i) for i in range(8)]
    zb = [AR.alloc(512, BF16, "zb%d" % i) for i in range(2)]
    t1 = [AR.alloc(512, F32, "t1_%d" % i) for i in range(2)]
    t2 = [AR.alloc(512, F32, "t2_%d" % i) for i in range(2)]
    yc = [AR.alloc(512, F32, "yc%d" % i) for i in range(2)]
    ost = [AR.alloc(512, BF16, "ost%d" % i) for i in range(3)]
    kst = [AR.alloc(4 * 512, BF16, "kst%d" % i) for i in range(2)]
    vst = [AR.alloc(520, BF16, "vst%d" % i) for i in range(4)]
    ktk = [AR.alloc(512, BF16, "ktk%d" % i) for i in range(2)]
    ef = [AR.alloc(512, F32, "ef%d" % i) for i in range(2)]
    gst = [AR.alloc(512, BF16, "gst%d" % i) for i in range(2)]
    ssA = [AR.alloc(4, F32, "ssA%d" % i) for i in range(4)]
    gts = [AR.alloc(16, F32, "gts%d" % i) for i in range(4)]
    sqj = AR.alloc(DM, BF16, "sqj")
    for v_ in vst:
        memset("dve", v_.ap, 0.0, [v_])
        memset("dve", v_.ap.rearrange("p (h c) -> p h c", c=130)[:, :, 128:129], 1.0, [v_])
    for z_ in zc:
        memset("dve", z_.ap[:, 0:3], 0.0, [z_])
    PS_TOK = [psum[0], psum[1]]
    PS_FM = [psum[2], psum[3]]
    PS_RZ = psum[4]
    PS_TR = psum[5]
    PS_G = psum[6]
    PS_KT = psum[7]
    cnt = {"x": 0, "tok": 0, "fm": 0, "z": 0, "o": 0, "v": 0, "k": 0, "e": 0, "g": 0, "y": 0, "kst": 0}

    def nxt(key, lst):
        i = cnt[key]
        cnt[key] += 1
        return lst[i % len(lst)]

    def lnt_dma(xsrc_rows, slot):
        x_ = xs[slot]
        dma("sp", x_.ap, xsrc_rows, [], [x_], x_.b.name)

    def lnt_compute(slot):
        x_, ss_, h_ = xs[slot], ssA[slot], hb[slot]
        act(sqj.ap, x_.ap, AF.Square, [x_], [sqj, ss_], accum=ss_.ap[:, 0:1])
        ts("dve", ss_.ap[:, 1:2], ss_.ap[:, 0:1], 1.0 / DM, EPS, ALU.mult, ALU.add, [ss_], [ss_])
        act(ss_.ap[:, 2:3], ss_.ap[:, 1:2], AF.Ln, [ss_], [ss_])
        act(ss_.ap[:, 2:3], ss_.ap[:, 2:3], AF.Exp, [ss_], [ss_], scale=-0.5)
        ts("dve", h_.ap, x_.ap, ss_.ap[:, 2:3], None, ALU.mult, None, [x_, ss_], [h_])

    def lnt_transpose(slot, hT_t, col):
        h_ = hb[slot]
        pst = PS_TR.ap.bitcast(BF16)
        for k in range(8):
            tr(pst[:, k * 128:(k + 1) * 128], h_.ap[:, k * 128:(k + 1) * 128], ident.ap, [h_, ident], [PS_TR])
        hv = hT_t.ap.rearrange("p (k t) -> p k t", t=512)
        cp("act", hv[:, :, col:col + 128], pst.rearrange("p (k t) -> p k t", t=128), [PS_TR], [hT_t])

    def load_norm_transpose(xsrc_rows, hT_t, col):
        lnt_dma(xsrc_rows, 0)
        lnt_compute(0)
        lnt_transpose(0, hT_t, col)

    def tok_matmul(hT_t, col, c0, ncols, ps):
        hv = hT_t.ap.rearrange("p (k t) -> p k t", t=512)
        for k in range(8):
            mm(ps.ap[:, 0:ncols], hv[:, k, col:col + 128], winv[:, k, c0:c0 + ncols], k == 0, k == 7, [hT_t, win], [ps])

    def fm_matmul(hT_t, c0, ps, ntok=512):
        hv = hT_t.ap.rearrange("p (k t) -> p k t", t=512)
        for k in range(8):
            mm(ps.ap[:, 0:ntok], winv[:, k, c0:c0 + 128], hv[:, k, 0:ntok], k == 0, k == 7, [hT_t, win], [ps])

    def rope_block(ps, tab0, dst, dstb):
        z_ = nxt("z", zb)
        i = (cnt["z"] - 1) % 2
        cp("act", z_.ap, ps.ap, [ps], [z_])
        mm(PS_RZ.ap, rmat.ap, z_.ap, True, True, [rmat, z_], [PS_RZ])
        tt("dve", t1[i].ap, ps.ap, Ctab.ap[:, tab0:tab0 + 512], ALU.mult, [ps, Ctab], [t1[i]])
        tt("dve", t2[i].ap, PS_RZ.ap, Stab.ap[:, tab0:tab0 + 512], ALU.mult, [PS_RZ, Stab], [t2[i]])
        o_ = nxt("o", ost)
        tt("dve", o_.ap, t1[i].ap, t2[i].ap, ALU.add, [t1[i], t2[i]], [o_])
        dma("pool", dst, o_.ap, [o_], [dstb], o_.b.name)

    def conv_block(ps, blk, dst_ap):
        z_ = zc[blk]
        cp("act", z_.ap[:, 3:515], ps.ap, [ps], [z_])
        y_ = nxt("y", yc)
        cw = convw.ap.rearrange("p (b j) -> p b j", j=4)
        ts("dve", y_.ap, z_.ap[:, 3:515], cw[:, blk, 3:4], convb.ap[:, blk:blk + 1], ALU.mult, ALU.add, [z_, convw, convb], [y_])
        for j in (2, 1, 0):
            stt(y_.ap, z_.ap[:, j:j + 512], cw[:, blk, j:j + 1], y_.ap, ALU.mult, ALU.add, [z_, convw, y_], [y_])
        cp("dve", z_.ap[:, 0:3], z_.ap[:, 512:515], [z_], [z_])
        return y_

    def gates_part1(ps):
        g_ = nxt("g", gts)
        tt("dve", g_.ap[:, 0:8], ps.ap[:, 0:8], gateb.ap, ALU.add, [ps, gateb], [g_])
        return g_

    def gates_part1b(g_):
        act(g_.ap[:, 8:12], g_.ap[:, 4:8], AF.Exp, [g_], [g_], scale=-1.0)
        act(g_.ap[:, 8:12], g_.ap[:, 8:12], AF.Ln, [g_], [g_], bias=onecol.ap)

    def gates_part2(g_, tile_idx):
        mm(PS_G.ap[:, 0:4], tri.ap, g_.ap[:, 8:12], True, True, [tri, g_], [PS_G])
        mm(PS_G.ap[:, 4:8], ones128.ap, g_.ap[:, 8:12], True, True, [ones128, g_], [PS_G])
        tt("dve", g_.ap[:, 12:16], g_.ap[:, 0:4], PS_G.ap[:, 0:4], ALU.add, [g_, PS_G], [g_])
        act(GPv[:, tile_idx, 0:4], g_.ap[:, 12:16], AF.Exp, [g_], [GP])
        act(GPv[:, tile_idx, 4:12], PS_G.ap[:, 0:8], AF.Exp, [PS_G], [GP], scale=-1.0)
        ts("dve", GPv[:, tile_idx, 4:8], GPv[:, tile_idx, 4:8], 128.0 ** -0.5, None, ALU.mult, None, [GP], [GP])


    def xrows(own, g, j):
        xsrc = x_own if own else x_pre
        row0 = (128 if own else 0) + g * 512
        return xsrc[row0 + j * 128:row0 + (j + 1) * 128, :]

    def lnt_group(own, g):
        hT_t = hT[(g + (NGP if own else 0)) % 2]
        for j in range(4):
            lnt_dma(xrows(own, g, j), j)
        for j in range(4):
            lnt_compute(j)
        for j in range(4):
            lnt_transpose(j, hT_t, j * 128)

    pend_k = []

    def phaseA_group(own, g, nxt_grp):
        hT_t = hT[(g + (NGP if own else 0)) % 2]
        if nxt_grp is not None:
            for j in range(4):
                lnt_dma(xrows(*nxt_grp, j), j)
        pend_g = []
        pend_g1 = []
        tile0 = (NTP + g * 4) if own else g * 4
        tab0 = g * 512
        for j in range(4):
            tile = tile0 + j
            col = j * 128
            ps = nxt("tok", PS_TOK)
            tok_matmul(hT_t, col, 1024, 512, ps)
            v_ = nxt("v", vst)
            cp("act", v_.ap.rearrange("p (h c) -> p h c", c=130)[:, :, 0:128], ps.ap.rearrange("p (h c) -> p h c", c=128), [ps], [v_])
            dma("pool", V1[:, :, tile, :].rearrange("h p c -> p h c"), v_.ap.rearrange("p (h c) -> p h c", c=130), [v_], [V1b], v_.b.name)
            ps = nxt("tok", PS_TOK)
            tok_matmul(hT_t, col, 2560, 512, ps)
            v_ = nxt("v", vst)
            cp("act", v_.ap.rearrange("p (h c) -> p h c", c=130)[:, :, 0:128], ps.ap.rearrange("p (h c) -> p h c", c=128), [ps], [v_])
            dma("pool", MV1[:, tile, :, :], v_.ap.rearrange("p (h c) -> p h c", c=130), [v_], [MV1b], v_.b.name)
            ps = nxt("tok", PS_TOK)
            tok_matmul(hT_t, col, 3584, 8, ps)
            g_ = gates_part1(ps)
            while pend_g:
                gates_part2(*pend_g.pop(0))
            while pend_g1:
                gg_, tt_ = pend_g1.pop(0)
                gates_part1b(gg_)
                pend_g.append((gg_, tt_))
            pend_g1.append((g_, tile))
            if nxt_grp is not None and j == 1:
                for jj in range(4):
                    lnt_compute(jj)
            if own:
                ps = nxt("tok", PS_TOK)
                tok_matmul(hT_t, col, 3072, 512, ps)
                e_ = nxt("e", ef)
                i = (cnt["e"] - 1) % 2
                act(e_.ap, ps.ap, AF.Exp, [ps], [e_], scale=-1.0)
                ts("dve", e_.ap, e_.ap, 1.0, None, ALU.add, None, [e_], [e_])
                recip(e_.ap, e_.ap, [e_], [e_])
                tt("dve", gst[i].ap, e_.ap, mlg.ap, ALU.mult, [e_, mlg], [gst[i]])
                dma("pool", GSO[:, tile - NTP, :], gst[i].ap, [gst[i]], [GSOb], gst[i].b.name)
        while pend_k:
            pend_k.pop(0)()
        if nxt_grp is not None:
            hT_n = hT[(nxt_grp[1] + (NGP if nxt_grp[0] else 0)) % 2]
            for jj in range(4):
                lnt_transpose(jj, hT_n, jj * 128)
        while pend_g1:
            gg_, tt_ = pend_g1.pop(0)
            gates_part1b(gg_)
            pend_g.append((gg_, tt_))
        while pend_g:
            gates_part2(*pend_g.pop(0))
        ks_ = nxt("kst", kst)
        ksv = ks_.ap.rearrange("p (h t) -> p h t", t=512)
        kbase = SP if own else 0

        def mlq_post(ps, h):
            y_ = conv_block(ps, h, None)

            def fin():
                o_ = nxt("o", ost)
                act(o_.ap, y_.ap, AF.Silu, [y_], [o_])
                dma("pool", MQT[h, :, g * 512:(g + 1) * 512], o_.ap, [o_], [MQTb], o_.b.name)
            return fin

        def mlk_post(ps, h):
            y_ = conv_block(ps, 4 + h, None)

            def fin():
                act(ksv[:, h, :], y_.ap, AF.Silu, [y_], [ks_])
            return fin

        blocks = []
        for h in range(4):
            if own:
                blocks.append((h * 128, lambda ps, h=h: rope_block(ps, tab0, QT[h, :, g * 512:(g + 1) * 512], QTb)))
            blocks.append((512 + h * 128, lambda ps, h=h: rope_block(ps, tab0, KT[h, :, kbase + g * 512:kbase + (g + 1) * 512], KTb)))
        if own:
            for h in range(4):
                blocks.append((1536 + h * 128, lambda ps, h=h: mlq_post(ps, h)))
        for h in range(4):
            blocks.append((2048 + h * 128, lambda ps, h=h: mlk_post(ps, h)))
        pend_fin = []
        ps_cur = nxt("fm", PS_FM)
        fm_matmul(hT_t, blocks[0][0], ps_cur)
        for b in range(len(blocks)):
            ps_next = None
            if b + 1 < len(blocks):
                ps_next = nxt("fm", PS_FM)
                fm_matmul(hT_t, blocks[b + 1][0], ps_next)
            fin = blocks[b][1](ps_cur)
            if pend_fin:
                pend_fin.pop(0)()
            if fin is not None:
                pend_fin.append(fin)
            ps_cur = ps_next
        while pend_fin:
            pend_fin.pop(0)()
        if own:
            dma("pool", MKT[:, :, g * 512:(g + 1) * 512].rearrange("h p t -> p h t"), ksv, [ks_], [MKTb], ks_.b.name)
        def ktrans():
            pkt = PS_KT.ap.bitcast(BF16)
            for j in range(4):
                for h in range(4):
                    tr(pkt[:, h * 128:(h + 1) * 128], ksv[:, h, j * 128:(j + 1) * 128], ident.ap, [ks_, ident], [PS_KT])
                k_ = nxt("k", ktk)
                cp("dve", k_.ap, pkt[:, 0:512], [PS_KT], [k_])
                dma("pool", MK[:, tile0 + j, :], k_.ap, [k_], [MKb], k_.b.name)
        pend_k.append(ktrans)

    def halo_step():
        hT_t = hT[0]
        load_norm_transpose(x_own[0:128, :], hT_t, 0)
        for blk in range(8):
            ps = nxt("fm", PS_FM)
            fm_matmul(hT_t, (1536 if blk < 4 else 2048) + (blk % 4) * 128, ps, ntok=128)
            cp("act", zc[blk].ap[:, 0:3], ps.ap[:, 125:128], [ps], [zc[blk]])

    lnt_group(False, 0)
    for g in range(NGP):
        phaseA_group(False, g, (False, g + 1) if g + 1 < NGP else None)
    while pend_k:
        pend_k.pop(0)()
    build_tables(SP, SO)
    halo_step()
    lnt_group(True, 0)
    for g in range(NGO):
        phaseA_group(True, g, (True, g + 1) if g + 1 < NGO else None)
    while pend_k:
        pend_k.pop(0)()
    P.barrier()
    if upto == "A":
        return finish_early()

    AR.off = PERSIST
    masks = AR.alloc(4 * 1024, BF16, "masks")
    dma("sp", masks.ap, masks_d.rearrange("p j c -> p (j c)"), [], [masks], "masks")
    masksv = masks.ap.rearrange("p (j c) -> p j c", c=1024)
    kts = [AR.alloc(SA, BF16, "kts%d" % i) for i in range(2)]
    vs = [AR.alloc(NT * 130, BF16, "vs%d" % i) for i in range(2)]
    qts = [AR.alloc(SO, BF16, "qts%d" % i) for i in range(2)]
    pT = [AR.alloc(1024, BF16, "pT%d" % i) for i in range(3)]
    osb = [AR.alloc(8 * 129, F32, "osb%d" % i) for i in range(2)]
    ob = AR.alloc(128, F32, "ob")
    aob = [AR.alloc(128, BF16, "aob%d" % i) for i in range(2)]
    ast = [AR.alloc(512, BF16, "ast%d" % i) for i in range(2)]
    sc8 = [AR.alloc(8, F32, "sc8_%d" % i) for i in range(2)]
    ST = [T(None, "ST0", True), T(None, "ST1", True)]
    OB = [psum[4], psum[5], psum[6]]

    def oreg(r):
        return OB[r // 3].ap[:, (r % 3) * 129:(r % 3) * 129 + 129], OB[r // 3]

    nB = {"p": 0, "o": 0, "a": 0, "st": 0}
    zer = AR.alloc(128, BF16, "zer")
    memset("dve", zer.ap, 0.0, [zer])
    STap = [psbig[:, 0:1024], psbig[:, 1024:2048]]
    steps = [(h, G, kb) for h in range(4) for G in range(NGO) for kb in range(NTP + 4 * (G + 1))]
    hbuf = {}

    def head_bufs(h):
        if h not in hbuf:
            kt_, v_, qt_ = kts[h % 2], vs[h % 2], qts[h % 2]
            dma("sp", kt_.ap, KT[h, :, :], [KTb], [kt_], kt_.b.name)
            dma("sp", v_.ap, V1[h, :, :, :].rearrange("p t c -> p (t c)"), [V1b], [v_], v_.b.name)
            dma("sp", qt_.ap, QT[h, :, :], [QTb], [qt_], qt_.b.name)
            hbuf[h] = (kt_, v_, qt_)
        return hbuf[h]

    def emit_st(i):
        h, G, kb = steps[i]
        kt_, v_, qt_ = head_bufs(h)
        stb = ST[i % 2]
        sap = STap[i % 2]
        mm(sap[:, 0:512], kt_.ap[0:64, kb * 128:(kb + 1) * 128], qt_.ap[0:64, G * 512:(G + 1) * 512], True, True, [kt_, qt_], [stb])
        mm(sap[:, 512:1024], kt_.ap[64:128, kb * 128:(kb + 1) * 128], qt_.ap[64:128, G * 512:(G + 1) * 512], True, True, [kt_, qt_], [stb])

    def post_part2(h, G, o_, s8):
        ov = o_.ap.rearrange("p (r c) -> p r c", c=129)
        recip(s8.ap, ov[:, :, 128], [o_], [s8])
        ts("dve", s8.ap[:, 4:8], s8.ap[:, 4:8], neglam.ap, None, ALU.mult, None, [s8, neglam], [s8])
        a_ = ast[nB["a"] % 2]
        nB["a"] += 1
        pst = psum[7].ap.bitcast(BF16)
        for qb in range(4):
            ts("dve", ob.ap, ov[:, qb, 0:128], s8.ap[:, qb:qb + 1], None, ALU.mult, None, [o_, s8], [ob])
            stt(ob.ap, ov[:, 4 + qb, 0:128], s8.ap[:, 4 + qb:5 + qb], ob.ap, ALU.mult, ALU.add, [o_, s8, ob], [ob])
            stt(junk.ap[:, 0:128], ob.ap, 1.0, ob.ap, ALU.mult, ALU.mult, [ob], [junk, small], accum=small.ap[:, 8 + qb:9 + qb])
            ts("dve", small.ap[:, 12 + qb:13 + qb], small.ap[:, 8 + qb:9 + qb], 1.0 / 128, EPS, ALU.mult, ALU.add, [small], [small])
            act(small.ap[:, 16 + qb:17 + qb], small.ap[:, 12 + qb:13 + qb], AF.Ln, [small], [small])
            act(small.ap[:, 16 + qb:17 + qb], small.ap[:, 16 + qb:17 + qb], AF.Exp, [small], [small], scale=-0.5)
            ab = aob[qb % 2]
            stt(ab.ap, ob.ap, small.ap[:, 16 + qb:17 + qb], g08.ap, ALU.mult, ALU.mult, [ob, small, g08], [ab])
            tr(pst[:, qb * 128:(qb + 1) * 128], ab.ap, ident.ap, [ab, ident], [psum[7]])
        cp("dve", a_.ap, pst[:, 0:512], [psum[7]], [a_])
        dma("pool", MIXT[h * 128:(h + 1) * 128, G * 512:(G + 1) * 512], a_.ap, [a_], [MIXTb], a_.b.name)

    deferred = []
    emit_st(0)
    if len(steps) > 1:
        emit_st(1)
    for i, (h, G, kb) in enumerate(steps):
        kt_, v_, qt_ = head_bufs(h)
        vv = v_.ap.rearrange("p (t c) -> p t c", c=130)
        while deferred and deferred[0][0] <= i:
            deferred.pop(0)[1]()
        stb = ST[i % 2]
        p_ = pT[nB["p"] % 3]
        nB["p"] += 1
        act(p_.ap, STap[i % 2], AF.Exp, [stb, keybias], [p_], bias=keybias.ap[:, kb:kb + 1], scale=0.125)
        j = kb - (NTP + 4 * G)
        if j >= 0:
            tt("dve", p_.ap, p_.ap, masksv[:, j, :], ALU.mult, [p_, masks], [p_])
        if i + 2 < len(steps):
            emit_st(i + 2)
        if kb == 0:
            for bnk in range(3):
                nreg = 3 if bnk < 2 else 2
                memset("dve", OB[bnk].ap[:, 0:nreg * 129], 0.0, [OB[bnk]])
        for m in range(2):
            for qb in range(4):
                last = NTP + 4 * G + qb
                if kb > last:
                    continue
                oap, obuf = oreg(m * 4 + qb)
                mm(oap, p_.ap[:, m * 512 + qb * 128:m * 512 + (qb + 1) * 128], vv[:, kb, 0:129], False, kb == last, [p_, v_], [obuf], skip=True)
        if kb == NTP + 4 * (G + 1) - 1:
            o_ = osb[nB["o"] % 2]
            s8 = sc8[nB["o"] % 2]
            nB["o"] += 1
            for bnk in range(3):
                nreg = 3 if bnk < 2 else 2
                cp("dve", o_.ap[:, bnk * 387:bnk * 387 + nreg * 129], OB[bnk].ap[:, 0:nreg * 129], [OB[bnk]], [o_])
            deferred.append((i + 6, (lambda h=h, G=G, o_=o_, s8=s8: post_part2(h, G, o_, s8))))
    while deferred:
        deferred.pop(0)[1]()
    P.barrier()
    if upto == "B":
        return finish_early()

    AR.off = PERSIST
    wo = AR.alloc(8 * DM, BF16, "wo")
    wg = AR.alloc(8 * DFF, BF16, "wg")
    wu = AR.alloc(8 * DFF, BF16, "wu")
    WD_OFF = AR.off
    wov = wo.ap.rearrange("p (k c) -> p k c", c=DM)
    wgv = wg.ap.rearrange("p (k c) -> p k c", c=DFF)
    wuv = wu.ap.rearrange("p (k c) -> p k c", c=DFF)
    pf_chunks = [(w_out[k * 128:(k + 1) * 128, :], wov[:, k, :], None) for k in range(8)]
    for k in range(8):
        for c0 in (0, 1408):
            pf_chunks.append((w_gate[k * 128:(k + 1) * 128, c0:c0 + 1408], wgv[:, k, c0:c0 + 1408], k))
            pf_chunks.append((w_up[k * 128:(k + 1) * 128, c0:c0 + 1408], wuv[:, k, c0:c0 + 1408], k))
    mks = [AR.alloc(4 * 512, BF16, "mks%d" % i) for i in range(2)]
    mvs = [AR.alloc(4 * 520, BF16, "mvs%d" % i) for i in range(2)]
    mqs = [AR.alloc(4 * 512, BF16, "mqs%d" % i) for i in range(2)]
    mkts = [AR.alloc(4 * 512, BF16, "mkts%d" % i) for i in range(2)]
    gss = [AR.alloc(4 * 512, BF16, "gss%d" % i) for i in range(2)]
    Cf = AR.alloc(4 * 129, F32, "Cf")
    Cb = AR.alloc(4 * 130, BF16, "Cb")
    kw = [AR.alloc(128, BF16, "kw%d" % i) for i in range(3)]
    scp = [AR.alloc(128, BF16, "scp%d" % i) for i in range(3)]
    hm = [AR.alloc(128, BF16, "hm%d" % i) for i in range(3)]
    hst = [AR.alloc(4 * 512, BF16, "hst%d" % i) for i in range(2)]
    pc = [AR.alloc(32, F32, "pc%d" % i) for i in range(2)]
    wstC = [AR.alloc(1408, F32, "wstC%d" % i) for i in range(4)]
    pf_state = {"dma": 0, "cvt": 0}

    def prefetch_step():
        n = pf_state["dma"]
        if n < len(pf_chunks):
            src, dst, k = pf_chunks[n]
            s_ = wstC[n % 4]
            dma("sp", s_.ap[:, 0:dst.shape[1]], src, [], [s_], s_.b.name)
            pf_state["dma"] += 1
        m = pf_state["cvt"]
        if m < len(pf_chunks) and (pf_state["dma"] - m >= 4 or pf_state["dma"] == len(pf_chunks)):
            src, dst, k = pf_chunks[m]
            s_ = wstC[m % 4]
            ncol = dst.shape[1]
            eng = "dve" if m % 2 == 0 else "act"
            if k is None:
                cp(eng, dst, s_.ap[:, 0:ncol], [s_], [wo])
            elif eng == "act":
                act(dst, s_.ap[:, 0:ncol], AF.Copy, [s_, gffn], [wo], scale=gffn.ap[:, k:k + 1])
            else:
                ts("dve", dst, s_.ap[:, 0:ncol], gffn.ap[:, k:k + 1], None, ALU.mult, None, [s_, gffn], [wo])
            pf_state["cvt"] += 1

    Cfv = Cf.ap.rearrange("p (h c) -> p h c", c=129)
    Cbv = Cb.ap.rearrange("p (h c) -> p h c", c=130)
    PS_SC = [psum[0], psum[1]]
    PS_N = [T(None, "N0", True), T(None, "N1", True)]
    PS_U = psum[6]
    PS_T = psum[7]
    nC = {"kw": 0, "scp": 0, "hm": 0, "sc": 0}

    def state_update(kk, vv4, tile, h, need_cb, mk_t, mv_t, PS_U=psum[6]):
        k_ = kw[nC["kw"] % 3]
        nC["kw"] += 1
        act(k_.ap, kk, AF.Copy, [GP, mk_t], [k_], scale=GPv[:, tile, h:h + 1])
        mm(PS_U.ap[:, 0:129], k_.ap, vv4[:, h, 0:129], True, True, [k_, mv_t], [PS_U])
        act(Cfv[:, h, :], Cfv[:, h, :], AF.Copy, [GP, Cfh[h]], [Cfh[h]], scale=GPv[:, tile, 8 + h:9 + h])
        stt(Cfv[:, h, :], PS_U.ap[:, 0:129], GPv[:, tile, 8 + h:9 + h], Cfv[:, h, :], ALU.mult, ALU.add, [PS_U, GP, Cfh[h]], [Cfh[h]])
        if need_cb:
            cp("act", Cbv[:, h, 0:129], Cfv[:, h, :], [Cfh[h]], [Cbh[h]])

    Cfh = [Buf("Cf%d" % h) for h in range(4)]
    Cbh = [Buf("Cb%d" % h) for h in range(4)]
    memset("dve", Cf.ap, 0.0, Cfh)
    memset("dve", Cb.ap, 0.0, Cbh)
    for grp in range(NGP + NGO):
        own = grp >= NGP
        go = grp - NGP
        mk_, mv_ = mks[grp % 2], mvs[grp % 2]
        dma("sp", mk_.ap.rearrange("p (t c) -> p t c", c=512), MK[:, grp * 4:(grp + 1) * 4, :], [MKb], [mk_], mk_.b.name)
        dma("sp", mv_.ap.rearrange("p (t c) -> p t c", c=520), MV1[:, grp * 4:(grp + 1) * 4, :, :].rearrange("p t h c -> p t (h c)"),
            [MV1b], [mv_], mv_.b.name)
        if own:
            mq_, mkt_, gs_, hs_ = mqs[go % 2], mkts[go % 2], gss[go % 2], hst[go % 2]
            dma("sp", mq_.ap.rearrange("p (h t) -> p h t", t=512), MQT[:, :, go * 512:(go + 1) * 512].rearrange("h p t -> p h t"), [MQTb], [mq_], mq_.b.name)
            dma("sp", mkt_.ap.rearrange("p (h t) -> p h t", t=512), MKT[:, :, go * 512:(go + 1) * 512].rearrange("h p t -> p h t"), [MKTb], [mkt_], mkt_.b.name)
            dma("sp", gs_.ap.rearrange("p (t c) -> p t c", c=512), GSO[:, go * 4:(go + 1) * 4, :], [GSOb], [gs_], gs_.b.name)
            mqv = mq_.ap.rearrange("p (h t) -> p h t", t=512)
            mktv = mkt_.ap.rearrange("p (h t) -> p h t", t=512)
            gsv = gs_.ap.rearrange("p (t c) -> p t c", c=512)
            hsv = hs_.ap.rearrange("p (h t) -> p h t", t=512)
        mkv = mk_.ap.rearrange("p (t c) -> p t c", c=512)
        mvv = mv_.ap.rearrange("p (t h c) -> p t h c", h=4, c=130)
        if own and go == 0:
            P.barrier()
            for h in range(4):
                ts("dve", Cfv[:, h, :], Cfv[:, h, :], flag.ap, None, ALU.mult, None, [Cfh[h], flag], [Cfh[h]])
                cp("act", Cbv[:, h, 0:129], Cfv[:, h, :], [Cfh[h]], [Cbh[h]])
        for j in range(4):
            tile = grp * 4 + j
            if not own:
                prefetch_step()
                prefetch_step()
                for h in range(4):
                    state_update(mkv[:, j, h * 128:(h + 1) * 128], mvv[:, j], tile, h, False, mk_, mv_, psum[(tile * 4 + h) % 7])
                continue
            nb_ = PS_N[tile % 2]
            nbank = 2 + 2 * (tile % 2)
            for h in range(4):
                sc_ = PS_SC[nC["sc"] % 2]
                nC["sc"] += 1
                mm(sc_.ap[:, 0:128], mktv[:, h, j * 128:(j + 1) * 128], mqv[:, h, j * 128:(j + 1) * 128], True, True, [mkt_, mq_], [sc_])
                s_ = scp[nC["scp"] % 3]
                nC["scp"] += 1
                stt(s_.ap, sc_.ap[:, 0:128], GPv[:, tile, h:h + 1], causal.ap, ALU.mult, ALU.mult, [sc_, GP, causal], [s_])
                nreg = psum[nbank + h // 2].ap[:, (h % 2) * 256:(h % 2) * 256 + 129]
                mm(nreg, s_.ap, mvv[:, j, h, 0:129], True, False, [s_, mv_], [nb_])
                mm(nreg, mqv[:, h, j * 128:(j + 1) * 128], Cbv[:, h, 0:129], False, True, [mq_, Cbh[h]], [nb_])
                state_update(mkv[:, j, h * 128:(h + 1) * 128], mvv[:, j], tile, h, True, mk_, mv_)
            p_ = pc[tile % 2]
            for h in range(4):
                nreg = psum[nbank + h // 2].ap[:, (h % 2) * 256:(h % 2) * 256 + 129]
                tt("dve", p_.ap[:, h:h + 1], nreg[:, 128:129], GPv[:, tile, 4 + h:5 + h], ALU.mult, [nb_, GP], [p_])
                act(junk.ap[:, 0:128], nreg[:, 0:128], AF.Square, [nb_], [junk, p_], accum=p_.ap[:, 8 + h:9 + h])
            stt(p_.ap[:, 4:8], p_.ap[:, 0:4], -1.0, p_.ap[:, 0:4], ALU.mult, ALU.max, [p_], [p_])
            ts("dve", p_.ap[:, 4:8], p_.ap[:, 4:8], 1.0, None, ALU.max, None, [p_], [p_])
            recip(p_.ap[:, 4:8], p_.ap[:, 4:8], [p_], [p_])
            tt("dve", p_.ap[:, 4:8], p_.ap[:, 4:8], GPv[:, tile, 4:8], ALU.mult, [p_, GP], [p_])
            tt("dve", p_.ap[:, 8:12], p_.ap[:, 8:12], p_.ap[:, 4:8], ALU.mult, [p_], [p_])
            tt("dve", p_.ap[:, 8:12], p_.ap[:, 8:12], p_.ap[:, 4:8], ALU.mult, [p_], [p_])
            ts("dve", p_.ap[:, 8:12], p_.ap[:, 8:12], 1.0 / 128, EPS, ALU.mult, ALU.add, [p_], [p_])
            act(p_.ap[:, 12:16], p_.ap[:, 8:12], AF.Ln, [p_], [p_])
            act(p_.ap[:, 12:16], p_.ap[:, 12:16], AF.Exp, [p_], [p_], scale=-0.5)
            tt("dve", p_.ap[:, 16:20], p_.ap[:, 12:16], p_.ap[:, 4:8], ALU.mult, [p_], [p_])
            ptr = PS_T.ap.bitcast(BF16)
            for h in range(4):
                nreg = psum[nbank + h // 2].ap[:, (h % 2) * 256:(h % 2) * 256 + 129]
                hm_ = hm[nC["hm"] % 3]
                nC["hm"] += 1
                stt(hm_.ap, nreg[:, 0:128], p_.ap[:, 16 + h:17 + h], gsv[:, j, h * 128:(h + 1) * 128], ALU.mult, ALU.mult, [nb_, p_, gs_], [hm_])
                tr(ptr[:, h * 128:(h + 1) * 128], hm_.ap, ident.ap, [hm_, ident], [PS_T])
            cp("act", hsv[:, :, j * 128:(j + 1) * 128], ptr[:, 0:512].rearrange("p (h t) -> p h t", t=128), [PS_T], [hs_])
        if own:
            dma("pool", MIXT[512:1024, go * 512:(go + 1) * 512].rearrange("(h e) t -> e h t", e=128), hsv, [hs_], [MIXTb], hs_.b.name)
    while pf_state["cvt"] < len(pf_chunks):
        prefetch_step()
    P.barrier()
    if upto == "C":
        return finish_early()

    AR.off = WD_OFF
    wd = AR.alloc(22 * DM, BF16, "wd")
    wdv = wd.ap.rearrange("p (k c) -> p k c", c=DM)
    D_MARK = AR.off
    wst = [AR.alloc(1408, F32, "wstD%d" % i) for i in range(3)]
    n = 0

    def wload(src, dst, scale_col):
        nonlocal n
        s = wst[n % 3]
        eng = ("dve", "act")[n % 2]
        ncol = dst.shape[1]
        dma("sp", s.ap[:, 0:ncol], src, [], [s], s.b.name)
        cp(eng, dst, s.ap[:, 0:ncol], [s], [wo])
        n += 1

    for k in range(22):
        wload(w_down[k * 128:(k + 1) * 128, :], wdv[:, k, :], None)
    P.barrier()
    AR.off = D_MARK
    GT = 256
    x1 = [AR.alloc(2 * DM, F32, "x1_%d" % i) for i in range(2)]
    mxs = [AR.alloc(8 * GT, BF16, "mxs%d" % i) for i in range(2)]
    h2 = AR.alloc(DM, BF16, "h2")
    h2T = AR.alloc(8 * GT, BF16, "h2T")
    aT = AR.alloc(22 * GT, BF16, "aT")
    sg = [AR.alloc(GT, BF16, "sg%d" % i) for i in range(2)]
    sD = [AR.alloc(8, F32, "sD%d" % i) for i in range(2)]
    sqd = AR.alloc(DM, BF16, "sqd")
    h2Tv = h2T.ap.rearrange("p (k t) -> p k t", t=GT)
    aTv = aT.ap.rearrange("p (k t) -> p k t", t=GT)
    PS_ACC = [psum[0], psum[1]]
    PS_GG = [psum[2], psum[3]]
    PS_UU = [psum[4], psum[5]]
    PS_TD = psum[6]
    nD = {"acc": 0, "g": 0}
    wD = [wo]
    for g in range(SO // GT):
        x1_ = x1[g % 2]
        mx_ = mxs[g % 2]
        x1v = x1_.ap.rearrange("p (j c) -> p j c", c=DM)
        mxv = mx_.ap.rearrange("p (k t) -> p k t", t=GT)
        dma("sp", mxv, MIXT[:, g * GT:(g + 1) * GT].rearrange("(k p) t -> p k t", p=128), [MIXTb], [mx_], mx_.b.name)
        dmas("sp", [(x1v[:, j, :], x_own[128 + g * GT + j * 128:128 + g * GT + (j + 1) * 128, :]) for j in range(2)], [], [x1_], x1_.b.name)
        for j in range(2):
            for c in range(2):
                acc = PS_ACC[nD["acc"] % 2]
                nD["acc"] += 1
                for k in range(8):
                    mm(acc.ap, mxv[:, k, j * 128:(j + 1) * 128], wov[:, k, c * 512:(c + 1) * 512], k == 0, k == 7, [mx_] + wD, [acc])
                tt("dve", x1v[:, j, c * 512:(c + 1) * 512], x1v[:, j, c * 512:(c + 1) * 512], acc.ap, ALU.add, [x1_, acc], [x1_])
            s_ = sD[j]
            act(sqd.ap, x1v[:, j, :], AF.Square, [x1_], [sqd, s_], accum=s_.ap[:, 0:1])
            ts("dve", s_.ap[:, 1:2], s_.ap[:, 0:1], 1.0 / DM, EPS, ALU.mult, ALU.add, [s_], [s_])
            act(s_.ap[:, 2:3], s_.ap[:, 1:2], AF.Ln, [s_], [s_])
            act(s_.ap[:, 2:3], s_.ap[:, 2:3], AF.Exp, [s_], [s_], scale=-0.5)
            ts("dve", h2.ap, x1v[:, j, :], s_.ap[:, 2:3], None, ALU.mult, None, [x1_, s_], [h2])
            ptd = PS_TD.ap.bitcast(BF16)
            for k in range(8):
                tr(ptd[:, k * 128:(k + 1) * 128], h2.ap[:, k * 128:(k + 1) * 128], ident.ap, [h2, ident], [PS_TD])
            cp("act", h2Tv[:, :, j * 128:(j + 1) * 128], ptd.rearrange("p (k t) -> p k t", t=128), [PS_TD], [h2T])
        for f in range(22):
            gg = PS_GG[f % 2]
            uu = PS_UU[f % 2]
            for k in range(8):
                mm(gg.ap[:, 0:GT], wgv[:, k, f * 128:(f + 1) * 128], h2Tv[:, k, :], k == 0, k == 7, [h2T] + wD, [gg])
            for k in range(8):
                mm(uu.ap[:, 0:GT], wuv[:, k, f * 128:(f + 1) * 128], h2Tv[:, k, :], k == 0, k == 7, [h2T] + wD, [uu])
            s_ = sg[f % 2]
            act(s_.ap, gg.ap[:, 0:GT], AF.Silu, [gg], [s_])
            tt("dve", aTv[:, f, :], s_.ap, uu.ap[:, 0:GT], ALU.mult, [s_, uu], [aT])
        for j in range(2):
            for c in range(2):
                acc = PS_ACC[nD["acc"] % 2]
                nD["acc"] += 1
                for f in range(22):
                    mm(acc.ap, aTv[:, f, j * 128:(j + 1) * 128], wdv[:, f, c * 512:(c + 1) * 512], f == 0, f == 21, [aT] + wD, [acc])
                tt("dve", x1v[:, j, c * 512:(c + 1) * 512], x1v[:, j, c * 512:(c + 1) * 512], acc.ap, ALU.add, [x1_, acc], [x1_])
            s_ = sD[j]
            act(sqd.ap, x1v[:, j, :], AF.Square, [x1_], [sqd, s_], accum=s_.ap[:, 4:5])
            ts("dve", s_.ap[:, 5:6], s_.ap[:, 4:5], 1.0 / DM, EPS, ALU.mult, ALU.add, [s_], [s_])
            act(s_.ap[:, 6:7], s_.ap[:, 5:6], AF.Ln, [s_], [s_])
            act(s_.ap[:, 6:7], s_.ap[:, 6:7], AF.Exp, [s_], [s_], scale=-0.5)
            stt(x1v[:, j, :], x1v[:, j, :], s_.ap[:, 6:7], fing.ap, ALU.mult, ALU.mult, [x1_, s_, fing], [x1_])
        dmas("pool", [(y_out[g * GT + j * 128:g * GT + (j + 1) * 128, :], x1v[:, j, :]) for j in range(2)], [x1_], [], x1_.b.name, final=True)

    P.emit(st)
    st.close()
    return nc


_INV_FREQ = (np.float32(500000.0) ** (-np.arange(0, 16, 2, dtype=np.float32) / np.float32(16.0))).astype(np.float32)


def _consts():
    bf = ml_dtypes.bfloat16
    c = {}
    c["ident"] = np.eye(128, dtype=np.float32).astype(bf)
    r = np.zeros((128, 128), np.float32)
    invf = np.zeros((128, 1), np.float32)
    sgn = np.zeros((128, 1), np.float32)
    for p in range(128):
        d = p % 64
        if d < 8:
            r[p + 8, p] = 1.0
            invf[p, 0] = _INV_FREQ[d]
            sgn[p, 0] = -1.0
        elif d < 16:
            r[p - 8, p] = 1.0
            invf[p, 0] = _INV_FREQ[d - 8]
            sgn[p, 0] = 1.0
    c["rmat"] = r.astype(bf)
    c["invf"] = invf
    c["sgn"] = sgn
    s = np.arange(128)
    c["causal"] = (s[:, None] <= s[None, :]).astype(np.float32).astype(bf)
    c["tri"] = (s[:, None] <= s[None, :]).astype(np.float32)
    m = np.zeros((128, 4, 1024), np.float32)
    q = np.arange(512)
    for j in range(4):
        kc = (j * 128 + s) // 64
        vis = (kc[:, None] <= (q[None, :] // 64)).astype(np.float32)
        m[:, j, 0:512] = vis
        m[:, j, 512:1024] = vis
    c["masks"] = m.astype(bf)
    return c


_NC_CACHE = {}


def _get_nc(NGP, NGO, dbg):
    key = (NGP, NGO, dbg)
    if key not in _NC_CACHE:
        _NC_CACHE[key] = build(NGP, NGO, dbg)
    return _NC_CACHE[key]


def make_in_maps(inputs, NGP=8, NGO=8, batches=4):
    f32 = np.float32
    x = np.asarray(inputs["x"], f32)
    positions = np.asarray(inputs["positions"], np.int32)
    SP, SO = NGP * 512, NGO * 512
    NT = (NGP + NGO) * 4
    consts = _consts()
    shared = dict(consts)
    shared["w_in"] = np.ascontiguousarray(np.asarray(inputs["w_in"], f32)[0])
    shared["w_out"] = np.ascontiguousarray(np.asarray(inputs["w_out"], f32)[0])
    shared["w_gate"] = np.ascontiguousarray(np.asarray(inputs["w_gate"], f32)[0])
    shared["w_up"] = np.ascontiguousarray(np.asarray(inputs["w_up"], f32)[0])
    shared["w_down"] = np.ascontiguousarray(np.asarray(inputs["w_down"], f32)[0])
    shared["gmix"] = np.ascontiguousarray(np.asarray(inputs["mix_norm_g"], f32)[0].reshape(8, 128).T)
    shared["gffn"] = np.ascontiguousarray(np.asarray(inputs["ffn_norm_g"], f32)[0].reshape(8, 128).T)
    shared["da_lambda"] = np.asarray(inputs["da_lambda"], f32)[0].reshape(1, 256)
    shared["da_subln_g"] = np.asarray(inputs["da_subln_g"], f32)[0].reshape(1, 128)
    cw = np.asarray(inputs["ml_conv_w"], f32)[0]
    shared["convw"] = np.ascontiguousarray(cw.reshape(4, 8, 128).transpose(2, 1, 0))
    shared["convb"] = np.ascontiguousarray(np.asarray(inputs["ml_conv_b"], f32)[0].reshape(8, 128).T)
    shared["ml_gate_b"] = np.asarray(inputs["ml_gate_b"], f32)[0].reshape(1, 8)
    shared["ml_norm_g"] = np.asarray(inputs["ml_norm_g"], f32)[0].reshape(1, 512)
    shared["final_norm_g"] = np.asarray(inputs["final_norm_g"], f32).reshape(1, DM)
    in_maps = []
    for c in range(2 * batches):
        b, g = c // 2, c % 2
        m = dict(shared)
        own0 = g * SO if g == 1 else 0
        if g == 1:
            own0 = SP
            halo = x[b, own0 - 128:own0]
        else:
            halo = np.zeros((128, DM), f32)
        m["x_own"] = np.ascontiguousarray(np.concatenate([halo, x[b, own0:own0 + SO]], axis=0))
        m["x_pre"] = np.ascontiguousarray(x[b, 0:SP])
        m["pos"] = np.ascontiguousarray(np.concatenate([positions[b, 0:SP], positions[b, own0:own0 + SO]])[None, :])
        kbias = np.zeros((128, NT), f32)
        if g == 0:
            kbias[:, 0:NGP * 4] = -30000.0
        m["keybias"] = kbias
        m["flag"] = np.full((128, 1), float(g), f32)
        in_maps.append(m)
    return in_maps


def kernel(**inputs):
    nc = _get_nc(8, 8, False)
    in_maps = make_in_maps(inputs, 8, 8, 4)
    res = run_bass_kernel_spmd(nc, in_maps, core_ids=list(range(8)))
    out = np.empty((4, 8192, DM), np.float32)
    for c in range(8):
        b, g = c // 2, c % 2
        out[b, g * 4096:(g + 1) * 4096] = res.results[c]["y"]
    return out
```

```python
import numpy as np
import ml_dtypes
from contextlib import ExitStack
import concourse.bass as bass
import concourse.mybir as mybir
from concourse.bass_utils import run_bass_kernel_spmd

F32 = mybir.dt.float32
BF16 = mybir.dt.bfloat16
I32 = mybir.dt.int32
AF = mybir.ActivationFunctionType
ALU = mybir.AluOpType

ENGS = ("pe", "act", "dve", "pool", "sp")
EPS = 1e-6
DM = 1024
DFF = 2816
INW = 3592
TWO_PI = 2.0 * np.pi


class Buf:
    __slots__ = ("name", "writers", "readers", "psum")

    def __init__(self, name, psum=False):
        self.name = name
        self.writers = {}
        self.readers = []
        self.psum = psum


class Op:
    __slots__ = ("eng", "fn", "deps", "dma", "semkey", "ndma", "signal", "count", "sem", "gidx")

    def __init__(self, eng, fn, dma, semkey, ndma):
        self.eng = eng
        self.fn = fn
        self.deps = set()
        self.dma = dma
        self.semkey = semkey
        self.ndma = ndma
        self.signal = False
        self.count = None
        self.sem = None


class Prog:
    def __init__(self, nc):
        self.nc = nc
        self.ops = []
        self.barrier_deps = set()
        self.last = {}
        self.pending_dma = []
        self.out_dma = []
        self.phase_map = {}

    def add(self, eng, fn, reads=(), writes=(), dma=False, semkey=None, ndma=1, out=False):
        if dma and semkey != "const":
            semkey = "%s%d" % (eng, self.phase_map.setdefault((eng, semkey), len(self.phase_map)))
        op = Op(eng, fn, dma, semkey, ndma)
        op.gidx = len(self.ops)
        deps = set(self.barrier_deps)
        for b in reads:
            deps.update(b.writers.values())
            if b.psum:
                deps.update(r for r in b.readers if r.eng != eng)
        for b in writes:
            deps.update(b.writers.values())
            deps.update(b.readers)
        for b in reads:
            b.readers.append(op)
        for b in writes:
            if b.readers:
                b.writers = {}
                b.readers = []
            b.writers[("dma", op.gidx) if dma else eng] = op
        deps.discard(op)
        op.deps = deps
        self.ops.append(op)
        self.last[eng] = op
        if dma:
            self.pending_dma.append(op)
            if out:
                self.out_dma.append(op)
        return op

    def barrier(self):
        self.barrier_deps = set(self.last.values()) | set(self.pending_dma)
        self.pending_dma = []
        self.phase_map = {}

    @staticmethod
    def needs_sem(op, d):
        if d.dma or op.dma:
            return True
        if d.eng != op.eng:
            return True
        return op.eng != "pe"

    def emit(self, stack):
        nc = self.nc
        for op in self.ops:
            for d in op.deps:
                if self.needs_sem(op, d):
                    d.signal = True
        for op in self.out_dma:
            op.signal = True
        eng_sem = {e: stack.enter_context(nc.semaphore("s_" + e)) for e in ENGS}
        dma_sem, dma_cnt = {}, {}
        eng_cnt = {e: 0 for e in ENGS}
        for op in self.ops:
            if op.dma:
                if op.semkey not in dma_sem:
                    dma_sem[op.semkey] = stack.enter_context(nc.semaphore("d_%s" % (op.semkey,)))
                    dma_cnt[op.semkey] = 0
                dma_cnt[op.semkey] += 16 * op.ndma
                op.sem = dma_sem[op.semkey]
                op.count = dma_cnt[op.semkey]
            elif op.signal:
                eng_cnt[op.eng] += 1
                op.sem = eng_sem[op.eng]
                op.count = eng_cnt[op.eng]
        for op in self.ops:
            if op.dma and op.semkey == "const":
                op.count = dma_cnt["const"]
        self.nsem = len(dma_sem) + len(ENGS)
        block = stack.enter_context(nc.Block())
        by_eng = {e: [o for o in self.ops if o.eng == e] for e in ENGS}
        final_waits = [(o.sem, o.count) for o in self.out_dma]

        def run(engobj, e):
            waited = {}
            for op in by_eng[e]:
                need = {}
                for d in op.deps:
                    if not self.needs_sem(op, d):
                        continue
                    k = id(d.sem)
                    if k not in need or need[k][1] < d.count:
                        need[k] = (d.sem, d.count)
                for k, (s, c) in need.items():
                    if waited.get(k, 0) < c:
                        engobj.wait_ge(s, c)
                        waited[k] = c
                if op.dma:
                    op.fn(engobj, op.sem)
                else:
                    ins = op.fn(engobj)
                    if op.signal:
                        ins.then_inc(op.sem, 1)
            if e == "sp":
                for s, c in final_waits:
                    if waited.get(id(s), 0) < c:
                        engobj.wait_ge(s, c)
                        waited[id(s)] = c

        @block.tensor
        def _(eng):
            run(eng, "pe")

        @block.scalar
        def _(eng):
            run(eng, "act")

        @block.vector
        def _(eng):
            run(eng, "dve")

        @block.gpsimd
        def _(eng):
            run(eng, "pool")

        @block.sync
        def _(eng):
            run(eng, "sp")


class T:
    __slots__ = ("ap", "b")

    def __init__(self, ap, name, psum=False):
        self.ap = ap
        self.b = Buf(name, psum)


class Arena:
    def __init__(self, t):
        self.t = t
        self.off = 0
        self.N = t.shape[1]
        self.n = 0

    def alloc(self, n_elems, dtype, name=None):
        nbytes = n_elems * (2 if dtype == BF16 else 4)
        n32 = (nbytes + 3) // 4
        n32 = (n32 + 7) // 8 * 8
        o = self.off
        self.off += n32
        assert self.off <= self.N, "arena overflow %d > %d (%s)" % (self.off, self.N, name)
        ap = self.t[:, o:o + n32]
        if dtype != F32:
            ap = ap.bitcast(dtype)
        ap = ap[:, 0:n_elems]
        self.n += 1
        return T(ap, name or ("t%d" % self.n))


def build(NGP=8, NGO=8, dbg=False, upto="D"):
    nc = bass.Bass("TRN2", target_bir_lowering=False)
    NTP, NTO = NGP * 4, NGO * 4
    NT = NTP + NTO
    SP, SO = NGP * 512, NGO * 512
    SA = SP + SO
    NKB = NT

    def din(name, shape, dt=F32):
        return nc.dram_tensor(name, list(shape), dt, kind="ExternalInput").ap()

    def dscr(name, shape, dt=BF16):
        kind = "ExternalOutput" if dbg else "Internal"
        return nc.dram_tensor(name, list(shape), dt, kind=kind).ap()

    x_own = din("x_own", [SO + 128, DM])
    x_pre = din("x_pre", [SP, DM])
    pos = din("pos", [1, SA], I32)
    keybias_d = din("keybias", [128, NKB])
    flag_d = din("flag", [128, 1])
    w_in = din("w_in", [DM, INW])
    w_out = din("w_out", [DM, DM])
    w_gate = din("w_gate", [DM, DFF])
    w_up = din("w_up", [DM, DFF])
    w_down = din("w_down", [DFF, DM])
    gmix_d = din("gmix", [128, 8])
    gffn_d = din("gffn", [128, 8])
    lam_d = din("da_lambda", [1, 256])
    subg_d = din("da_subln_g", [1, 128])
    convw_d = din("convw", [128, 8, 4])
    convb_d = din("convb", [128, 8])
    gateb_d = din("ml_gate_b", [1, 8])
    mlg_d = din("ml_norm_g", [1, 512])
    fing_d = din("final_norm_g", [1, DM])
    ident_d = din("ident", [128, 128], BF16)
    rmat_d = din("rmat", [128, 128], BF16)
    causal_d = din("causal", [128, 128], BF16)
    tri_d = din("tri", [128, 128])
    masks_d = din("masks", [128, 4, 1024], BF16)
    invf_d = din("invf", [128, 1])
    sgn_d = din("sgn", [128, 1])
    y_out = nc.dram_tensor("y", [SO, DM], F32, kind="ExternalOutput").ap()

    QT = dscr("QT", [4, 128, SO])
    KT = dscr("KT", [4, 128, SA])
    V1 = dscr("V1", [4, 128, NT, 130])
    MQT = dscr("MQT", [4, 128, SO])
    MKT = dscr("MKT", [4, 128, SO])
    MK = dscr("MK", [128, NT, 512])
    MV1 = dscr("MV1", [128, NT, 4, 130])
    GSO = dscr("GSO", [128, NTO, 512])
    MIXT = dscr("MIXT", [DM, SO])
    QTb, KTb, V1b, MQTb, MKTb, MKb, MV1b, GSOb, MIXTb = [Buf(n) for n in
                                                          "QT KT V1 MQT MKT MK MV1 GSO MIXT".split()]

    st = ExitStack()
    ARENA_N = 53000
    arena_t = st.enter_context(nc.sbuf_tensor("arena", [128, ARENA_N], F32))
    AR = Arena(arena_t)
    psbig = st.enter_context(nc.psum_tensor("psbig", [128, 4096], F32))
    psum = [T(psbig[:, i * 512:(i + 1) * 512], "ps%d" % i, True) for i in range(8)]
    P = Prog(nc)

    def bufs(ts):
        return [t.b if isinstance(t, T) else t for t in ts]

    def mm(out, lhsT, rhs, start, stop, r, w, skip=False):
        P.add("pe", lambda e: e.matmul(out, lhsT=lhsT, rhs=rhs, start=start, stop=stop, skip_group_check=skip),
              reads=bufs(r), writes=bufs(w))

    def tr(out, in_, ident, r, w):
        P.add("pe", lambda e: e.transpose(out=out, in_=in_, identity=ident), reads=bufs(r), writes=bufs(w))

    def act(out, in_, func, r, w, bias=None, scale=None, accum=None, eng="act"):
        kw = {}
        if bias is not None:
            kw["bias"] = bias
        if scale is not None:
            kw["scale"] = scale
        if accum is not None:
            kw["accum_out"] = accum
        P.add("act", lambda e: e.activation(out=out, in_=in_, func=func, **kw), reads=bufs(r), writes=bufs(w))

    def ts(eng, out, in0, s1, s2, op0, op1, r, w):
        if op1 is None:
            P.add(eng, lambda e: e.tensor_scalar(out=out, in0=in0, scalar1=s1, scalar2=None, op0=op0),
                  reads=bufs(r), writes=bufs(w))
        else:
            P.add(eng, lambda e: e.tensor_scalar(out=out, in0=in0, scalar1=s1, scalar2=s2, op0=op0, op1=op1),
                  reads=bufs(r), writes=bufs(w))

    def tt(eng, out, in0, in1, op, r, w):
        P.add(eng, lambda e: e.tensor_tensor(out=out, in0=in0, in1=in1, op=op), reads=bufs(r), writes=bufs(w))

    def stt(out, in0, scalar, in1, op0, op1, r, w, accum=None):
        if accum is None:
            P.add("dve", lambda e: e.scalar_tensor_tensor(out=out, in0=in0, scalar=scalar, in1=in1, op0=op0, op1=op1),
                  reads=bufs(r), writes=bufs(w))
        else:
            P.add("dve", lambda e: e.scalar_tensor_tensor(out=out, in0=in0, scalar=scalar, in1=in1, op0=op0,
                                                          op1=op1, accum_out=accum), reads=bufs(r), writes=bufs(w))

    def cp(eng, out, in_, r, w):
        if eng == "act":
            act(out, in_, AF.Copy, r, w)
        else:
            P.add(eng, lambda e: e.tensor_copy(out=out, in_=in_), reads=bufs(r), writes=bufs(w))

    def recip(out, in_, r, w):
        P.add("dve", lambda e: e.reciprocal(out=out, in_=in_), reads=bufs(r), writes=bufs(w))

    def memset(eng, ap, val, w):
        P.add(eng, lambda e: e.memset(ap, val), writes=bufs(w))

    STQ = "pool"

    def dma(q, out, in_, r, w, key, final=False):
        if q == "pool":
            q = STQ
        P.add(q, lambda e, s: e.dma_start(out=out, in_=in_).then_inc(s, 16), reads=bufs(r), writes=bufs(w),
              dma=True, semkey=key, out=final)

    def dmas(q, pairs, r, w, key, final=False):
        if q == "pool":
            q = STQ
        def fn(e, s):
            for o, i in pairs:
                e.dma_start(out=o, in_=i).then_inc(s, 16)
        P.add(q, fn, reads=bufs(r), writes=bufs(w), dma=True, semkey=key, ndma=len(pairs), out=final)

    def pconst(n, dt, name, src, q="sp"):
        t = AR.alloc(n, dt, name)
        dma(q, t.ap, src, [], [t], "const")
        return t

    ident = pconst(128, BF16, "ident", ident_d[:, :])
    rmat = pconst(128, BF16, "rmat", rmat_d[:, :])
    causal = pconst(128, BF16, "causal", causal_d[:, :])
    tri = pconst(128, F32, "tri", tri_d[:, :])
    invf = pconst(1, F32, "invf", invf_d[:, :])
    sgn = pconst(1, F32, "sgn", sgn_d[:, :])
    keybias = pconst(NKB, F32, "keybias", keybias_d[:, :])
    flag = pconst(1, F32, "flag", flag_d[:, :])
    gmix = pconst(8, F32, "gmix", gmix_d[:, :])
    gffn = pconst(8, F32, "gffn", gffn_d[:, :])
    convw = pconst(32, F32, "convw", convw_d.rearrange("p b j -> p (b j)"))
    convb = pconst(8, F32, "convb", convb_d[:, :])
    gateb = pconst(8, F32, "gateb", gateb_d[0:1, :].partition_broadcast(128))
    mlg = pconst(512, F32, "mlg", mlg_d[0:1, :].partition_broadcast(128))
    fing = pconst(DM, F32, "fing", fing_d[0:1, :].partition_broadcast(128))
    g08 = pconst(128, F32, "g08", subg_d[0:1, :].partition_broadcast(128))
    lamv = pconst(256, F32, "lamv", lam_d[0:1, :].partition_broadcast(128))
    P.barrier()
    ones128 = AR.alloc(128, F32, "ones128")
    memset("dve", ones128.ap, 1.0, [ones128])
    neghalf = AR.alloc(1, F32, "neghalf")
    memset("dve", neghalf.ap, -0.5, [neghalf])
    onecol = AR.alloc(1, F32, "onecol")
    memset("dve", onecol.ap, 1.0, [onecol])
    GP = AR.alloc(NT * 12, F32, "GP")
    GPv = GP.ap.rearrange("p (t c) -> p t c", c=12)
    small = AR.alloc(64, F32, "small")
    lam = T(small.ap[:, 0:1], "lam")
    neglam = T(small.ap[:, 1:2], "neglam")
    s01 = T(small.ap[:, 2:3], "s01")
    s23 = T(small.ap[:, 3:4], "s23")
    junk = AR.alloc(256, F32, "junkc")
    ts("dve", g08.ap, g08.ap, 0.8, None, ALU.mult, None, [g08], [g08])
    stt(junk.ap[:, 0:64], lamv.ap[:, 0:64], 1.0, lamv.ap[:, 64:128], ALU.mult, ALU.mult, [lamv], [junk, s01], accum=s01.ap)
    stt(junk.ap[:, 0:64], lamv.ap[:, 128:192], 1.0, lamv.ap[:, 192:256], ALU.mult, ALU.mult, [lamv], [junk, s23], accum=s23.ap)
    act(s01.ap, s01.ap, AF.Exp, [s01], [s01])
    act(s23.ap, s23.ap, AF.Exp, [s23], [s23])
    tt("dve", lam.ap, s01.ap, s23.ap, ALU.subtract, [s01, s23], [lam])
    ts("dve", lam.ap, lam.ap, 0.2, None, ALU.add, None, [lam], [lam])
    ts("dve", neglam.ap, lam.ap, -1.0, None, ALU.mult, None, [lam], [neglam])
    PERSIST = AR.off
    P.barrier()

    def finish_early():
        dma("sp", y_out[0:128, :], fing.ap, [fing], [], "const", final=True)
        P.emit(st)
        st.close()
        return nc

    if upto == "0":
        return finish_early()
    AR.off = PERSIST
    win = AR.alloc(8 * INW, BF16, "win")
    winv = win.ap.rearrange("p (k c) -> p k c", c=INW)
    TAB = 4096 if max(SP, SO) > 2048 else max(SP, SO)
    Ctab = AR.alloc(max(SP, SO), F32, "Ctab")
    Stab = AR.alloc(max(SP, SO), F32, "Stab")
    ttmp = [AR.alloc(512, F32, "ttmp%d" % i) for i in range(3)]
    tint = AR.alloc(512, I32, "tint")
    A_MARK = AR.off
    wst = [AR.alloc(1796, F32, "wst%d" % i) for i in range(2)]
    n = 0
    for k in range(8):
        for c0 in (0, 1796):
            s = wst[n % 2]
            dma("sp", s.ap, w_in[k * 128:(k + 1) * 128, c0:c0 + 1796], [], [s], s.b.name)
            ts("dve", winv[:, k, c0:c0 + 1796], s.ap, gmix.ap[:, k:k + 1], None, ALU.mult, None,
               [s, gmix], [win])
            n += 1


    def build_tables(t0, ntok):
        for c0 in range(0, ntok, 512):
            cn = min(512, ntok - c0)
            pi_, ang, u = ttmp[0], ttmp[1], ttmp[2]
            dma("sp", tint.ap[:, 0:cn], pos[0:1, t0 + c0:t0 + c0 + cn].partition_broadcast(128), [], [tint], "tint")
            cp("dve", pi_.ap[:, 0:cn], tint.ap[:, 0:cn], [tint], [pi_])
            ts("dve", ang.ap[:, 0:cn], pi_.ap[:, 0:cn], invf.ap, None, ALU.mult, None, [pi_, invf], [ang])
            for tab, shift in ((Stab, 0.0), (Ctab, 0.25)):
                ts("dve", u.ap[:, 0:cn], ang.ap[:, 0:cn], 1.0 / TWO_PI, shift, ALU.mult, ALU.add, [ang], [u])
                cp("dve", tint.ap[:, 0:cn], u.ap[:, 0:cn], [u], [tint])
                cp("dve", pi_.ap[:, 0:cn], tint.ap[:, 0:cn], [tint], [pi_])
                tt("dve", u.ap[:, 0:cn], u.ap[:, 0:cn], pi_.ap[:, 0:cn], ALU.subtract, [u, pi_], [u])
                ts("dve", u.ap[:, 0:cn], u.ap[:, 0:cn], TWO_PI, None, ALU.mult, None, [u], [u])
                ts("dve", u.ap[:, 0:cn], u.ap[:, 0:cn], 3.1415925, -3.1415925, ALU.min, ALU.max, [u], [u])
                act(tab.ap[:, c0:c0 + cn], u.ap[:, 0:cn], AF.Sin, [u], [tab])
            ts("dve", Stab.ap[:, c0:c0 + cn], Stab.ap[:, c0:c0 + cn], sgn.ap, None, ALU.mult, None, [Stab, sgn], [Stab])

    build_tables(0, SP)
    P.barrier()
    if upto == "A0":
        return finish_early()
    AR.off = A_MARK
    xs = [AR.alloc(DM, F32, "xs%d" % i) for i in range(4)]
    hb = [AR.alloc(DM, BF16, "hb%d" % i) for i in range(4)]
    hT = [AR.alloc(8 * 512, BF16, "hT%d" % i) for i in range(2)]
    zc = [AR.alloc(515, F32, "zc%d" % i) for i in range(8)]
    zb = [AR.alloc(512, BF16, "zb%d" % i) for i in range(2)]
    t1 = [AR.alloc(512, F32, "t1_%d" % i) for i in range(2)]
    t2 = [AR.alloc(512, F32, "t2_%d" % i) for i in range(2)]
    yc = [AR.alloc(512, F32, "yc%d" % i) for i in range(2)]
    ost = [AR.alloc(512, BF16, "ost%d" % i) for i in range(3)]
    kst = [AR.alloc(4 * 512, BF16, "kst%d" % i) for i in range(2)]
    vst = [AR.alloc(520, BF16, "vst%d" % i) for i in range(4)]
    ktk = [AR.alloc(512, BF16, "ktk%d" % i) for i in range(2)]
    ef = [AR.alloc(512, F32, "ef%d" % i) for i in range(2)]
    gst = [AR.alloc(512, BF16, "gst%d" % i) for i in range(2)]
    ssA = [AR.alloc(4, F32, "ssA%d" % i) for i in range(4)]
    gts = [AR.alloc(16, F32, "gts%d" % i) for i in range(4)]
    sqj = AR.alloc(DM, BF16, "sqj")
    for v_ in vst:
        memset("dve", v_.ap, 0.0, [v_])
        memset("dve", v_.ap.rearrange("p (h c) -> p h c", c=130)[:, :, 128:129], 1.0, [v_])
    for z_ in zc:
        memset("dve", z_.ap[:, 0:3], 0.0, [z_])
    PS_TOK = [psum[0], psum[1]]
    PS_FM = [psum[2], psum[3]]
    PS_RZ = psum[4]
    PS_TR = psum[5]
    PS_G = psum[6]
    PS_KT = psum[7]
    cnt = {"x": 0, "tok": 0, "fm": 0, "z": 0, "o": 0, "v": 0, "k": 0, "e": 0, "g": 0, "y": 0, "kst": 0}

    def nxt(key, lst):
        i = cnt[key]
        cnt[key] += 1
        return lst[i % len(lst)]

    def lnt_dma(xsrc_rows, slot):
        x_ = xs[slot]
        dma("sp", x_.ap, xsrc_rows, [], [x_], x_.b.name)

    def lnt_compute(slot):
        x_, ss_, h_ = xs[slot], ssA[slot], hb[slot]
        act(sqj.ap, x_.ap, AF.Square, [x_], [sqj, ss_], accum=ss_.ap[:, 0:1])
        ts("dve", ss_.ap[:, 1:2], ss_.ap[:, 0:1], 1.0 / DM, EPS, ALU.mult, ALU.add, [ss_], [ss_])
        act(ss_.ap[:, 2:3], ss_.ap[:, 1:2], AF.Ln, [ss_], [ss_])
        act(ss_.ap[:, 2:3], ss_.ap[:, 2:3], AF.Exp, [ss_], [ss_], scale=-0.5)
        ts("dve", h_.ap, x_.ap, ss_.ap[:, 2:3], None, ALU.mult, None, [x_, ss_], [h_])

    def lnt_transpose(slot, hT_t, col):
        h_ = hb[slot]
        pst = PS_TR.ap.bitcast(BF16)
        for k in range(8):
            tr(pst[:, k * 128:(k + 1) * 128], h_.ap[:, k * 128:(k + 1) * 128], ident.ap, [h_, ident], [PS_TR])
        hv = hT_t.ap.rearrange("p (k t) -> p k t", t=512)
        cp("act", hv[:, :, col:col + 128], pst.rearrange("p (k t) -> p k t", t=128), [PS_TR], [hT_t])

    def load_norm_transpose(xsrc_rows, hT_t, col):
        lnt_dma(xsrc_rows, 0)
        lnt_compute(0)
        lnt_transpose(0, hT_t, col)

    def tok_matmul(hT_t, col, c0, ncols, ps):
        hv = hT_t.ap.rearrange("p (k t) -> p k t", t=512)
        for k in range(8):
            mm(ps.ap[:, 0:ncols], hv[:, k, col:col + 128], winv[:, k, c0:c0 + ncols], k == 0, k == 7, [hT_t, win], [ps])

    def fm_matmul(hT_t, c0, ps, ntok=512):
        hv = hT_t.ap.rearrange("p (k t) -> p k t", t=512)
        for k in range(8):
            mm(ps.ap[:, 0:ntok], winv[:, k, c0:c0 + 128], hv[:, k, 0:ntok], k == 0, k == 7, [hT_t, win], [ps])

    def rope_block(ps, tab0, dst, dstb):
        z_ = nxt("z", zb)
        i = (cnt["z"] - 1) % 2
        cp("act", z_.ap, ps.ap, [ps], [z_])
        mm(PS_RZ.ap, rmat.ap, z_.ap, True, True, [rmat, z_], [PS_RZ])
        tt("dve", t1[i].ap, ps.ap, Ctab.ap[:, tab0:tab0 + 512], ALU.mult, [ps, Ctab], [t1[i]])
        tt("dve", t2[i].ap, PS_RZ.ap, Stab.ap[:, tab0:tab0 + 512], ALU.mult, [PS_RZ, Stab], [t2[i]])
        o_ = nxt("o", ost)
        tt("dve", o_.ap, t1[i].ap, t2[i].ap, ALU.add, [t1[i], t2[i]], [o_])
        dma("pool", dst, o_.ap, [o_], [dstb], o_.b.name)

    def conv_block(ps, blk, dst_ap):
        z_ = zc[blk]
        cp("act", z_.ap[:, 3:515], ps.ap, [ps], [z_])
        y_ = nxt("y", yc)
        cw = convw.ap.rearrange("p (b j) -> p b j", j=4)
        ts("dve", y_.ap, z_.ap[:, 3:515], cw[:, blk, 3:4], convb.ap[:, blk:blk + 1], ALU.mult, ALU.add, [z_, convw, convb], [y_])
        for j in (2, 1, 0):
            stt(y_.ap, z_.ap[:, j:j + 512], cw[:, blk, j:j + 1], y_.ap, ALU.mult, ALU.add, [z_, convw, y_], [y_])
        cp("dve", z_.ap[:, 0:3], z_.ap[:, 512:515], [z_], [z_])
        return y_

    def gates_part1(ps):
        g_ = nxt("g", gts)
        tt("dve", g_.ap[:, 0:8], ps.ap[:, 0:8], gateb.ap, ALU.add, [ps, gateb], [g_])
        act(g_.ap[:, 8:12], g_.ap[:, 4:8], AF.Exp, [g_], [g_], scale=-1.0)
        act(g_.ap[:, 8:12], g_.ap[:, 8:12], AF.Ln, [g_], [g_], bias=onecol.ap)
        return g_

    def gates_part2(g_, tile_idx):
        mm(PS_G.ap[:, 0:4], tri.ap, g_.ap[:, 8:12], True, True, [tri, g_], [PS_G])
        mm(PS_G.ap[:, 4:8], ones128.ap, g_.ap[:, 8:12], True, True, [ones128, g_], [PS_G])
        tt("dve", g_.ap[:, 12:16], g_.ap[:, 0:4], PS_G.ap[:, 0:4], ALU.add, [g_, PS_G], [g_])
        act(GPv[:, tile_idx, 0:4], g_.ap[:, 12:16], AF.Exp, [g_], [GP])
        act(GPv[:, tile_idx, 4:12], PS_G.ap[:, 0:8], AF.Exp, [PS_G], [GP], scale=-1.0)
        ts("dve", GPv[:, tile_idx, 4:8], GPv[:, tile_idx, 4:8], 128.0 ** -0.5, None, ALU.mult, None, [GP], [GP])


    def xrows(own, g, j):
        xsrc = x_own if own else x_pre
        row0 = (128 if own else 0) + g * 512
        return xsrc[row0 + j * 128:row0 + (j + 1) * 128, :]

    def lnt_group(own, g):
        hT_t = hT[(g + (NGP if own else 0)) % 2]
        for j in range(4):
            lnt_dma(xrows(own, g, j), j)
        for j in range(4):
            lnt_compute(j)
        for j in range(4):
            lnt_transpose(j, hT_t, j * 128)

    pend_k = []

    def phaseA_group(own, g, nxt_grp):
        hT_t = hT[(g + (NGP if own else 0)) % 2]
        if nxt_grp is not None:
            for j in range(4):
                lnt_dma(xrows(*nxt_grp, j), j)
        pend_g = []
        tile0 = (NTP + g * 4) if own else g * 4
        tab0 = g * 512
        for j in range(4):
            tile = tile0 + j
            col = j * 128
            ps = nxt("tok", PS_TOK)
            tok_matmul(hT_t, col, 1024, 512, ps)
            v_ = nxt("v", vst)
            cp("act", v_.ap.rearrange("p (h c) -> p h c", c=130)[:, :, 0:128], ps.ap.rearrange("p (h c) -> p h c", c=128), [ps], [v_])
            dma("pool", V1[:, :, tile, :].rearrange("h p c -> p h c"), v_.ap.rearrange("p (h c) -> p h c", c=130), [v_], [V1b], v_.b.name)
            ps = nxt("tok", PS_TOK)
            tok_matmul(hT_t, col, 2560, 512, ps)
            v_ = nxt("v", vst)
            cp("act", v_.ap.rearrange("p (h c) -> p h c", c=130)[:, :, 0:128], ps.ap.rearrange("p (h c) -> p h c", c=128), [ps], [v_])
            dma("pool", MV1[:, tile, :, :], v_.ap.rearrange("p (h c) -> p h c", c=130), [v_], [MV1b], v_.b.name)
            ps = nxt("tok", PS_TOK)
            tok_matmul(hT_t, col, 3584, 8, ps)
            g_ = gates_part1(ps)
            while pend_g:
                gates_part2(*pend_g.pop(0))
            pend_g.append((g_, tile))
            if nxt_grp is not None and j == 1:
                for jj in range(4):
                    lnt_compute(jj)
            if own:
                ps = nxt("tok", PS_TOK)
                tok_matmul(hT_t, col, 3072, 512, ps)
                e_ = nxt("e", ef)
                i = (cnt["e"] - 1) % 2
                act(e_.ap, ps.ap, AF.Exp, [ps], [e_], scale=-1.0)
                ts("dve", e_.ap, e_.ap, 1.0, None, ALU.add, None, [e_], [e_])
                recip(e_.ap, e_.ap, [e_], [e_])
                tt("dve", gst[i].ap, e_.ap, mlg.ap, ALU.mult, [e_, mlg], [gst[i]])
                dma("pool", GSO[:, tile - NTP, :], gst[i].ap, [gst[i]], [GSOb], gst[i].b.name)
        while pend_k:
            pend_k.pop(0)()
        if nxt_grp is not None:
            hT_n = hT[(nxt_grp[1] + (NGP if nxt_grp[0] else 0)) % 2]
            for jj in range(4):
                lnt_transpose(jj, hT_n, jj * 128)
        while pend_g:
            gates_part2(*pend_g.pop(0))
        ks_ = nxt("kst", kst)
        ksv = ks_.ap.rearrange("p (h t) -> p h t", t=512)
        kbase = SP if own else 0

        def mlq_post(ps, h):
            y_ = conv_block(ps, h, None)

            def fin():
                o_ = nxt("o", ost)
                act(o_.ap, y_.ap, AF.Silu, [y_], [o_])
                dma("pool", MQT[h, :, g * 512:(g + 1) * 512], o_.ap, [o_], [MQTb], o_.b.name)
            return fin

        def mlk_post(ps, h):
            y_ = conv_block(ps, 4 + h, None)

            def fin():
                act(ksv[:, h, :], y_.ap, AF.Silu, [y_], [ks_])
            return fin

        blocks = []
        for h in range(4):
            if own:
                blocks.append((h * 128, lambda ps, h=h: rope_block(ps, tab0, QT[h, :, g * 512:(g + 1) * 512], QTb)))
            blocks.append((512 + h * 128, lambda ps, h=h: rope_block(ps, tab0, KT[h, :, kbase + g * 512:kbase + (g + 1) * 512], KTb)))
        if own:
            for h in range(4):
                blocks.append((1536 + h * 128, lambda ps, h=h: mlq_post(ps, h)))
        for h in range(4):
            blocks.append((2048 + h * 128, lambda ps, h=h: mlk_post(ps, h)))
        pend_fin = []
        ps_cur = nxt("fm", PS_FM)
        fm_matmul(hT_t, blocks[0][0], ps_cur)
        for b in range(len(blocks)):
            ps_next = None
            if b + 1 < len(blocks):
                ps_next = nxt("fm", PS_FM)
                fm_matmul(hT_t, blocks[b + 1][0], ps_next)
            fin = blocks[b][1](ps_cur)
            if pend_fin:
                pend_fin.pop(0)()
            if fin is not None:
                pend_fin.append(fin)
            ps_cur = ps_next
        while pend_fin:
            pend_fin.pop(0)()
        if own:
            dma("pool", MKT[:, :, g * 512:(g + 1) * 512].rearrange("h p t -> p h t"), ksv, [ks_], [MKTb], ks_.b.name)
        def ktrans():
            pkt = PS_KT.ap.bitcast(BF16)
            for j in range(4):
                for h in range(4):
                    tr(pkt[:, h * 128:(h + 1) * 128], ksv[:, h, j * 128:(j + 1) * 128], ident.ap, [ks_, ident], [PS_KT])
                k_ = nxt("k", ktk)
                cp("dve", k_.ap, pkt[:, 0:512], [PS_KT], [k_])
                dma("pool", MK[:, tile0 + j, :], k_.ap, [k_], [MKb], k_.b.name)
        pend_k.append(ktrans)

    def halo_step():
        hT_t = hT[0]
        load_norm_transpose(x_own[0:128, :], hT_t, 0)
        for blk in range(8):
            ps = nxt("fm", PS_FM)
            fm_matmul(hT_t, (1536 if blk < 4 else 2048) + (blk % 4) * 128, ps, ntok=128)
            cp("act", zc[blk].ap[:, 0:3], ps.ap[:, 125:128], [ps], [zc[blk]])

    lnt_group(False, 0)
    for g in range(NGP):
        phaseA_group(False, g, (False, g + 1) if g + 1 < NGP else None)
    while pend_k:
        pend_k.pop(0)()
    build_tables(SP, SO)
    halo_step()
    lnt_group(True, 0)
    for g in range(NGO):
        phaseA_group(True, g, (True, g + 1) if g + 1 < NGO else None)
    while pend_k:
        pend_k.pop(0)()
    P.barrier()
    if upto == "A":
        return finish_early()

    AR.off = PERSIST
    masks = AR.alloc(4 * 1024, BF16, "masks")
    dma("sp", masks.ap, masks_d.rearrange("p j c -> p (j c)"), [], [masks], "masks")
    masksv = masks.ap.rearrange("p (j c) -> p j c", c=1024)
    kts = [AR.alloc(SA, BF16, "kts%d" % i) for i in range(2)]
    vs = [AR.alloc(NT * 130, BF16, "vs%d" % i) for i in range(2)]
    qts = [AR.alloc(SO, BF16, "qts%d" % i) for i in range(2)]
    pT = [AR.alloc(1024, BF16, "pT%d" % i) for i in range(3)]
    osb = [AR.alloc(8 * 129, F32, "osb%d" % i) for i in range(2)]
    ob = AR.alloc(128, F32, "ob")
    aob = [AR.alloc(128, BF16, "aob%d" % i) for i in range(2)]
    ast = [AR.alloc(512, BF16, "ast%d" % i) for i in range(2)]
    sc8 = [AR.alloc(8, F32, "sc8_%d" % i) for i in range(2)]
    ST = [T(None, "ST0", True), T(None, "ST1", True)]
    OB = [psum[4], psum[5], psum[6]]

    def oreg(r):
        return OB[r // 3].ap[:, (r % 3) * 129:(r % 3) * 129 + 129], OB[r // 3]

    nB = {"p": 0, "o": 0, "a": 0, "st": 0}
    zer = AR.alloc(128, BF16, "zer")
    memset("dve", zer.ap, 0.0, [zer])
    STap = [psbig[:, 0:1024], psbig[:, 1024:2048]]
    steps = [(h, G, kb) for h in range(4) for G in range(NGO) for kb in range(NTP + 4 * (G + 1))]
    hbuf = {}

    def head_bufs(h):
        if h not in hbuf:
            kt_, v_, qt_ = kts[h % 2], vs[h % 2], qts[h % 2]
            dma("sp", kt_.ap, KT[h, :, :], [KTb], [kt_], kt_.b.name)
            dma("sp", v_.ap, V1[h, :, :, :].rearrange("p t c -> p (t c)"), [V1b], [v_], v_.b.name)
            dma("sp", qt_.ap, QT[h, :, :], [QTb], [qt_], qt_.b.name)
            hbuf[h] = (kt_, v_, qt_)
        return hbuf[h]

    def emit_st(i):
        h, G, kb = steps[i]
        kt_, v_, qt_ = head_bufs(h)
        stb = ST[i % 2]
        sap = STap[i % 2]
        mm(sap[:, 0:512], kt_.ap[0:64, kb * 128:(kb + 1) * 128], qt_.ap[0:64, G * 512:(G + 1) * 512], True, True, [kt_, qt_], [stb])
        mm(sap[:, 512:1024], kt_.ap[64:128, kb * 128:(kb + 1) * 128], qt_.ap[64:128, G * 512:(G + 1) * 512], True, True, [kt_, qt_], [stb])

    def post_part2(h, G, o_, s8):
        ov = o_.ap.rearrange("p (r c) -> p r c", c=129)
        recip(s8.ap, ov[:, :, 128], [o_], [s8])
        ts("dve", s8.ap[:, 4:8], s8.ap[:, 4:8], neglam.ap, None, ALU.mult, None, [s8, neglam], [s8])
        a_ = ast[nB["a"] % 2]
        nB["a"] += 1
        pst = psum[7].ap.bitcast(BF16)
        for qb in range(4):
            ts("dve", ob.ap, ov[:, qb, 0:128], s8.ap[:, qb:qb + 1], None, ALU.mult, None, [o_, s8], [ob])
            stt(ob.ap, ov[:, 4 + qb, 0:128], s8.ap[:, 4 + qb:5 + qb], ob.ap, ALU.mult, ALU.add, [o_, s8, ob], [ob])
            stt(junk.ap[:, 0:128], ob.ap, 1.0, ob.ap, ALU.mult, ALU.mult, [ob], [junk, small], accum=small.ap[:, 8 + qb:9 + qb])
            ts("dve", small.ap[:, 12 + qb:13 + qb], small.ap[:, 8 + qb:9 + qb], 1.0 / 128, EPS, ALU.mult, ALU.add, [small], [small])
            act(small.ap[:, 16 + qb:17 + qb], small.ap[:, 12 + qb:13 + qb], AF.Ln, [small], [small])
            act(small.ap[:, 16 + qb:17 + qb], small.ap[:, 16 + qb:17 + qb], AF.Exp, [small], [small], scale=-0.5)
            ab = aob[qb % 2]
            stt(ab.ap, ob.ap, small.ap[:, 16 + qb:17 + qb], g08.ap, ALU.mult, ALU.mult, [ob, small, g08], [ab])
            tr(pst[:, qb * 128:(qb + 1) * 128], ab.ap, ident.ap, [ab, ident], [psum[7]])
        cp("dve", a_.ap, pst[:, 0:512], [psum[7]], [a_])
        dma("pool", MIXT[h * 128:(h + 1) * 128, G * 512:(G + 1) * 512], a_.ap, [a_], [MIXTb], a_.b.name)

    deferred = []
    emit_st(0)
    if len(steps) > 1:
        emit_st(1)
    for i, (h, G, kb) in enumerate(steps):
        kt_, v_, qt_ = head_bufs(h)
        vv = v_.ap.rearrange("p (t c) -> p t c", c=130)
        while deferred and deferred[0][0] <= i:
            deferred.pop(0)[1]()
        stb = ST[i % 2]
        p_ = pT[nB["p"] % 3]
        nB["p"] += 1
        act(p_.ap, STap[i % 2], AF.Exp, [stb, keybias], [p_], bias=keybias.ap[:, kb:kb + 1], scale=0.125)
        j = kb - (NTP + 4 * G)
        if j >= 0:
            tt("dve", p_.ap, p_.ap, masksv[:, j, :], ALU.mult, [p_, masks], [p_])
        if i + 2 < len(steps):
            emit_st(i + 2)
        if kb == 0:
            for bnk in range(3):
                nreg = 3 if bnk < 2 else 2
                memset("dve", OB[bnk].ap[:, 0:nreg * 129], 0.0, [OB[bnk]])
        for m in range(2):
            for qb in range(4):
                last = NTP + 4 * G + qb
                if kb > last:
                    continue
                oap, obuf = oreg(m * 4 + qb)
                mm(oap, p_.ap[:, m * 512 + qb * 128:m * 512 + (qb + 1) * 128], vv[:, kb, 0:129], False, kb == last, [p_, v_], [obuf], skip=True)
        if kb == NTP + 4 * (G + 1) - 1:
            o_ = osb[nB["o"] % 2]
            s8 = sc8[nB["o"] % 2]
            nB["o"] += 1
            for bnk in range(3):
                nreg = 3 if bnk < 2 else 2
                cp("dve", o_.ap[:, bnk * 387:bnk * 387 + nreg * 129], OB[bnk].ap[:, 0:nreg * 129], [OB[bnk]], [o_])
            deferred.append((i + 6, (lambda h=h, G=G, o_=o_, s8=s8: post_part2(h, G, o_, s8))))
    while deferred:
        deferred.pop(0)[1]()
    P.barrier()
    if upto == "B":
        return finish_early()

    AR.off = PERSIST
    wo = AR.alloc(8 * DM, BF16, "wo")
    wg = AR.alloc(8 * DFF, BF16, "wg")
    wu = AR.alloc(8 * DFF, BF16, "wu")
    WD_OFF = AR.off
    wov = wo.ap.rearrange("p (k c) -> p k c", c=DM)
    wgv = wg.ap.rearrange("p (k c) -> p k c", c=DFF)
    wuv = wu.ap.rearrange("p (k c) -> p k c", c=DFF)
    pf_chunks = [(w_out[k * 128:(k + 1) * 128, :], wov[:, k, :], None) for k in range(8)]
    for k in range(8):
        for c0 in (0, 1408):
            pf_chunks.append((w_gate[k * 128:(k + 1) * 128, c0:c0 + 1408], wgv[:, k, c0:c0 + 1408], k))
            pf_chunks.append((w_up[k * 128:(k + 1) * 128, c0:c0 + 1408], wuv[:, k, c0:c0 + 1408], k))
    mks = [AR.alloc(4 * 512, BF16, "mks%d" % i) for i in range(2)]
    mvs = [AR.alloc(4 * 520, BF16, "mvs%d" % i) for i in range(2)]
    mqs = [AR.alloc(4 * 512, BF16, "mqs%d" % i) for i in range(2)]
    mkts = [AR.alloc(4 * 512, BF16, "mkts%d" % i) for i in range(2)]
    gss = [AR.alloc(4 * 512, BF16, "gss%d" % i) for i in range(2)]
    Cf = AR.alloc(4 * 129, F32, "Cf")
    Cb = AR.alloc(4 * 130, BF16, "Cb")
    kw = [AR.alloc(128, BF16, "kw%d" % i) for i in range(8)]
    scp = [AR.alloc(128, BF16, "scp%d" % i) for i in range(3)]
    hm = [AR.alloc(128, BF16, "hm%d" % i) for i in range(3)]
    hst = [AR.alloc(4 * 512, BF16, "hst%d" % i) for i in range(2)]
    pc = [AR.alloc(32, F32, "pc%d" % i) for i in range(2)]
    wstC = [AR.alloc(1408, F32, "wstC%d" % i) for i in range(4)]
    pf_state = {"dma": 0, "cvt": 0}

    def prefetch_step():
        n = pf_state["dma"]
        if n < len(pf_chunks):
            src, dst, k = pf_chunks[n]
            s_ = wstC[n % 4]
            dma("sp", s_.ap[:, 0:dst.shape[1]], src, [], [s_], s_.b.name)
            pf_state["dma"] += 1
        m = pf_state["cvt"]
        if m < len(pf_chunks) and (pf_state["dma"] - m >= 4 or pf_state["dma"] == len(pf_chunks)):
            src, dst, k = pf_chunks[m]
            s_ = wstC[m % 4]
            ncol = dst.shape[1]
            eng = "dve" if m % 2 == 0 else "act"
            if k is None:
                cp(eng, dst, s_.ap[:, 0:ncol], [s_], [wo])
            elif eng == "act":
                act(dst, s_.ap[:, 0:ncol], AF.Copy, [s_, gffn], [wo], scale=gffn.ap[:, k:k + 1])
            else:
                ts("dve", dst, s_.ap[:, 0:ncol], gffn.ap[:, k:k + 1], None, ALU.mult, None, [s_, gffn], [wo])
            pf_state["cvt"] += 1

    Cfv = Cf.ap.rearrange("p (h c) -> p h c", c=129)
    Cbv = Cb.ap.rearrange("p (h c) -> p h c", c=130)
    PS_SC = [psum[0], psum[1]]
    PS_N = [T(None, "N0", True), T(None, "N1", True)]
    PS_U = psum[6]
    PS_T = psum[7]
    nC = {"kw": 0, "scp": 0, "hm": 0, "sc": 0}

    def make_kw(kk, tile, h, mk_t):
        k_ = kw[nC["kw"] % 8]
        nC["kw"] += 1
        act(k_.ap, kk, AF.Copy, [GP, mk_t], [k_], scale=GPv[:, tile, h:h + 1])
        return k_

    def state_update(k_, vv4, tile, h, need_cb, mk_t, mv_t, PS_U=psum[6]):
        mm(PS_U.ap[:, 0:129], k_.ap, vv4[:, h, 0:129], True, True, [k_, mv_t], [PS_U])
        act(Cfv[:, h, :], Cfv[:, h, :], AF.Copy, [GP, Cfh[h]], [Cfh[h]], scale=GPv[:, tile, 8 + h:9 + h])
        stt(Cfv[:, h, :], PS_U.ap[:, 0:129], GPv[:, tile, 8 + h:9 + h], Cfv[:, h, :], ALU.mult, ALU.add, [PS_U, GP, Cfh[h]], [Cfh[h]])
        if need_cb:
            cp("act", Cbv[:, h, 0:129], Cfv[:, h, :], [Cfh[h]], [Cbh[h]])

    Cfh = [Buf("Cf%d" % h) for h in range(4)]
    Cbh = [Buf("Cb%d" % h) for h in range(4)]
    memset("dve", Cf.ap, 0.0, Cfh)
    memset("dve", Cb.ap, 0.0, Cbh)
    for grp in range(NGP + NGO):
        own = grp >= NGP
        go = grp - NGP
        mk_, mv_ = mks[grp % 2], mvs[grp % 2]
        dma("sp", mk_.ap.rearrange("p (t c) -> p t c", c=512), MK[:, grp * 4:(grp + 1) * 4, :], [MKb], [mk_], mk_.b.name)
        dma("sp", mv_.ap.rearrange("p (t c) -> p t c", c=520), MV1[:, grp * 4:(grp + 1) * 4, :, :].rearrange("p t h c -> p t (h c)"),
            [MV1b], [mv_], mv_.b.name)
        if own:
            mq_, mkt_, gs_, hs_ = mqs[go % 2], mkts[go % 2], gss[go % 2], hst[go % 2]
            dma("sp", mq_.ap.rearrange("p (h t) -> p h t", t=512), MQT[:, :, go * 512:(go + 1) * 512].rearrange("h p t -> p h t"), [MQTb], [mq_], mq_.b.name)
            dma("sp", mkt_.ap.rearrange("p (h t) -> p h t", t=512), MKT[:, :, go * 512:(go + 1) * 512].rearrange("h p t -> p h t"), [MKTb], [mkt_], mkt_.b.name)
            dma("sp", gs_.ap.rearrange("p (t c) -> p t c", c=512), GSO[:, go * 4:(go + 1) * 4, :], [GSOb], [gs_], gs_.b.name)
            mqv = mq_.ap.rearrange("p (h t) -> p h t", t=512)
            mktv = mkt_.ap.rearrange("p (h t) -> p h t", t=512)
            gsv = gs_.ap.rearrange("p (t c) -> p t c", c=512)
            hsv = hs_.ap.rearrange("p (h t) -> p h t", t=512)
        mkv = mk_.ap.rearrange("p (t c) -> p t c", c=512)
        mvv = mv_.ap.rearrange("p (t h c) -> p t h c", h=4, c=130)
        if own and go == 0:
            P.barrier()
            for h in range(4):
                ts("dve", Cfv[:, h, :], Cfv[:, h, :], flag.ap, None, ALU.mult, None, [Cfh[h], flag], [Cfh[h]])
                cp("act", Cbv[:, h, 0:129], Cfv[:, h, :], [Cfh[h]], [Cbh[h]])
        for j in range(4):
            tile = grp * 4 + j
            if not own:
                prefetch_step()
                prefetch_step()
                kws = [make_kw(mkv[:, j, h * 128:(h + 1) * 128], tile, h, mk_) for h in range(4)]
                for h in range(4):
                    state_update(kws[h], mvv[:, j], tile, h, False, mk_, mv_, psum[(tile * 4 + h) % 7])
                continue
            nb_ = PS_N[tile % 2]
            nbank = 2 + 2 * (tile % 2)
            kws = [make_kw(mkv[:, j, h * 128:(h + 1) * 128], tile, h, mk_) for h in range(4)]
            for h in range(4):
                sc_ = PS_SC[nC["sc"] % 2]
                nC["sc"] += 1
                mm(sc_.ap[:, 0:128], mktv[:, h, j * 128:(j + 1) * 128], mqv[:, h, j * 128:(j + 1) * 128], True, True, [mkt_, mq_], [sc_])
                s_ = scp[nC["scp"] % 3]
                nC["scp"] += 1
                stt(s_.ap, sc_.ap[:, 0:128], GPv[:, tile, h:h + 1], causal.ap, ALU.mult, ALU.mult, [sc_, GP, causal], [s_])
                nreg = psum[nbank + h // 2].ap[:, (h % 2) * 256:(h % 2) * 256 + 129]
                mm(nreg, s_.ap, mvv[:, j, h, 0:129], True, False, [s_, mv_], [nb_])
                mm(nreg, mqv[:, h, j * 128:(j + 1) * 128], Cbv[:, h, 0:129], False, True, [mq_, Cbh[h]], [nb_])
                state_update(kws[h], mvv[:, j], tile, h, True, mk_, mv_)
            p_ = pc[tile % 2]
            for h in range(4):
                nreg = psum[nbank + h // 2].ap[:, (h % 2) * 256:(h % 2) * 256 + 129]
                tt("dve", p_.ap[:, h:h + 1], nreg[:, 128:129], GPv[:, tile, 4 + h:5 + h], ALU.mult, [nb_, GP], [p_])
                act(junk.ap[:, 0:128], nreg[:, 0:128], AF.Square, [nb_], [junk, p_], accum=p_.ap[:, 8 + h:9 + h])
            stt(p_.ap[:, 4:8], p_.ap[:, 0:4], -1.0, p_.ap[:, 0:4], ALU.mult, ALU.max, [p_], [p_])
            ts("dve", p_.ap[:, 4:8], p_.ap[:, 4:8], 1.0, None, ALU.max, None, [p_], [p_])
            recip(p_.ap[:, 4:8], p_.ap[:, 4:8], [p_], [p_])
            tt("dve", p_.ap[:, 4:8], p_.ap[:, 4:8], GPv[:, tile, 4:8], ALU.mult, [p_, GP], [p_])
            tt("dve", p_.ap[:, 8:12], p_.ap[:, 8:12], p_.ap[:, 4:8], ALU.mult, [p_], [p_])
            tt("dve", p_.ap[:, 8:12], p_.ap[:, 8:12], p_.ap[:, 4:8], ALU.mult, [p_], [p_])
            ts("dve", p_.ap[:, 8:12], p_.ap[:, 8:12], 1.0 / 128, EPS, ALU.mult, ALU.add, [p_], [p_])
            act(p_.ap[:, 12:16], p_.ap[:, 8:12], AF.Ln, [p_], [p_])
            act(p_.ap[:, 12:16], p_.ap[:, 12:16], AF.Exp, [p_], [p_], scale=-0.5)
            tt("dve", p_.ap[:, 16:20], p_.ap[:, 12:16], p_.ap[:, 4:8], ALU.mult, [p_], [p_])
            ptr = PS_T.ap.bitcast(BF16)
            for h in range(4):
                nreg = psum[nbank + h // 2].ap[:, (h % 2) * 256:(h % 2) * 256 + 129]
                hm_ = hm[nC["hm"] % 3]
                nC["hm"] += 1
                stt(hm_.ap, nreg[:, 0:128], p_.ap[:, 16 + h:17 + h], gsv[:, j, h * 128:(h + 1) * 128], ALU.mult, ALU.mult, [nb_, p_, gs_], [hm_])
                tr(ptr[:, h * 128:(h + 1) * 128], hm_.ap, ident.ap, [hm_, ident], [PS_T])
            cp("act", hsv[:, :, j * 128:(j + 1) * 128], ptr[:, 0:512].rearrange("p (h t) -> p h t", t=128), [PS_T], [hs_])
        if own:
            dma("pool", MIXT[512:1024, go * 512:(go + 1) * 512].rearrange("(h e) t -> e h t", e=128), hsv, [hs_], [MIXTb], hs_.b.name)
    while pf_state["cvt"] < len(pf_chunks):
        prefetch_step()
    P.barrier()
    if upto == "C":
        return finish_early()

    AR.off = WD_OFF
    wd = AR.alloc(22 * DM, BF16, "wd")
    wdv = wd.ap.rearrange("p (k c) -> p k c", c=DM)
    D_MARK = AR.off
    wst = [AR.alloc(1408, F32, "wstD%d" % i) for i in range(3)]
    n = 0

    def wload(src, dst, scale_col):
        nonlocal n
        s = wst[n % 3]
        eng = ("dve", "act")[n % 2]
        ncol = dst.shape[1]
        dma("sp", s.ap[:, 0:ncol], src, [], [s], s.b.name)
        cp(eng, dst, s.ap[:, 0:ncol], [s], [wo])
        n += 1

    for k in range(22):
        wload(w_down[k * 128:(k + 1) * 128, :], wdv[:, k, :], None)
    P.barrier()
    AR.off = D_MARK
    GT = 256
    x1 = [AR.alloc(2 * DM, F32, "x1_%d" % i) for i in range(2)]
    mxs = [AR.alloc(8 * GT, BF16, "mxs%d" % i) for i in range(2)]
    h2 = AR.alloc(DM, BF16, "h2")
    h2T = AR.alloc(8 * GT, BF16, "h2T")
    aT = AR.alloc(22 * GT, BF16, "aT")
    sg = [AR.alloc(GT, BF16, "sg%d" % i) for i in range(2)]
    sD = [AR.alloc(8, F32, "sD%d" % i) for i in range(2)]
    sqd = AR.alloc(DM, BF16, "sqd")
    h2Tv = h2T.ap.rearrange("p (k t) -> p k t", t=GT)
    aTv = aT.ap.rearrange("p (k t) -> p k t", t=GT)
    PS_ACC = [psum[0], psum[1]]
    PS_GG = [psum[2], psum[3]]
    PS_UU = [psum[4], psum[5]]
    PS_TD = psum[6]
    nD = {"acc": 0, "g": 0}
    wD = [wo]
    for g in range(SO // GT):
        x1_ = x1[g % 2]
        mx_ = mxs[g % 2]
        x1v = x1_.ap.rearrange("p (j c) -> p j c", c=DM)
        mxv = mx_.ap.rearrange("p (k t) -> p k t", t=GT)
        dma("sp", mxv, MIXT[:, g * GT:(g + 1) * GT].rearrange("(k p) t -> p k t", p=128), [MIXTb], [mx_], mx_.b.name)
        dmas("sp", [(x1v[:, j, :], x_own[128 + g * GT + j * 128:128 + g * GT + (j + 1) * 128, :]) for j in range(2)], [], [x1_], x1_.b.name)
        for j in range(2):
            for c in range(2):
                acc = PS_ACC[nD["acc"] % 2]
                nD["acc"] += 1
                for k in range(8):
                    mm(acc.ap, mxv[:, k, j * 128:(j + 1) * 128], wov[:, k, c * 512:(c + 1) * 512], k == 0, k == 7, [mx_] + wD, [acc])
                tt("dve", x1v[:, j, c * 512:(c + 1) * 512], x1v[:, j, c * 512:(c + 1) * 512], acc.ap, ALU.add, [x1_, acc], [x1_])
            s_ = sD[j]
            act(sqd.ap, x1v[:, j, :], AF.Square, [x1_], [sqd, s_], accum=s_.ap[:, 0:1])
            ts("dve", s_.ap[:, 1:2], s_.ap[:, 0:1], 1.0 / DM, EPS, ALU.mult, ALU.add, [s_], [s_])
            act(s_.ap[:, 2:3], s_.ap[:, 1:2], AF.Ln, [s_], [s_])
            act(s_.ap[:, 2:3], s_.ap[:, 2:3], AF.Exp, [s_], [s_], scale=-0.5)
            ts("dve", h2.ap, x1v[:, j, :], s_.ap[:, 2:3], None, ALU.mult, None, [x1_, s_], [h2])
            ptd = PS_TD.ap.bitcast(BF16)
            for k in range(8):
                tr(ptd[:, k * 128:(k + 1) * 128], h2.ap[:, k * 128:(k + 1) * 128], ident.ap, [h2, ident], [PS_TD])
            cp("act", h2Tv[:, :, j * 128:(j + 1) * 128], ptd.rearrange("p (k t) -> p k t", t=128), [PS_TD], [h2T])
        for f in range(22):
            gg = PS_GG[f % 2]
            uu = PS_UU[f % 2]
            for k in range(8):
                mm(gg.ap[:, 0:GT], wgv[:, k, f * 128:(f + 1) * 128], h2Tv[:, k, :], k == 0, k == 7, [h2T] + wD, [gg])
            for k in range(8):
                mm(uu.ap[:, 0:GT], wuv[:, k, f * 128:(f + 1) * 128], h2Tv[:, k, :], k == 0, k == 7, [h2T] + wD, [uu])
            s_ = sg[f % 2]
            act(s_.ap, gg.ap[:, 0:GT], AF.Silu, [gg], [s_])
            tt("dve", aTv[:, f, :], s_.ap, uu.ap[:, 0:GT], ALU.mult, [s_, uu], [aT])
        for j in range(2):
            for c in range(2):
                acc = PS_ACC[nD["acc"] % 2]
                nD["acc"] += 1
                for f in range(22):
                    mm(acc.ap, aTv[:, f, j * 128:(j + 1) * 128], wdv[:, f, c * 512:(c + 1) * 512], f == 0, f == 21, [aT] + wD, [acc])
                tt("dve", x1v[:, j, c * 512:(c + 1) * 512], x1v[:, j, c * 512:(c + 1) * 512], acc.ap, ALU.add, [x1_, acc], [x1_])
            s_ = sD[j]
            act(sqd.ap, x1v[:, j, :], AF.Square, [x1_], [sqd, s_], accum=s_.ap[:, 4:5])
            ts("dve", s_.ap[:, 5:6], s_.ap[:, 4:5], 1.0 / DM, EPS, ALU.mult, ALU.add, [s_], [s_])
            act(s_.ap[:, 6:7], s_.ap[:, 5:6], AF.Ln, [s_], [s_])
            act(s_.ap[:, 6:7], s_.ap[:, 6:7], AF.Exp, [s_], [s_], scale=-0.5)
            stt(x1v[:, j, :], x1v[:, j, :], s_.ap[:, 6:7], fing.ap, ALU.mult, ALU.mult, [x1_, s_, fing], [x1_])
        dmas("pool", [(y_out[g * GT + j * 128:g * GT + (j + 1) * 128, :], x1v[:, j, :]) for j in range(2)], [x1_], [], x1_.b.name, final=True)

    P.emit(st)
    st.close()
    return nc


_INV_FREQ = (np.float32(500000.0) ** (-np.arange(0, 16, 2, dtype=np.float32) / np.float32(16.0))).astype(np.float32)


def _consts():
    bf = ml_dtypes.bfloat16
    c = {}
    c["ident"] = np.eye(128, dtype=np.float32).astype(bf)
    r = np.zeros((128, 128), np.float32)
    invf = np.zeros((128, 1), np.float32)
    sgn = np.zeros((128, 1), np.float32)
    for p in range(128):
        d = p % 64
        if d < 8:
            r[p + 8, p] = 1.0
            invf[p, 0] = _INV_FREQ[d]
            sgn[p, 0] = -1.0
        elif d < 16:
            r[p - 8, p] = 1.0
            invf[p, 0] = _INV_FREQ[d - 8]
            sgn[p, 0] = 1.0
    c["rmat"] = r.astype(bf)
    c["invf"] = invf
    c["sgn"] = sgn
    s = np.arange(128)
    c["causal"] = (s[:, None] <= s[None, :]).astype(np.float32).astype(bf)
    c["tri"] = (s[:, None] <= s[None, :]).astype(np.float32)
    m = np.zeros((128, 4, 1024), np.float32)
    q = np.arange(512)
    for j in range(4):
        kc = (j * 128 + s) // 64
        vis = (kc[:, None] <= (q[None, :] // 64)).astype(np.float32)
        m[:, j, 0:512] = vis
        m[:, j, 512:1024] = vis
    c["masks"] = m.astype(bf)
    return c


_NC_CACHE = {}


def _get_nc(NGP, NGO, dbg):
    key = (NGP, NGO, dbg)
    if key not in _NC_CACHE:
        _NC_CACHE[key] = build(NGP, NGO, dbg)
    return _NC_CACHE[key]


def make_in_maps(inputs, NGP=8, NGO=8, batches=4):
    f32 = np.float32
    x = np.asarray(inputs["x"], f32)
    positions = np.asarray(inputs["positions"], np.int32)
    SP, SO = NGP * 512, NGO * 512
    NT = (NGP + NGO) * 4
    consts = _consts()
    shared = dict(consts)
    shared["w_in"] = np.ascontiguousarray(np.asarray(inputs["w_in"], f32)[0])
    shared["w_out"] = np.ascontiguousarray(np.asarray(inputs["w_out"], f32)[0])
    shared["w_gate"] = np.ascontiguousarray(np.asarray(inputs["w_gate"], f32)[0])
    shared["w_up"] = np.ascontiguousarray(np.asarray(inputs["w_up"], f32)[0])
    shared["w_down"] = np.ascontiguousarray(np.asarray(inputs["w_down"], f32)[0])
    shared["gmix"] = np.ascontiguousarray(np.asarray(inputs["mix_norm_g"], f32)[0].reshape(8, 128).T)
    shared["gffn"] = np.ascontiguousarray(np.asarray(inputs["ffn_norm_g"], f32)[0].reshape(8, 128).T)
    shared["da_lambda"] = np.asarray(inputs["da_lambda"], f32)[0].reshape(1, 256)
    shared["da_subln_g"] = np.asarray(inputs["da_subln_g"], f32)[0].reshape(1, 128)
    cw = np.asarray(inputs["ml_conv_w"], f32)[0]
    shared["convw"] = np.ascontiguousarray(cw.reshape(4, 8, 128).transpose(2, 1, 0))
    shared["convb"] = np.ascontiguousarray(np.asarray(inputs["ml_conv_b"], f32)[0].reshape(8, 128).T)
    shared["ml_gate_b"] = np.asarray(inputs["ml_gate_b"], f32)[0].reshape(1, 8)
    shared["ml_norm_g"] = np.asarray(inputs["ml_norm_g"], f32)[0].reshape(1, 512)
    shared["final_norm_g"] = np.asarray(inputs["final_norm_g"], f32).reshape(1, DM)
    in_maps = []
    for c in range(2 * batches):
        b, g = c // 2, c % 2
        m = dict(shared)
        own0 = g * SO if g == 1 else 0
        if g == 1:
            own0 = SP
            halo = x[b, own0 - 128:own0]
        else:
            halo = np.zeros((128, DM), f32)
        m["x_own"] = np.ascontiguousarray(np.concatenate([halo, x[b, own0:own0 + SO]], axis=0))
        m["x_pre"] = np.ascontiguousarray(x[b, 0:SP])
        m["pos"] = np.ascontiguousarray(np.concatenate([positions[b, 0:SP], positions[b, own0:own0 + SO]])[None, :])
        kbias = np.zeros((128, NT), f32)
        if g == 0:
            kbias[:, 0:NGP * 4] = -30000.0
        m["keybias"] = kbias
        m["flag"] = np.full((128, 1), float(g), f32)
        in_maps.append(m)
    return in_maps


def kernel(**inputs):
    nc = _get_nc(8, 8, False)
    in_maps = make_in_maps(inputs, 8, 8, 4)
    res = run_bass_kernel_spmd(nc, in_maps, core_ids=list(range(8)))
    out = np.empty((4, 8192, DM), np.float32)
    for c in range(8):
        b, g = c // 2, c % 2
        out[b, g * 4096:(g + 1) * 4096] = res.results[c]["y"]
    return out
```

```python
import numpy as np
import ml_dtypes
from contextlib import ExitStack
import concourse.bass as bass
import concourse.mybir as mybir
from concourse.bass_utils import run_bass_kernel_spmd

F32 = mybir.dt.float32
BF16 = mybir.dt.bfloat16
I32 = mybir.dt.int32
AF = mybir.ActivationFunctionType
ALU = mybir.AluOpType

ENGS = ("pe", "act", "dve", "pool", "sp")
EPS = 1e-6
DM = 1024
DFF = 2816
INW = 3592
TWO_PI = 2.0 * np.pi


class Buf:
    __slots__ = ("name", "writers", "readers", "psum")

    def __init__(self, name, psum=False):
        self.name = name
        self.writers = {}
        self.readers = []
        self.psum = psum


class Op:
    __slots__ = ("eng", "fn", "deps", "dma", "semkey", "ndma", "signal", "count", "sem", "gidx")

    def __init__(self, eng, fn, dma, semkey, ndma):
        self.eng = eng
        self.fn = fn
        self.deps = set()
        self.dma = dma
        self.semkey = semkey
        self.ndma = ndma
        self.signal = False
        self.count = None
        self.sem = None


class Prog:
    def __init__(self, nc):
        self.nc = nc
        self.ops = []
        self.barrier_deps = set()
        self.last = {}
        self.pending_dma = []
        self.out_dma = []
        self.phase_map = {}

    def add(self, eng, fn, reads=(), writes=(), dma=False, semkey=None, ndma=1, out=False):
        if dma and semkey != "const":
            semkey = "%s%d" % (eng, self.phase_map.setdefault((eng, semkey), len(self.phase_map)))
        op = Op(eng, fn, dma, semkey, ndma)
        op.gidx = len(self.ops)
        deps = set(self.barrier_deps)
        for b in reads:
            deps.update(b.writers.values())
            if b.psum:
                deps.update(r for r in b.readers if r.eng != eng)
        for b in writes:
            deps.update(b.writers.values())
            deps.update(b.readers)
        for b in reads:
            b.readers.append(op)
        for b in writes:
            if b.readers:
                b.writers = {}
                b.readers = []
            b.writers[("dma", op.gidx) if dma else eng] = op
        deps.discard(op)
        op.deps = deps
        self.ops.append(op)
        self.last[eng] = op
        if dma:
            self.pending_dma.append(op)
            if out:
                self.out_dma.append(op)
        return op

    def barrier(self):
        self.barrier_deps = set(self.last.values()) | set(self.pending_dma)
        self.pending_dma = []
        self.phase_map = {}

    @staticmethod
    def needs_sem(op, d):
        if d.dma or op.dma:
            return True
        if d.eng != op.eng:
            return True
        return op.eng != "pe"

    def emit(self, stack):
        nc = self.nc
        for op in self.ops:
            for d in op.deps:
                if self.needs_sem(op, d):
                    d.signal = True
        for op in self.out_dma:
            op.signal = True
        eng_sem = {e: stack.enter_context(nc.semaphore("s_" + e)) for e in ENGS}
        dma_sem, dma_cnt = {}, {}
        eng_cnt = {e: 0 for e in ENGS}
        for op in self.ops:
            if op.dma:
                if op.semkey not in dma_sem:
                    dma_sem[op.semkey] = stack.enter_context(nc.semaphore("d_%s" % (op.semkey,)))
                    dma_cnt[op.semkey] = 0
                dma_cnt[op.semkey] += 16 * op.ndma
                op.sem = dma_sem[op.semkey]
                op.count = dma_cnt[op.semkey]
            elif op.signal:
                eng_cnt[op.eng] += 1
                op.sem = eng_sem[op.eng]
                op.count = eng_cnt[op.eng]
        for op in self.ops:
            if op.dma and op.semkey == "const":
                op.count = dma_cnt["const"]
        self.nsem = len(dma_sem) + len(ENGS)
        block = stack.enter_context(nc.Block())
        by_eng = {e: [o for o in self.ops if o.eng == e] for e in ENGS}
        final_waits = [(o.sem, o.count) for o in self.out_dma]

        def run(engobj, e):
            waited = {}
            for op in by_eng[e]:
                need = {}
                for d in op.deps:
                    if not self.needs_sem(op, d):
                        continue
                    k = id(d.sem)
                    if k not in need or need[k][1] < d.count:
                        need[k] = (d.sem, d.count)
                for k, (s, c) in need.items():
                    if waited.get(k, 0) < c:
                        engobj.wait_ge(s, c)
                        waited[k] = c
                if op.dma:
                    op.fn(engobj, op.sem)
                else:
                    ins = op.fn(engobj)
                    if op.signal:
                        ins.then_inc(op.sem, 1)
            if e == "sp":
                for s, c in final_waits:
                    if waited.get(id(s), 0) < c:
                        engobj.wait_ge(s, c)
                        waited[id(s)] = c

        @block.tensor
        def _(eng):
            run(eng, "pe")

        @block.scalar
        def _(eng):
            run(eng, "act")

        @block.vector
        def _(eng):
            run(eng, "dve")

        @block.gpsimd
        def _(eng):
            run(eng, "pool")

        @block.sync
        def _(eng):
            run(eng, "sp")


class T:
    __slots__ = ("ap", "b")

    def __init__(self, ap, name, psum=False):
        self.ap = ap
        self.b = Buf(name, psum)


class Arena:
    def __init__(self, t):
        self.t = t
        self.off = 0
        self.N = t.shape[1]
        self.n = 0

    def alloc(self, n_elems, dtype, name=None):
        nbytes = n_elems * (2 if dtype == BF16 else 4)
        n32 = (nbytes + 3) // 4
        n32 = (n32 + 7) // 8 * 8
        o = self.off
        self.off += n32
        assert self.off <= self.N, "arena overflow %d > %d (%s)" % (self.off, self.N, name)
        ap = self.t[:, o:o + n32]
        if dtype != F32:
            ap = ap.bitcast(dtype)
        ap = ap[:, 0:n_elems]
        self.n += 1
        return T(ap, name or ("t%d" % self.n))


def build(NGP=8, NGO=8, dbg=False, upto="D"):
    nc = bass.Bass("TRN2", target_bir_lowering=False)
    NTP, NTO = NGP * 4, NGO * 4
    NT = NTP + NTO
    SP, SO = NGP * 512, NGO * 512
    SA = SP + SO
    NKB = NT

    def din(name, shape, dt=F32):
        return nc.dram_tensor(name, list(shape), dt, kind="ExternalInput").ap()

    def dscr(name, shape, dt=BF16):
        kind = "ExternalOutput" if dbg else "Internal"
        return nc.dram_tensor(name, list(shape), dt, kind=kind).ap()

    x_own = din("x_own", [SO + 128, DM])
    x_pre = din("x_pre", [SP, DM])
    pos = din("pos", [1, SA], I32)
    keybias_d = din("keybias", [128, NKB])
    flag_d = din("flag", [128, 1])
    w_in = din("w_in", [DM, INW])
    w_out = din("w_out", [DM, DM])
    w_gate = din("w_gate", [DM, DFF])
    w_up = din("w_up", [DM, DFF])
    w_down = din("w_down", [DFF, DM])
    gmix_d = din("gmix", [128, 8])
    gffn_d = din("gffn", [128, 8])
    lam_d = din("da_lambda", [1, 256])
    subg_d = din("da_subln_g", [1, 128])
    convw_d = din("convw", [128, 8, 4])
    convb_d = din("convb", [128, 8])
    gateb_d = din("ml_gate_b", [1, 8])
    mlg_d = din("ml_norm_g", [1, 512])
    fing_d = din("final_norm_g", [1, DM])
    ident_d = din("ident", [128, 128], BF16)
    rmat_d = din("rmat", [128, 128], BF16)
    causal_d = din("causal", [128, 128], BF16)
    tri_d = din("tri", [128, 128])
    masks_d = din("masks", [128, 4, 1024], BF16)
    invf_d = din("invf", [128, 1])
    sgn_d = din("sgn", [128, 1])
    y_out = nc.dram_tensor("y", [SO, DM], F32, kind="ExternalOutput").ap()

    QT = dscr("QT", [4, 128, SO])
    KT = dscr("KT", [4, 128, SA])
    V1 = dscr("V1", [4, 128, NT, 130])
    MQT = dscr("MQT", [4, 128, SO])
    MKT = dscr("MKT", [4, 128, SO])
    MK = dscr("MK", [128, NT, 512])
    MV1 = dscr("MV1", [128, NT, 4, 130])
    GSO = dscr("GSO", [128, NTO, 512])
    MIXT = dscr("MIXT", [DM, SO])
    QTb, KTb, V1b, MQTb, MKTb, MKb, MV1b, GSOb, MIXTb = [Buf(n) for n in
                                                          "QT KT V1 MQT MKT MK MV1 GSO MIXT".split()]

    st = ExitStack()
    ARENA_N = 53000
    arena_t = st.enter_context(nc.sbuf_tensor("arena", [128, ARENA_N], F32))
    AR = Arena(arena_t)
    psbig = st.enter_context(nc.psum_tensor("psbig", [128, 4096], F32))
    psum = [T(psbig[:, i * 512:(i + 1) * 512], "ps%d" % i, True) for i in range(8)]
    P = Prog(nc)

    def bufs(ts):
        return [t.b if isinstance(t, T) else t for t in ts]

    def mm(out, lhsT, rhs, start, stop, r, w, skip=False):
        P.add("pe", lambda e: e.matmul(out, lhsT=lhsT, rhs=rhs, start=start, stop=stop, skip_group_check=skip),
              reads=bufs(r), writes=bufs(w))

    def tr(out, in_, ident, r, w):
        P.add("pe", lambda e: e.transpose(out=out, in_=in_, identity=ident), reads=bufs(r), writes=bufs(w))

    def act(out, in_, func, r, w, bias=None, scale=None, accum=None, eng="act"):
        kw = {}
        if bias is not None:
            kw["bias"] = bias
        if scale is not None:
            kw["scale"] = scale
        if accum is not None:
            kw["accum_out"] = accum
        P.add("act", lambda e: e.activation(out=out, in_=in_, func=func, **kw), reads=bufs(r), writes=bufs(w))

    def ts(eng, out, in0, s1, s2, op0, op1, r, w):
        if op1 is None:
            P.add(eng, lambda e: e.tensor_scalar(out=out, in0=in0, scalar1=s1, scalar2=None, op0=op0),
                  reads=bufs(r), writes=bufs(w))
        else:
            P.add(eng, lambda e: e.tensor_scalar(out=out, in0=in0, scalar1=s1, scalar2=s2, op0=op0, op1=op1),
                  reads=bufs(r), writes=bufs(w))

    def tt(eng, out, in0, in1, op, r, w):
        P.add(eng, lambda e: e.tensor_tensor(out=out, in0=in0, in1=in1, op=op), reads=bufs(r), writes=bufs(w))

    def stt(out, in0, scalar, in1, op0, op1, r, w, accum=None):
        if accum is None:
            P.add("dve", lambda e: e.scalar_tensor_tensor(out=out, in0=in0, scalar=scalar, in1=in1, op0=op0, op1=op1),
                  reads=bufs(r), writes=bufs(w))
        else:
            P.add("dve", lambda e: e.scalar_tensor_tensor(out=out, in0=in0, scalar=scalar, in1=in1, op0=op0,
                                                          op1=op1, accum_out=accum), reads=bufs(r), writes=bufs(w))

    def cp(eng, out, in_, r, w):
        if eng == "act":
            act(out, in_, AF.Copy, r, w)
        else:
            P.add(eng, lambda e: e.tensor_copy(out=out, in_=in_), reads=bufs(r), writes=bufs(w))

    def recip(out, in_, r, w):
        P.add("dve", lambda e: e.reciprocal(out=out, in_=in_), reads=bufs(r), writes=bufs(w))

    def memset(eng, ap, val, w):
        P.add(eng, lambda e: e.memset(ap, val), writes=bufs(w))

    STQ = "pool"

    def dma(q, out, in_, r, w, key, final=False):
        if q == "pool":
            q = STQ
        P.add(q, lambda e, s: e.dma_start(out=out, in_=in_).then_inc(s, 16), reads=bufs(r), writes=bufs(w),
              dma=True, semkey=key, out=final)

    def dmas(q, pairs, r, w, key, final=False):
        if q == "pool":
            q = STQ
        def fn(e, s):
            for o, i in pairs:
                e.dma_start(out=o, in_=i).then_inc(s, 16)
        P.add(q, fn, reads=bufs(r), writes=bufs(w), dma=True, semkey=key, ndma=len(pairs), out=final)

    def pconst(n, dt, name, src, q="sp"):
        t = AR.alloc(n, dt, name)
        dma(q, t.ap, src, [], [t], "const")
        return t

    ident = pconst(128, BF16, "ident", ident_d[:, :])
    rmat = pconst(128, BF16, "rmat", rmat_d[:, :])
    causal = pconst(128, BF16, "causal", causal_d[:, :])
    tri = pconst(128, F32, "tri", tri_d[:, :])
    invf = pconst(1, F32, "invf", invf_d[:, :])
    sgn = pconst(1, F32, "sgn", sgn_d[:, :])
    keybias = pconst(NKB, F32, "keybias", keybias_d[:, :])
    flag = pconst(1, F32, "flag", flag_d[:, :])
    gmix = pconst(8, F32, "gmix", gmix_d[:, :])
    gffn = pconst(8, F32, "gffn", gffn_d[:, :])
    convw = pconst(32, F32, "convw", convw_d.rearrange("p b j -> p (b j)"))
    convb = pconst(8, F32, "convb", convb_d[:, :])
    gateb = pconst(8, F32, "gateb", gateb_d[0:1, :].partition_broadcast(128))
    mlg = pconst(512, F32, "mlg", mlg_d[0:1, :].partition_broadcast(128))
    fing = pconst(DM, F32, "fing", fing_d[0:1, :].partition_broadcast(128))
    g08 = pconst(128, F32, "g08", subg_d[0:1, :].partition_broadcast(128))
    lamv = pconst(256, F32, "lamv", lam_d[0:1, :].partition_broadcast(128))
    P.barrier()
    ones128 = AR.alloc(128, F32, "ones128")
    memset("dve", ones128.ap, 1.0, [ones128])
    neghalf = AR.alloc(1, F32, "neghalf")
    memset("dve", neghalf.ap, -0.5, [neghalf])
    onecol = AR.alloc(1, F32, "onecol")
    memset("dve", onecol.ap, 1.0, [onecol])
    GP = AR.alloc(NT * 12, F32, "GP")
    GPv = GP.ap.rearrange("p (t c) -> p t c", c=12)
    small = AR.alloc(64, F32, "small")
    lam = T(small.ap[:, 0:1], "lam")
    neglam = T(small.ap[:, 1:2], "neglam")
    s01 = T(small.ap[:, 2:3], "s01")
    s23 = T(small.ap[:, 3:4], "s23")
    junk = AR.alloc(256, F32, "junkc")
    ts("dve", g08.ap, g08.ap, 0.8, None, ALU.mult, None, [g08], [g08])
    stt(junk.ap[:, 0:64], lamv.ap[:, 0:64], 1.0, lamv.ap[:, 64:128], ALU.mult, ALU.mult, [lamv], [junk, s01], accum=s01.ap)
    stt(junk.ap[:, 0:64], lamv.ap[:, 128:192], 1.0, lamv.ap[:, 192:256], ALU.mult, ALU.mult, [lamv], [junk, s23], accum=s23.ap)
    act(s01.ap, s01.ap, AF.Exp, [s01], [s01])
    act(s23.ap, s23.ap, AF.Exp, [s23], [s23])
    tt("dve", lam.ap, s01.ap, s23.ap, ALU.subtract, [s01, s23], [lam])
    ts("dve", lam.ap, lam.ap, 0.2, None, ALU.add, None, [lam], [lam])
    ts("dve", neglam.ap, lam.ap, -1.0, None, ALU.mult, None, [lam], [neglam])
    PERSIST = AR.off
    P.barrier()

    def finish_early():
        dma("sp", y_out[0:128, :], fing.ap, [fing], [], "const", final=True)
        P.emit(st)
        st.close()
        return nc

    if upto == "0":
        return finish_early()
    AR.off = PERSIST
    win = AR.alloc(8 * INW, BF16, "win")
    winv = win.ap.rearrange("p (k c) -> p k c", c=INW)
    TAB = 4096 if max(SP, SO) > 2048 else max(SP, SO)
    Ctab = AR.alloc(max(SP, SO), F32, "Ctab")
    Stab = AR.alloc(max(SP, SO), F32, "Stab")
    ttmp = [AR.alloc(512, F32, "ttmp%d" % i) for i in range(3)]
    tint = AR.alloc(512, I32, "tint")
    A_MARK = AR.off
    wst = [AR.alloc(1796, F32, "wst%d" % i) for i in range(2)]
    n = 0
    for k in range(8):
        for c0 in (0, 1796):
            s = wst[n % 2]
            dma("sp", s.ap, w_in[k * 128:(k + 1) * 128, c0:c0 + 1796], [], [s], s.b.name)
            ts("dve", winv[:, k, c0:c0 + 1796], s.ap, gmix.ap[:, k:k + 1], None, ALU.mult, None,
               [s, gmix], [win])
            n += 1


    def build_tables(t0, ntok):
        for c0 in range(0, ntok, 512):
            cn = min(512, ntok - c0)
            pi_, ang, u = ttmp[0], ttmp[1], ttmp[2]
            dma("sp", tint.ap[:, 0:cn], pos[0:1, t0 + c0:t0 + c0 + cn].partition_broadcast(128), [], [tint], "tint")
            cp("dve", pi_.ap[:, 0:cn], tint.ap[:, 0:cn], [tint], [pi_])
            ts("dve", ang.ap[:, 0:cn], pi_.ap[:, 0:cn], invf.ap, None, ALU.mult, None, [pi_, invf], [ang])
            for tab, shift in ((Stab, 0.0), (Ctab, 0.25)):
                ts("dve", u.ap[:, 0:cn], ang.ap[:, 0:cn], 1.0 / TWO_PI, shift, ALU.mult, ALU.add, [ang], [u])
                cp("dve", tint.ap[:, 0:cn], u.ap[:, 0:cn], [u], [tint])
                cp("dve", pi_.ap[:, 0:cn], tint.ap[:, 0:cn], [tint], [pi_])
                tt("dve", u.ap[:, 0:cn], u.ap[:, 0:cn], pi_.ap[:, 0:cn], ALU.subtract, [u, pi_], [u])
                ts("dve", u.ap[:, 0:cn], u.ap[:, 0:cn], TWO_PI, None, ALU.mult, None, [u], [u])
                ts("dve", u.ap[:, 0:cn], u.ap[:, 0:cn], 3.1415925, -3.1415925, ALU.min, ALU.max, [u], [u])
                act(tab.ap[:, c0:c0 + cn], u.ap[:, 0:cn], AF.Sin, [u], [tab])
            ts("dve", Stab.ap[:, c0:c0 + cn], Stab.ap[:, c0:c0 + cn], sgn.ap, None, ALU.mult, None, [Stab, sgn], [Stab])

    build_tables(0, SP)
    P.barrier()
    if upto == "A0":
        return finish_early()
    AR.off = A_MARK
    xs = [AR.alloc(DM, F32, "xs%d" % i) for i in range(4)]
    hb = [AR.alloc(DM, BF16, "hb%d" % i) for i in range(4)]
    hT = [AR.alloc(8 * 512, BF16, "hT%d" % i) for i in range(2)]
    zc = [AR.alloc(515, F32, "zc%d" % i) for i in range(8)]
    zb = [AR.alloc(512, BF16, "zb%d" % i) for i in range(2)]
    t1 = [AR.alloc(512, F32, "t1_%d" % i) for i in range(2)]
    t2 = [AR.alloc(512, F32, "t2_%d" % i) for i in range(2)]
    yc = [AR.alloc(512, F32, "yc%d" % i) for i in range(2)]
    ost = [AR.alloc(512, BF16, "ost%d" % i) for i in range(3)]
    kst = [AR.alloc(4 * 512, BF16, "kst%d" % i) for i in range(2)]
    vst = [AR.alloc(520, BF16, "vst%d" % i) for i in range(4)]
    ktk = [AR.alloc(512, BF16, "ktk%d" % i) for i in range(2)]
    ef = [AR.alloc(512, F32, "ef%d" % i) for i in range(2)]
    gst = [AR.alloc(512, BF16, "gst%d" % i) for i in range(2)]
    ssA = [AR.alloc(4, F32, "ssA%d" % i) for i in range(4)]
    gts = [AR.alloc(16, F32, "gts%d" % i) for i in range(4)]
    sqj = AR.alloc(DM, BF16, "sqj")
    for v_ in vst:
        memset("dve", v_.ap, 0.0, [v_])
        memset("dve", v_.ap.rearrange("p (h c) -> p h c", c=130)[:, :, 128:129], 1.0, [v_])
    for z_ in zc:
        memset("dve", z_.ap[:, 0:3], 0.0, [z_])
    PS_TOK = [psum[0], psum[1]]
    PS_FM = [psum[2], psum[3]]
    PS_RZ = psum[4]
    PS_TR = psum[5]
    PS_G = psum[6]
    PS_KT = psum[7]
    cnt = {"x": 0, "tok": 0, "fm": 0, "z": 0, "o": 0, "v": 0, "k": 0, "e": 0, "g": 0, "y": 0, "kst": 0}

    def nxt(key, lst):
        i = cnt[key]
        cnt[key] += 1
        return lst[i % len(lst)]

    def lnt_dma(xsrc_rows, slot):
        x_ = xs[slot]
        dma("sp", x_.ap, xsrc_rows, [], [x_], x_.b.name)

    def lnt_compute(slot):
        x_, ss_, h_ = xs[slot], ssA[slot], hb[slot]
        act(sqj.ap, x_.ap, AF.Square, [x_], [sqj, ss_], accum=ss_.ap[:, 0:1])
        ts("dve", ss_.ap[:, 1:2], ss_.ap[:, 0:1], 1.0 / DM, EPS, ALU.mult, ALU.add, [ss_], [ss_])
        act(ss_.ap[:, 2:3], ss_.ap[:, 1:2], AF.Ln, [ss_], [ss_])
        act(ss_.ap[:, 2:3], ss_.ap[:, 2:3], AF.Exp, [ss_], [ss_], scale=-0.5)
        ts("dve", h_.ap, x_.ap, ss_.ap[:, 2:3], None, ALU.mult, None, [x_, ss_], [h_])

    def lnt_transpose(slot, hT_t, col):
        h_ = hb[slot]
        pst = PS_TR.ap.bitcast(BF16)
        for k in range(8):
            tr(pst[:, k * 128:(k + 1) * 128], h_.ap[:, k * 128:(k + 1) * 128], ident.ap, [h_, ident], [PS_TR])
        hv = hT_t.ap.rearrange("p (k t) -> p k t", t=512)
        cp("act", hv[:, :, col:col + 128], pst.rearrange("p (k t) -> p k t", t=128), [PS_TR], [hT_t])

    def load_norm_transpose(xsrc_rows, hT_t, col):
        lnt_dma(xsrc_rows, 0)
        lnt_compute(0)
        lnt_transpose(0, hT_t, col)

    def tok_matmul(hT_t, col, c0, ncols, ps):
        hv = hT_t.ap.rearrange("p (k t) -> p k t", t=512)
        for k in range(8):
            mm(ps.ap[:, 0:ncols], hv[:, k, col:col + 128], winv[:, k, c0:c0 + ncols], k == 0, k == 7, [hT_t, win], [ps])

    def fm_matmul(hT_t, c0, ps, ntok=512):
        hv = hT_t.ap.rearrange("p (k t) -> p k t", t=512)
        for k in range(8):
            mm(ps.ap[:, 0:ntok], winv[:, k, c0:c0 + 128], hv[:, k, 0:ntok], k == 0, k == 7, [hT_t, win], [ps])

    def rope_block(ps, tab0, dst, dstb):
        z_ = nxt("z", zb)
        i = (cnt["z"] - 1) % 2
        cp("act", z_.ap, ps.ap, [ps], [z_])
        mm(PS_RZ.ap, rmat.ap, z_.ap, True, True, [rmat, z_], [PS_RZ])
        tt("dve", t1[i].ap, ps.ap, Ctab.ap[:, tab0:tab0 + 512], ALU.mult, [ps, Ctab], [t1[i]])
        tt("dve", t2[i].ap, PS_RZ.ap, Stab.ap[:, tab0:tab0 + 512], ALU.mult, [PS_RZ, Stab], [t2[i]])
        o_ = nxt("o", ost)
        tt("dve", o_.ap, t1[i].ap, t2[i].ap, ALU.add, [t1[i], t2[i]], [o_])
        dma("pool", dst, o_.ap, [o_], [dstb], o_.b.name)

    def conv_block(ps, blk, dst_ap):
        z_ = zc[blk]
        cp("act", z_.ap[:, 3:515], ps.ap, [ps], [z_])
        y_ = nxt("y", yc)
        cw = convw.ap.rearrange("p (b j) -> p b j", j=4)
        ts("dve", y_.ap, z_.ap[:, 3:515], cw[:, blk, 3:4], convb.ap[:, blk:blk + 1], ALU.mult, ALU.add, [z_, convw, convb], [y_])
        for j in (2, 1, 0):
            stt(y_.ap, z_.ap[:, j:j + 512], cw[:, blk, j:j + 1], y_.ap, ALU.mult, ALU.add, [z_, convw, y_], [y_])
        cp("dve", z_.ap[:, 0:3], z_.ap[:, 512:515], [z_], [z_])
        return y_

    def gates_part1(ps):
        g_ = nxt("g", gts)
        tt("dve", g_.ap[:, 0:8], ps.ap[:, 0:8], gateb.ap, ALU.add, [ps, gateb], [g_])
        act(g_.ap[:, 8:12], g_.ap[:, 4:8], AF.Exp, [g_], [g_], scale=-1.0)
        act(g_.ap[:, 8:12], g_.ap[:, 8:12], AF.Ln, [g_], [g_], bias=onecol.ap)
        return g_

    def gates_part2(g_, tile_idx):
        mm(PS_G.ap[:, 0:4], tri.ap, g_.ap[:, 8:12], True, True, [tri, g_], [PS_G])
        mm(PS_G.ap[:, 4:8], ones128.ap, g_.ap[:, 8:12], True, True, [ones128, g_], [PS_G])
        tt("dve", g_.ap[:, 12:16], g_.ap[:, 0:4], PS_G.ap[:, 0:4], ALU.add, [g_, PS_G], [g_])
        act(GPv[:, tile_idx, 0:4], g_.ap[:, 12:16], AF.Exp, [g_], [GP])
        act(GPv[:, tile_idx, 4:12], PS_G.ap[:, 0:8], AF.Exp, [PS_G], [GP], scale=-1.0)
        ts("dve", GPv[:, tile_idx, 4:8], GPv[:, tile_idx, 4:8], 128.0 ** -0.5, None, ALU.mult, None, [GP], [GP])


    def xrows(own, g, j):
        xsrc = x_own if own else x_pre
        row0 = (128 if own else 0) + g * 512
        return xsrc[row0 + j * 128:row0 + (j + 1) * 128, :]

    def lnt_group(own, g):
        hT_t = hT[(g + (NGP if own else 0)) % 2]
        for j in range(4):
            lnt_dma(xrows(own, g, j), j)
        for j in range(4):
            lnt_compute(j)
        for j in range(4):
            lnt_transpose(j, hT_t, j * 128)

    pend_k = []

    def phaseA_group(own, g, nxt_grp):
        hT_t = hT[(g + (NGP if own else 0)) % 2]
        if nxt_grp is not None:
            for j in range(4):
                lnt_dma(xrows(*nxt_grp, j), j)
        pend_g = []
        tile0 = (NTP + g * 4) if own else g * 4
        tab0 = g * 512
        for j in range(4):
            tile = tile0 + j
            col = j * 128
            ps = nxt("tok", PS_TOK)
            tok_matmul(hT_t, col, 1024, 512, ps)
            v_ = nxt("v", vst)
            cp("act", v_.ap.rearrange("p (h c) -> p h c", c=130)[:, :, 0:128], ps.ap.rearrange("p (h c) -> p h c", c=128), [ps], [v_])
            dma("pool", V1[:, :, tile, :].rearrange("h p c -> p h c"), v_.ap.rearrange("p (h c) -> p h c", c=130), [v_], [V1b], v_.b.name)
            ps = nxt("tok", PS_TOK)
            tok_matmul(hT_t, col, 2560, 512, ps)
            v_ = nxt("v", vst)
            cp("act", v_.ap.rearrange("p (h c) -> p h c", c=130)[:, :, 0:128], ps.ap.rearrange("p (h c) -> p h c", c=128), [ps], [v_])
            dma("pool", MV1[:, tile, :, :], v_.ap.rearrange("p (h c) -> p h c", c=130), [v_], [MV1b], v_.b.name)
            ps = nxt("tok", PS_TOK)
            tok_matmul(hT_t, col, 3584, 8, ps)
            g_ = gates_part1(ps)
            while pend_g:
                gates_part2(*pend_g.pop(0))
            pend_g.append((g_, tile))
            if nxt_grp is not None and j == 1:
                for jj in range(4):
                    lnt_compute(jj)
            if own:
                ps = nxt("tok", PS_TOK)
                tok_matmul(hT_t, col, 3072, 512, ps)
                e_ = nxt("e", ef)
                i = (cnt["e"] - 1) % 2
                act(e_.ap, ps.ap, AF.Exp, [ps], [e_], scale=-1.0)
                ts("dve", e_.ap, e_.ap, 1.0, None, ALU.add, None, [e_], [e_])
                recip(e_.ap, e_.ap, [e_], [e_])
                tt("dve", gst[i].ap, e_.ap, mlg.ap, ALU.mult, [e_, mlg], [gst[i]])
                dma("pool", GSO[:, tile - NTP, :], gst[i].ap, [gst[i]], [GSOb], gst[i].b.name)
        while pend_k:
            pend_k.pop(0)()
        if nxt_grp is not None:
            hT_n = hT[(nxt_grp[1] + (NGP if nxt_grp[0] else 0)) % 2]
            for jj in range(4):
                lnt_transpose(jj, hT_n, jj * 128)
        while pend_g:
            gates_part2(*pend_g.pop(0))
        ks_ = nxt("kst", kst)
        ksv = ks_.ap.rearrange("p (h t) -> p h t", t=512)
        kbase = SP if own else 0

        def mlq_post(ps, h):
            y_ = conv_block(ps, h, None)

            def fin():
                o_ = nxt("o", ost)
                act(o_.ap, y_.ap, AF.Silu, [y_], [o_])
                dma("pool", MQT[h, :, g * 512:(g + 1) * 512], o_.ap, [o_], [MQTb], o_.b.name)
            return fin

        def mlk_post(ps, h):
            y_ = conv_block(ps, 4 + h, None)

            def fin():
                act(ksv[:, h, :], y_.ap, AF.Silu, [y_], [ks_])
            return fin

        blocks = []
        for h in range(4):
            if own:
                blocks.append((h * 128, lambda ps, h=h: rope_block(ps, tab0, QT[h, :, g * 512:(g + 1) * 512], QTb)))
            blocks.append((512 + h * 128, lambda ps, h=h: rope_block(ps, tab0, KT[h, :, kbase + g * 512:kbase + (g + 1) * 512], KTb)))
        if own:
            for h in range(4):
                blocks.append((1536 + h * 128, lambda ps, h=h: mlq_post(ps, h)))
        for h in range(4):
            blocks.append((2048 + h * 128, lambda ps, h=h: mlk_post(ps, h)))
        pend_fin = []
        ps_cur = nxt("fm", PS_FM)
        fm_matmul(hT_t, blocks[0][0], ps_cur)
        for b in range(len(blocks)):
            ps_next = None
            if b + 1 < len(blocks):
                ps_next = nxt("fm", PS_FM)
                fm_matmul(hT_t, blocks[b + 1][0], ps_next)
            fin = blocks[b][1](ps_cur)
            if pend_fin:
                pend_fin.pop(0)()
            if fin is not None:
                pend_fin.append(fin)
            ps_cur = ps_next
        while pend_fin:
            pend_fin.pop(0)()
        if own:
            dma("pool", MKT[:, :, g * 512:(g + 1) * 512].rearrange("h p t -> p h t"), ksv, [ks_], [MKTb], ks_.b.name)
        def ktrans():
            pkt = PS_KT.ap.bitcast(BF16)
            for j in range(4):
                for h in range(4):
                    tr(pkt[:, h * 128:(h + 1) * 128], ksv[:, h, j * 128:(j + 1) * 128], ident.ap, [ks_, ident], [PS_KT])
                k_ = nxt("k", ktk)
                cp("dve", k_.ap, pkt[:, 0:512], [PS_KT], [k_])
                dma("pool", MK[:, tile0 + j, :], k_.ap, [k_], [MKb], k_.b.name)
        pend_k.append(ktrans)

    def halo_step():
        hT_t = hT[0]
        load_norm_transpose(x_own[0:128, :], hT_t, 0)
        for blk in range(8):
            ps = nxt("fm", PS_FM)
            fm_matmul(hT_t, (1536 if blk < 4 else 2048) + (blk % 4) * 128, ps, ntok=128)
            cp("act", zc[blk].ap[:, 0:3], ps.ap[:, 125:128], [ps], [zc[blk]])

    lnt_group(False, 0)
    for g in range(NGP):
        phaseA_group(False, g, (False, g + 1) if g + 1 < NGP else None)
    while pend_k:
        pend_k.pop(0)()
    build_tables(SP, SO)
    halo_step()
    lnt_group(True, 0)
    for g in range(NGO):
        phaseA_group(True, g, (True, g + 1) if g + 1 < NGO else None)
    while pend_k:
        pend_k.pop(0)()
    P.barrier()
    if upto == "A":
        return finish_early()

    AR.off = PERSIST
    masks = AR.alloc(4 * 1024, BF16, "masks")
    dma("sp", masks.ap, masks_d.rearrange("p j c -> p (j c)"), [], [masks], "masks")
    masksv = masks.ap.rearrange("p (j c) -> p j c", c=1024)
    kts = [AR.alloc(SA, BF16, "kts%d" % i) for i in range(2)]
    vs = [AR.alloc(NT * 130, BF16, "vs%d" % i) for i in range(2)]
    qts = [AR.alloc(SO, BF16, "qts%d" % i) for i in range(2)]
    pT = [AR.alloc(1024, BF16, "pT%d" % i) for i in range(3)]
    osb = [AR.alloc(8 * 129, F32, "osb%d" % i) for i in range(2)]
    ob = AR.alloc(128, F32, "ob")
    aob = [AR.alloc(128, BF16, "aob%d" % i) for i in range(2)]
    ast = [AR.alloc(512, BF16, "ast%d" % i) for i in range(2)]
    sc8 = [AR.alloc(8, F32, "sc8_%d" % i) for i in range(2)]
    ST = [T(None, "ST0", True), T(None, "ST1", True)]
    OB = [psum[4], psum[5], psum[6]]

    def oreg(r):
        return OB[r // 3].ap[:, (r % 3) * 129:(r % 3) * 129 + 129], OB[r // 3]

    nB = {"p": 0, "o": 0, "a": 0, "st": 0}
    zer = AR.alloc(128, BF16, "zer")
    memset("dve", zer.ap, 0.0, [zer])
    STap = [psbig[:, 0:1024], psbig[:, 1024:2048]]
    steps = [(h, G, kb) for h in range(4) for G in range(NGO) for kb in range(NTP + 4 * (G + 1))]
    hbuf = {}

    def head_bufs(h):
        if h not in hbuf:
            kt_, v_, qt_ = kts[h % 2], vs[h % 2], qts[h % 2]
            dma("sp", kt_.ap, KT[h, :, :], [KTb], [kt_], kt_.b.name)
            dma("sp", v_.ap, V1[h, :, :, :].rearrange("p t c -> p (t c)"), [V1b], [v_], v_.b.name)
            dma("sp", qt_.ap, QT[h, :, :], [QTb], [qt_], qt_.b.name)
            hbuf[h] = (kt_, v_, qt_)
        return hbuf[h]

    def emit_st(i):
        h, G, kb = steps[i]
        kt_, v_, qt_ = head_bufs(h)
        stb = ST[i % 2]
        sap = STap[i % 2]
        mm(sap[:, 0:512], kt_.ap[0:64, kb * 128:(kb + 1) * 128], qt_.ap[0:64, G * 512:(G + 1) * 512], True, True, [kt_, qt_], [stb])
        mm(sap[:, 512:1024], kt_.ap[64:128, kb * 128:(kb + 1) * 128], qt_.ap[64:128, G * 512:(G + 1) * 512], True, True, [kt_, qt_], [stb])

    def post_part2(h, G, o_, s8):
        ov = o_.ap.rearrange("p (r c) -> p r c", c=129)
        recip(s8.ap, ov[:, :, 128], [o_], [s8])
        ts("dve", s8.ap[:, 4:8], s8.ap[:, 4:8], neglam.ap, None, ALU.mult, None, [s8, neglam], [s8])
        a_ = ast[nB["a"] % 2]
        nB["a"] += 1
        pst = psum[7].ap.bitcast(BF16)
        for qb in range(4):
            ts("dve", ob.ap, ov[:, qb, 0:128], s8.ap[:, qb:qb + 1], None, ALU.mult, None, [o_, s8], [ob])
            stt(ob.ap, ov[:, 4 + qb, 0:128], s8.ap[:, 4 + qb:5 + qb], ob.ap, ALU.mult, ALU.add, [o_, s8, ob], [ob])
            stt(junk.ap[:, 0:128], ob.ap, 1.0, ob.ap, ALU.mult, ALU.mult, [ob], [junk, small], accum=small.ap[:, 8 + qb:9 + qb])
            ts("dve", small.ap[:, 12 + qb:13 + qb], small.ap[:, 8 + qb:9 + qb], 1.0 / 128, EPS, ALU.mult, ALU.add, [small], [small])
            act(small.ap[:, 16 + qb:17 + qb], small.ap[:, 12 + qb:13 + qb], AF.Ln, [small], [small])
            act(small.ap[:, 16 + qb:17 + qb], small.ap[:, 16 + qb:17 + qb], AF.Exp, [small], [small], scale=-0.5)
            ab = aob[qb % 2]
            stt(ab.ap, ob.ap, small.ap[:, 16 + qb:17 + qb], g08.ap, ALU.mult, ALU.mult, [ob, small, g08], [ab])
            tr(pst[:, qb * 128:(qb + 1) * 128], ab.ap, ident.ap, [ab, ident], [psum[7]])
        cp("dve", a_.ap, pst[:, 0:512], [psum[7]], [a_])
        dma("pool", MIXT[h * 128:(h + 1) * 128, G * 512:(G + 1) * 512], a_.ap, [a_], [MIXTb], a_.b.name)

    deferred = []
    emit_st(0)
    if len(steps) > 1:
        emit_st(1)
    for i, (h, G, kb) in enumerate(steps):
        kt_, v_, qt_ = head_bufs(h)
        vv = v_.ap.rearrange("p (t c) -> p t c", c=130)
        while deferred and deferred[0][0] <= i:
            deferred.pop(0)[1]()
        stb = ST[i % 2]
        p_ = pT[nB["p"] % 3]
        nB["p"] += 1
        j = kb - (NTP + 4 * G)
        if j > 0:
            c0 = 128 * j
            pv = p_.ap.rearrange("p (m c) -> p m c", c=512)[:, :, c0:512]
            sv = STap[i % 2].rearrange("p (m c) -> p m c", c=512)[:, :, c0:512]
            mv = masksv[:, j, :].rearrange("p (m c) -> p m c", c=512)[:, :, c0:512]
            act(pv, sv, AF.Exp, [stb, keybias], [p_], bias=keybias.ap[:, kb:kb + 1], scale=0.125)
            tt("dve", pv, pv, mv, ALU.mult, [p_, masks], [p_])
        else:
            act(p_.ap, STap[i % 2], AF.Exp, [stb, keybias], [p_], bias=keybias.ap[:, kb:kb + 1], scale=0.125)
            if j == 0:
                tt("dve", p_.ap, p_.ap, masksv[:, j, :], ALU.mult, [p_, masks], [p_])
        if i + 2 < len(steps):
            emit_st(i + 2)
        if kb == 0:
            for bnk in range(3):
                nreg = 3 if bnk < 2 else 2
                memset("dve", OB[bnk].ap[:, 0:nreg * 129], 0.0, [OB[bnk]])
        for m in range(2):
            for qb in range(4):
                last = NTP + 4 * G + qb
                if kb > last:
                    continue
                oap, obuf = oreg(m * 4 + qb)
                mm(oap, p_.ap[:, m * 512 + qb * 128:m * 512 + (qb + 1) * 128], vv[:, kb, 0:129], False, kb == last, [p_, v_], [obuf], skip=True)
        if kb == NTP + 4 * (G + 1) - 1:
            o_ = osb[nB["o"] % 2]
            s8 = sc8[nB["o"] % 2]
            nB["o"] += 1
            for bnk in range(3):
                nreg = 3 if bnk < 2 else 2
                cp("dve", o_.ap[:, bnk * 387:bnk * 387 + nreg * 129], OB[bnk].ap[:, 0:nreg * 129], [OB[bnk]], [o_])
            deferred.append((i + 6, (lambda h=h, G=G, o_=o_, s8=s8: post_part2(h, G, o_, s8))))
    while deferred:
        deferred.pop(0)[1]()
    P.barrier()
    if upto == "B":
        return finish_early()

    AR.off = PERSIST
    wo = AR.alloc(8 * DM, BF16, "wo")
    wg = AR.alloc(8 * DFF, BF16, "wg")
    wu = AR.alloc(8 * DFF, BF16, "wu")
    WD_OFF = AR.off
    wov = wo.ap.rearrange("p (k c) -> p k c", c=DM)
    wgv = wg.ap.rearrange("p (k c) -> p k c", c=DFF)
    wuv = wu.ap.rearrange("p (k c) -> p k c", c=DFF)
    pf_chunks = [(w_out[k * 128:(k + 1) * 128, :], wov[:, k, :], None) for k in range(8)]
    for k in range(8):
        for c0 in (0, 1408):
            pf_chunks.append((w_gate[k * 128:(k + 1) * 128, c0:c0 + 1408], wgv[:, k, c0:c0 + 1408], k))
            pf_chunks.append((w_up[k * 128:(k + 1) * 128, c0:c0 + 1408], wuv[:, k, c0:c0 + 1408], k))
    mks = [AR.alloc(4 * 512, BF16, "mks%d" % i) for i in range(2)]
    mvs = [AR.alloc(4 * 520, BF16, "mvs%d" % i) for i in range(2)]
    mqs = [AR.alloc(4 * 512, BF16, "mqs%d" % i) for i in range(2)]
    mkts = [AR.alloc(4 * 512, BF16, "mkts%d" % i) for i in range(2)]
    gss = [AR.alloc(4 * 512, BF16, "gss%d" % i) for i in range(2)]
    Cf = AR.alloc(4 * 129, F32, "Cf")
    Cb = AR.alloc(4 * 130, BF16, "Cb")
    kw = [AR.alloc(128, BF16, "kw%d" % i) for i in range(3)]
    scp = [AR.alloc(128, BF16, "scp%d" % i) for i in range(3)]
    hm = [AR.alloc(128, BF16, "hm%d" % i) for i in range(3)]
    hst = [AR.alloc(4 * 512, BF16, "hst%d" % i) for i in range(2)]
    pc = [AR.alloc(32, F32, "pc%d" % i) for i in range(2)]
    wstC = [AR.alloc(1408, F32, "wstC%d" % i) for i in range(4)]
    pf_state = {"dma": 0, "cvt": 0}

    def prefetch_step():
        n = pf_state["dma"]
        if n < len(pf_chunks):
            src, dst, k = pf_chunks[n]
            s_ = wstC[n % 4]
            dma("sp", s_.ap[:, 0:dst.shape[1]], src, [], [s_], s_.b.name)
            pf_state["dma"] += 1
        m = pf_state["cvt"]
        if m < len(pf_chunks) and (pf_state["dma"] - m >= 4 or pf_state["dma"] == len(pf_chunks)):
            src, dst, k = pf_chunks[m]
            s_ = wstC[m % 4]
            ncol = dst.shape[1]
            eng = "dve" if m % 2 == 0 else "act"
            if k is None:
                cp(eng, dst, s_.ap[:, 0:ncol], [s_], [wo])
            elif eng == "act":
                act(dst, s_.ap[:, 0:ncol], AF.Copy, [s_, gffn], [wo], scale=gffn.ap[:, k:k + 1])
            else:
                ts("dve", dst, s_.ap[:, 0:ncol], gffn.ap[:, k:k + 1], None, ALU.mult, None, [s_, gffn], [wo])
            pf_state["cvt"] += 1

    Cfv = Cf.ap.rearrange("p (h c) -> p h c", c=129)
    Cbv = Cb.ap.rearrange("p (h c) -> p h c", c=130)
    PS_SC = [psum[0], psum[1]]
    PS_N = [T(None, "N0", True), T(None, "N1", True)]
    PS_U = psum[6]
    PS_T = psum[7]
    nC = {"kw": 0, "scp": 0, "hm": 0, "sc": 0}

    def state_update(kk, vv4, tile, h, need_cb, mk_t, mv_t, PS_U=psum[6]):
        k_ = kw[nC["kw"] % 3]
        nC["kw"] += 1
        act(k_.ap, kk, AF.Copy, [GP, mk_t], [k_], scale=GPv[:, tile, h:h + 1])
        mm(PS_U.ap[:, 0:129], k_.ap, vv4[:, h, 0:129], True, True, [k_, mv_t], [PS_U])
        act(Cfv[:, h, :], Cfv[:, h, :], AF.Copy, [GP, Cfh[h]], [Cfh[h]], scale=GPv[:, tile, 8 + h:9 + h])
        stt(Cfv[:, h, :], PS_U.ap[:, 0:129], GPv[:, tile, 8 + h:9 + h], Cfv[:, h, :], ALU.mult, ALU.add, [PS_U, GP, Cfh[h]], [Cfh[h]])
        if need_cb:
            cp("act", Cbv[:, h, 0:129], Cfv[:, h, :], [Cfh[h]], [Cbh[h]])

    Cfh = [Buf("Cf%d" % h) for h in range(4)]
    Cbh = [Buf("Cb%d" % h) for h in range(4)]
    memset("dve", Cf.ap, 0.0, Cfh)
    memset("dve", Cb.ap, 0.0, Cbh)
    for grp in range(NGP + NGO):
        own = grp >= NGP
        go = grp - NGP
        mk_, mv_ = mks[grp % 2], mvs[grp % 2]
        dma("sp", mk_.ap.rearrange("p (t c) -> p t c", c=512), MK[:, grp * 4:(grp + 1) * 4, :], [MKb], [mk_], mk_.b.name)
        dma("sp", mv_.ap.rearrange("p (t c) -> p t c", c=520), MV1[:, grp * 4:(grp + 1) * 4, :, :].rearrange("p t h c -> p t (h c)"),
            [MV1b], [mv_], mv_.b.name)
        if own:
            mq_, mkt_, gs_, hs_ = mqs[go % 2], mkts[go % 2], gss[go % 2], hst[go % 2]
            dma("sp", mq_.ap.rearrange("p (h t) -> p h t", t=512), MQT[:, :, go * 512:(go + 1) * 512].rearrange("h p t -> p h t"), [MQTb], [mq_], mq_.b.name)
            dma("sp", mkt_.ap.rearrange("p (h t) -> p h t", t=512), MKT[:, :, go * 512:(go + 1) * 512].rearrange("h p t -> p h t"), [MKTb], [mkt_], mkt_.b.name)
            dma("sp", gs_.ap.rearrange("p (t c) -> p t c", c=512), GSO[:, go * 4:(go + 1) * 4, :], [GSOb], [gs_], gs_.b.name)
            mqv = mq_.ap.rearrange("p (h t) -> p h t", t=512)
            mktv = mkt_.ap.rearrange("p (h t) -> p h t", t=512)
            gsv = gs_.ap.rearrange("p (t c) -> p t c", c=512)
            hsv = hs_.ap.rearrange("p (h t) -> p h t", t=512)
        mkv = mk_.ap.rearrange("p (t c) -> p t c", c=512)
        mvv = mv_.ap.rearrange("p (t h c) -> p t h c", h=4, c=130)
        if own and go == 0:
            P.barrier()
            for h in range(4):
                ts("dve", Cfv[:, h, :], Cfv[:, h, :], flag.ap, None, ALU.mult, None, [Cfh[h], flag], [Cfh[h]])
                cp("act", Cbv[:, h, 0:129], Cfv[:, h, :], [Cfh[h]], [Cbh[h]])
        for j in range(4):
            tile = grp * 4 + j
            if not own:
                prefetch_step()
                prefetch_step()
                for h in range(4):
                    state_update(mkv[:, j, h * 128:(h + 1) * 128], mvv[:, j], tile, h, False, mk_, mv_, psum[(tile * 4 + h) % 7])
                continue
            nb_ = PS_N[tile % 2]
            nbank = 2 + 2 * (tile % 2)
            for h in range(4):
                sc_ = PS_SC[nC["sc"] % 2]
                nC["sc"] += 1
                mm(sc_.ap[:, 0:128], mktv[:, h, j * 128:(j + 1) * 128], mqv[:, h, j * 128:(j + 1) * 128], True, True, [mkt_, mq_], [sc_])
                s_ = scp[nC["scp"] % 3]
                nC["scp"] += 1
                stt(s_.ap, sc_.ap[:, 0:128], GPv[:, tile, h:h + 1], causal.ap, ALU.mult, ALU.mult, [sc_, GP, causal], [s_])
                nreg = psum[nbank + h // 2].ap[:, (h % 2) * 256:(h % 2) * 256 + 129]
                mm(nreg, s_.ap, mvv[:, j, h, 0:129], True, False, [s_, mv_], [nb_])
                mm(nreg, mqv[:, h, j * 128:(j + 1) * 128], Cbv[:, h, 0:129], False, True, [mq_, Cbh[h]], [nb_])
                state_update(mkv[:, j, h * 128:(h + 1) * 128], mvv[:, j], tile, h, True, mk_, mv_)
            p_ = pc[tile % 2]
            for h in range(4):
                nreg = psum[nbank + h // 2].ap[:, (h % 2) * 256:(h % 2) * 256 + 129]
                tt("dve", p_.ap[:, h:h + 1], nreg[:, 128:129], GPv[:, tile, 4 + h:5 + h], ALU.mult, [nb_, GP], [p_])
                act(junk.ap[:, 0:128], nreg[:, 0:128], AF.Square, [nb_], [junk, p_], accum=p_.ap[:, 8 + h:9 + h])
            stt(p_.ap[:, 4:8], p_.ap[:, 0:4], -1.0, p_.ap[:, 0:4], ALU.mult, ALU.max, [p_], [p_])
            ts("dve", p_.ap[:, 4:8], p_.ap[:, 4:8], 1.0, None, ALU.max, None, [p_], [p_])
            recip(p_.ap[:, 4:8], p_.ap[:, 4:8], [p_], [p_])
            tt("dve", p_.ap[:, 4:8], p_.ap[:, 4:8], GPv[:, tile, 4:8], ALU.mult, [p_, GP], [p_])
            tt("dve", p_.ap[:, 8:12], p_.ap[:, 8:12], p_.ap[:, 4:8], ALU.mult, [p_], [p_])
            tt("dve", p_.ap[:, 8:12], p_.ap[:, 8:12], p_.ap[:, 4:8], ALU.mult, [p_], [p_])
            ts("dve", p_.ap[:, 8:12], p_.ap[:, 8:12], 1.0 / 128, EPS, ALU.mult, ALU.add, [p_], [p_])
            act(p_.ap[:, 12:16], p_.ap[:, 8:12], AF.Ln, [p_], [p_])
            act(p_.ap[:, 12:16], p_.ap[:, 12:16], AF.Exp, [p_], [p_], scale=-0.5)
            tt("dve", p_.ap[:, 16:20], p_.ap[:, 12:16], p_.ap[:, 4:8], ALU.mult, [p_], [p_])
            ptr = PS_T.ap.bitcast(BF16)
            for h in range(4):
                nreg = psum[nbank + h // 2].ap[:, (h % 2) * 256:(h % 2) * 256 + 129]
                hm_ = hm[nC["hm"] % 3]
                nC["hm"] += 1
                stt(hm_.ap, nreg[:, 0:128], p_.ap[:, 16 + h:17 + h], gsv[:, j, h * 128:(h + 1) * 128], ALU.mult, ALU.mult, [nb_, p_, gs_], [hm_])
                tr(ptr[:, h * 128:(h + 1) * 128], hm_.ap, ident.ap, [hm_, ident], [PS_T])
            cp("act", hsv[:, :, j * 128:(j + 1) * 128], ptr[:, 0:512].rearrange("p (h t) -> p h t", t=128), [PS_T], [hs_])
        if own:
            dma("pool", MIXT[512:1024, go * 512:(go + 1) * 512].rearrange("(h e) t -> e h t", e=128), hsv, [hs_], [MIXTb], hs_.b.name)
    while pf_state["cvt"] < len(pf_chunks):
        prefetch_step()
    P.barrier()
    if upto == "C":
        return finish_early()

    AR.off = WD_OFF
    wd = AR.alloc(22 * DM, BF16, "wd")
    wdv = wd.ap.rearrange("p (k c) -> p k c", c=DM)
    D_MARK = AR.off
    wst = [AR.alloc(1408, F32, "wstD%d" % i) for i in range(3)]
    n = 0

    def wload(src, dst, scale_col):
        nonlocal n
        s = wst[n % 3]
        eng = ("dve", "act")[n % 2]
        ncol = dst.shape[1]
        dma("sp", s.ap[:, 0:ncol], src, [], [s], s.b.name)
        cp(eng, dst, s.ap[:, 0:ncol], [s], [wo])
        n += 1

    for k in range(22):
        wload(w_down[k * 128:(k + 1) * 128, :], wdv[:, k, :], None)
    P.barrier()
    AR.off = D_MARK
    GT = 256
    x1 = [AR.alloc(2 * DM, F32, "x1_%d" % i) for i in range(2)]
    mxs = [AR.alloc(8 * GT, BF16, "mxs%d" % i) for i in range(2)]
    h2 = AR.alloc(DM, BF16, "h2")
    h2T = AR.alloc(8 * GT, BF16, "h2T")
    aT = AR.alloc(22 * GT, BF16, "aT")
    sg = [AR.alloc(GT, BF16, "sg%d" % i) for i in range(2)]
    sD = [AR.alloc(8, F32, "sD%d" % i) for i in range(2)]
    sqd = AR.alloc(DM, BF16, "sqd")
    h2Tv = h2T.ap.rearrange("p (k t) -> p k t", t=GT)
    aTv = aT.ap.rearrange("p (k t) -> p k t", t=GT)
    PS_ACC = [psum[0], psum[1]]
    PS_GG = [psum[2], psum[3]]
    PS_UU = [psum[4], psum[5]]
    PS_TD = psum[6]
    nD = {"acc": 0, "g": 0}
    wD = [wo]
    for g in range(SO // GT):
        x1_ = x1[g % 2]
        mx_ = mxs[g % 2]
        x1v = x1_.ap.rearrange("p (j c) -> p j c", c=DM)
        mxv = mx_.ap.rearrange("p (k t) -> p k t", t=GT)
        dma("sp", mxv, MIXT[:, g * GT:(g + 1) * GT].rearrange("(k p) t -> p k t", p=128), [MIXTb], [mx_], mx_.b.name)
        dmas("sp", [(x1v[:, j, :], x_own[128 + g * GT + j * 128:128 + g * GT + (j + 1) * 128, :]) for j in range(2)], [], [x1_], x1_.b.name)
        for j in range(2):
            for c in range(2):
                acc = PS_ACC[nD["acc"] % 2]
                nD["acc"] += 1
                for k in range(8):
                    mm(acc.ap, mxv[:, k, j * 128:(j + 1) * 128], wov[:, k, c * 512:(c + 1) * 512], k == 0, k == 7, [mx_] + wD, [acc])
                tt("dve", x1v[:, j, c * 512:(c + 1) * 512], x1v[:, j, c * 512:(c + 1) * 512], acc.ap, ALU.add, [x1_, acc], [x1_])
            s_ = sD[j]
            act(sqd.ap, x1v[:, j, :], AF.Square, [x1_], [sqd, s_], accum=s_.ap[:, 0:1])
            ts("dve", s_.ap[:, 1:2], s_.ap[:, 0:1], 1.0 / DM, EPS, ALU.mult, ALU.add, [s_], [s_])
            act(s_.ap[:, 2:3], s_.ap[:, 1:2], AF.Ln, [s_], [s_])
            act(s_.ap[:, 2:3], s_.ap[:, 2:3], AF.Exp, [s_], [s_], scale=-0.5)
            ts("dve", h2.ap, x1v[:, j, :], s_.ap[:, 2:3], None, ALU.mult, None, [x1_, s_], [h2])
            ptd = PS_TD.ap.bitcast(BF16)
            for k in range(8):
                tr(ptd[:, k * 128:(k + 1) * 128], h2.ap[:, k * 128:(k + 1) * 128], ident.ap, [h2, ident], [PS_TD])
            cp("act", h2Tv[:, :, j * 128:(j + 1) * 128], ptd.rearrange("p (k t) -> p k t", t=128), [PS_TD], [h2T])
        for f in range(22):
            gg = PS_GG[f % 2]
            uu = PS_UU[f % 2]
            for k in range(8):
                mm(gg.ap[:, 0:GT], wgv[:, k, f * 128:(f + 1) * 128], h2Tv[:, k, :], k == 0, k == 7, [h2T] + wD, [gg])
            for k in range(8):
                mm(uu.ap[:, 0:GT], wuv[:, k, f * 128:(f + 1) * 128], h2Tv[:, k, :], k == 0, k == 7, [h2T] + wD, [uu])
            s_ = sg[f % 2]
            act(s_.ap, gg.ap[:, 0:GT], AF.Silu, [gg], [s_])
            tt("dve", aTv[:, f, :], s_.ap, uu.ap[:, 0:GT], ALU.mult, [s_, uu], [aT])
        for j in range(2):
            for c in range(2):
                acc = PS_ACC[nD["acc"] % 2]
                nD["acc"] += 1
                for f in range(22):
                    mm(acc.ap, aTv[:, f, j * 128:(j + 1) * 128], wdv[:, f, c * 512:(c + 1) * 512], f == 0, f == 21, [aT] + wD, [acc])
                tt("dve", x1v[:, j, c * 512:(c + 1) * 512], x1v[:, j, c * 512:(c + 1) * 512], acc.ap, ALU.add, [x1_, acc], [x1_])
            s_ = sD[j]
            act(sqd.ap, x1v[:, j, :], AF.Square, [x1_], [sqd, s_], accum=s_.ap[:, 4:5])
            ts("dve", s_.ap[:, 5:6], s_.ap[:, 4:5], 1.0 / DM, EPS, ALU.mult, ALU.add, [s_], [s_])
            act(s_.ap[:, 6:7], s_.ap[:, 5:6], AF.Ln, [s_], [s_])
            act(s_.ap[:, 6:7], s_.ap[:, 6:7], AF.Exp, [s_], [s_], scale=-0.5)
            stt(x1v[:, j, :], x1v[:, j, :], s_.ap[:, 6:7], fing.ap, ALU.mult, ALU.mult, [x1_, s_, fing], [x1_])
        dmas("pool", [(y_out[g * GT + j * 128:g * GT + (j + 1) * 128, :], x1v[:, j, :]) for j in range(2)], [x1_], [], x1_.b.name, final=True)

    P.emit(st)
    st.close()
    return nc


_INV_FREQ = (np.float32(500000.0) ** (-np.arange(0, 16, 2, dtype=np.float32) / np.float32(16.0))).astype(np.float32)


def _consts():
    bf = ml_dtypes.bfloat16
    c = {}
    c["ident"] = np.eye(128, dtype=np.float32).astype(bf)
    r = np.zeros((128, 128), np.float32)
    invf = np.zeros((128, 1), np.float32)
    sgn = np.zeros((128, 1), np.float32)
    for p in range(128):
        d = p % 64
        if d < 8:
            r[p + 8, p] = 1.0
            invf[p, 0] = _INV_FREQ[d]
            sgn[p, 0] = -1.0
        elif d < 16:
            r[p - 8, p] = 1.0
            invf[p, 0] = _INV_FREQ[d - 8]
            sgn[p, 0] = 1.0
    c["rmat"] = r.astype(bf)
    c["invf"] = invf
    c["sgn"] = sgn
    s = np.arange(128)
    c["causal"] = (s[:, None] <= s[None, :]).astype(np.float32).astype(bf)
    c["tri"] = (s[:, None] <= s[None, :]).astype(np.float32)
    m = np.zeros((128, 4, 1024), np.float32)
    q = np.arange(512)
    for j in range(4):
        kc = (j * 128 + s) // 64
        vis = (kc[:, None] <= (q[None, :] // 64)).astype(np.float32)
        m[:, j, 0:512] = vis
        m[:, j, 512:1024] = vis
    c["masks"] = m.astype(bf)
    return c


_NC_CACHE = {}


def _get_nc(NGP, NGO, dbg):
    key = (NGP, NGO, dbg)
    if key not in _NC_CACHE:
        _NC_CACHE[key] = build(NGP, NGO, dbg)
    return _NC_CACHE[key]


def make_in_maps(inputs, NGP=8, NGO=8, batches=4):
    f32 = np.float32
    x = np.asarray(inputs["x"], f32)
    positions = np.asarray(inputs["positions"], np.int32)
    SP, SO = NGP * 512, NGO * 512
    NT = (NGP + NGO) * 4
    consts = _consts()
    shared = dict(consts)
    shared["w_in"] = np.ascontiguousarray(np.asarray(inputs["w_in"], f32)[0])
    shared["w_out"] = np.ascontiguousarray(np.asarray(inputs["w_out"], f32)[0])
    shared["w_gate"] = np.ascontiguousarray(np.asarray(inputs["w_gate"], f32)[0])
    shared["w_up"] = np.ascontiguousarray(np.asarray(inputs["w_up"], f32)[0])
    shared["w_down"] = np.ascontiguousarray(np.asarray(inputs["w_down"], f32)[0])
    shared["gmix"] = np.ascontiguousarray(np.asarray(inputs["mix_norm_g"], f32)[0].reshape(8, 128).T)
    shared["gffn"] = np.ascontiguousarray(np.asarray(inputs["ffn_norm_g"], f32)[0].reshape(8, 128).T)
    shared["da_lambda"] = np.asarray(inputs["da_lambda"], f32)[0].reshape(1, 256)
    shared["da_subln_g"] = np.asarray(inputs["da_subln_g"], f32)[0].reshape(1, 128)
    cw = np.asarray(inputs["ml_conv_w"], f32)[0]
    shared["convw"] = np.ascontiguousarray(cw.reshape(4, 8, 128).transpose(2, 1, 0))
    shared["convb"] = np.ascontiguousarray(np.asarray(inputs["ml_conv_b"], f32)[0].reshape(8, 128).T)
    shared["ml_gate_b"] = np.asarray(inputs["ml_gate_b"], f32)[0].reshape(1, 8)
    shared["ml_norm_g"] = np.asarray(inputs["ml_norm_g"], f32)[0].reshape(1, 512)
    shared["final_norm_g"] = np.asarray(inputs["final_norm_g"], f32).reshape(1, DM)
    in_maps = []
    for c in range(2 * batches):
        b, g = c // 2, c % 2
        m = dict(shared)
        own0 = g * SO if g == 1 else 0
        if g == 1:
            own0 = SP
            halo = x[b, own0 - 128:own0]
        else:
            halo = np.zeros((128, DM), f32)
        m["x_own"] = np.ascontiguousarray(np.concatenate([halo, x[b, own0:own0 + SO]], axis=0))
        m["x_pre"] = np.ascontiguousarray(x[b, 0:SP])
        m["pos"] = np.ascontiguousarray(np.concatenate([positions[b, 0:SP], positions[b, own0:own0 + SO]])[None, :])
        kbias = np.zeros((128, NT), f32)
        if g == 0:
            kbias[:, 0:NGP * 4] = -30000.0
        m["keybias"] = kbias
        m["flag"] = np.full((128, 1), float(g), f32)
        in_maps.append(m)
    return in_maps


def kernel(**inputs):
    nc = _get_nc(8, 8, False)
    in_maps = make_in_maps(inputs, 8, 8, 4)
    res = run_bass_kernel_spmd(nc, in_maps, core_ids=list(range(8)))
    out = np.empty((4, 8192, DM), np.float32)
    for c in range(8):
        b, g = c // 2, c % 2
        out[b, g * 4096:(g + 1) * 4096] = res.results[c]["y"]
    return out
```

```python
import numpy as np
import ml_dtypes
from contextlib import ExitStack
import concourse.bass as bass
import concourse.mybir as mybir
from concourse.bass_utils import run_bass_kernel_spmd

F32 = mybir.dt.float32
BF16 = mybir.dt.bfloat16
I32 = mybir.dt.int32
AF = mybir.ActivationFunctionType
ALU = mybir.AluOpType

ENGS = ("pe", "act", "dve", "pool", "sp")
EPS = 1e-6
DM = 1024
DFF = 2816
INW = 3592
TWO_PI = 2.0 * np.pi


class Buf:
    __slots__ = ("name", "writers", "readers", "psum")

    def __init__(self, name, psum=False):
        self.name = name
        self.writers = {}
        self.readers = []
        self.psum = psum


class Op:
    __slots__ = ("eng", "fn", "deps", "dma", "semkey", "ndma", "signal", "count", "sem", "gidx")

    def __init__(self, eng, fn, dma, semkey, ndma):
        self.eng = eng
        self.fn = fn
        self.deps = set()
        self.dma = dma
        self.semkey = semkey
        self.ndma = ndma
        self.signal = False
        self.count = None
        self.sem = None


class Prog:
    def __init__(self, nc):
        self.nc = nc
        self.ops = []
        self.barrier_deps = set()
        self.last = {}
        self.pending_dma = []
        self.out_dma = []
        self.phase_map = {}

    def add(self, eng, fn, reads=(), writes=(), dma=False, semkey=None, ndma=1, out=False):
        if dma and semkey != "const":
            semkey = "%s%d" % (eng, self.phase_map.setdefault((eng, semkey), len(self.phase_map)))
        op = Op(eng, fn, dma, semkey, ndma)
        op.gidx = len(self.ops)
        deps = set(self.barrier_deps)
        for b in reads:
            deps.update(b.writers.values())
            if b.psum:
                deps.update(r for r in b.readers if r.eng != eng)
        for b in writes:
            deps.update(b.writers.values())
            deps.update(b.readers)
        for b in reads:
            b.readers.append(op)
        for b in writes:
            if b.readers:
                b.writers = {}
                b.readers = []
            b.writers[("dma", op.gidx) if dma else eng] = op
        deps.discard(op)
        op.deps = deps
        self.ops.append(op)
        self.last[eng] = op
        if dma:
            self.pending_dma.append(op)
            if out:
                self.out_dma.append(op)
        return op

    def barrier(self):
        self.barrier_deps = set(self.last.values()) | set(self.pending_dma)
        self.pending_dma = []
        self.phase_map = {}

    @staticmethod
    def needs_sem(op, d):
        if d.dma or op.dma:
            return True
        if d.eng != op.eng:
            return True
        return op.eng != "pe"

    def emit(self, stack):
        nc = self.nc
        for op in self.ops:
            for d in op.deps:
                if self.needs_sem(op, d):
                    d.signal = True
        for op in self.out_dma:
            op.signal = True
        eng_sem = {e: stack.enter_context(nc.semaphore("s_" + e)) for e in ENGS}
        dma_sem, dma_cnt = {}, {}
        eng_cnt = {e: 0 for e in ENGS}
        for op in self.ops:
            if op.dma:
                if op.semkey not in dma_sem:
                    dma_sem[op.semkey] = stack.enter_context(nc.semaphore("d_%s" % (op.semkey,)))
                    dma_cnt[op.semkey] = 0
                dma_cnt[op.semkey] += 16 * op.ndma
                op.sem = dma_sem[op.semkey]
                op.count = dma_cnt[op.semkey]
            elif op.signal:
                eng_cnt[op.eng] += 1
                op.sem = eng_sem[op.eng]
                op.count = eng_cnt[op.eng]
        for op in self.ops:
            if op.dma and op.semkey == "const":
                op.count = dma_cnt["const"]
        self.nsem = len(dma_sem) + len(ENGS)
        block = stack.enter_context(nc.Block())
        by_eng = {e: [o for o in self.ops if o.eng == e] for e in ENGS}
        final_waits = [(o.sem, o.count) for o in self.out_dma]

        def run(engobj, e):
            waited = {}
            for op in by_eng[e]:
                need = {}
                for d in op.deps:
                    if not self.needs_sem(op, d):
                        continue
                    k = id(d.sem)
                    if k not in need or need[k][1] < d.count:
                        need[k] = (d.sem, d.count)
                for k, (s, c) in need.items():
                    if waited.get(k, 0) < c:
                        engobj.wait_ge(s, c)
                        waited[k] = c
                if op.dma:
                    op.fn(engobj, op.sem)
                else:
                    ins = op.fn(engobj)
                    if op.signal:
                        ins.then_inc(op.sem, 1)
            if e == "sp":
                for s, c in final_waits:
                    if waited.get(id(s), 0) < c:
                        engobj.wait_ge(s, c)
                        waited[id(s)] = c

        @block.tensor
        def _(eng):
            run(eng, "pe")

        @block.scalar
        def _(eng):
            run(eng, "act")

        @block.vector
        def _(eng):
            run(eng, "dve")

        @block.gpsimd
        def _(eng):
            run(eng, "pool")

        @block.sync
        def _(eng):
            run(eng, "sp")


class T:
    __slots__ = ("ap", "b")

    def __init__(self, ap, name, psum=False):
        self.ap = ap
        self.b = Buf(name, psum)


class Arena:
    def __init__(self, t):
        self.t = t
        self.off = 0
        self.N = t.shape[1]
        self.n = 0

    def alloc(self, n_elems, dtype, name=None):
        nbytes = n_elems * (2 if dtype == BF16 else 4)
        n32 = (nbytes + 3) // 4
        n32 = (n32 + 7) // 8 * 8
        o = self.off
        self.off += n32
        assert self.off <= self.N, "arena overflow %d > %d (%s)" % (self.off, self.N, name)
        ap = self.t[:, o:o + n32]
        if dtype != F32:
            ap = ap.bitcast(dtype)
        ap = ap[:, 0:n_elems]
        self.n += 1
        return T(ap, name or ("t%d" % self.n))


def build(NGP=8, NGO=8, dbg=False, upto="D"):
    nc = bass.Bass("TRN2", target_bir_lowering=False)
    NTP, NTO = NGP * 4, NGO * 4
    NT = NTP + NTO
    SP, SO = NGP * 512, NGO * 512
    SA = SP + SO
    NKB = NT

    def din(name, shape, dt=F32):
        return nc.dram_tensor(name, list(shape), dt, kind="ExternalInput").ap()

    def dscr(name, shape, dt=BF16):
        kind = "ExternalOutput" if dbg else "Internal"
        return nc.dram_tensor(name, list(shape), dt, kind=kind).ap()

    x_own = din("x_own", [SO + 128, DM])
    x_pre = din("x_pre", [SP, DM])
    pos = din("pos", [1, SA], I32)
    keybias_d = din("keybias", [128, NKB])
    flag_d = din("flag", [128, 1])
    w_in = din("w_in", [DM, INW])
    w_out = din("w_out", [DM, DM])
    w_gate = din("w_gate", [DM, DFF])
    w_up = din("w_up", [DM, DFF])
    w_down = din("w_down", [DFF, DM])
    gmix_d = din("gmix", [128, 8])
    gffn_d = din("gffn", [128, 8])
    lam_d = din("da_lambda", [1, 256])
    subg_d = din("da_subln_g", [1, 128])
    convw_d = din("convw", [128, 8, 4])
    convb_d = din("convb", [128, 8])
    gateb_d = din("ml_gate_b", [1, 8])
    mlg_d = din("ml_norm_g", [1, 512])
    fing_d = din("final_norm_g", [1, DM])
    ident_d = din("ident", [128, 128], BF16)
    rmat_d = din("rmat", [128, 128], BF16)
    causal_d = din("causal", [128, 128], BF16)
    tri_d = din("tri", [128, 128])
    masks_d = din("masks", [128, 4, 1024], BF16)
    invf_d = din("invf", [128, 1])
    sgn_d = din("sgn", [128, 1])
    y_out = nc.dram_tensor("y", [SO, DM], F32, kind="ExternalOutput").ap()

    QT = dscr("QT", [4, 128, SO])
    KT = dscr("KT", [4, 128, SA])
    V1 = dscr("V1", [4, 128, NT, 130])
    MQT = dscr("MQT", [4, 128, SO])
    MKT = dscr("MKT", [4, 128, SO])
    MK = dscr("MK", [128, NT, 512])
    MV1 = dscr("MV1", [128, NT, 4, 130])
    GSO = dscr("GSO", [128, NTO, 512])
    MIXT = dscr("MIXT", [DM, SO])
    QTb, KTb, V1b, MQTb, MKTb, MKb, MV1b, GSOb, MIXTb = [Buf(n) for n in
                                                          "QT KT V1 MQT MKT MK MV1 GSO MIXT".split()]

    st = ExitStack()
    ARENA_N = 53000
    arena_t = st.enter_context(nc.sbuf_tensor("arena", [128, ARENA_N], F32))
    AR = Arena(arena_t)
    psbig = st.enter_context(nc.psum_tensor("psbig", [128, 4096], F32))
    psum = [T(psbig[:, i * 512:(i + 1) * 512], "ps%d" % i, True) for i in range(8)]
    P = Prog(nc)

    def bufs(ts):
        return [t.b if isinstance(t, T) else t for t in ts]

    def mm(out, lhsT, rhs, start, stop, r, w, skip=False):
        P.add("pe", lambda e: e.matmul(out, lhsT=lhsT, rhs=rhs, start=start, stop=stop, skip_group_check=skip),
              reads=bufs(r), writes=bufs(w))

    def tr(out, in_, ident, r, w):
        P.add("pe", lambda e: e.transpose(out=out, in_=in_, identity=ident), reads=bufs(r), writes=bufs(w))

    def act(out, in_, func, r, w, bias=None, scale=None, accum=None, eng="act"):
        kw = {}
        if bias is not None:
            kw["bias"] = bias
        if scale is not None:
            kw["scale"] = scale
        if accum is not None:
            kw["accum_out"] = accum
        P.add("act", lambda e: e.activation(out=out, in_=in_, func=func, **kw), reads=bufs(r), writes=bufs(w))

    def ts(eng, out, in0, s1, s2, op0, op1, r, w):
        if op1 is None:
            P.add(eng, lambda e: e.tensor_scalar(out=out, in0=in0, scalar1=s1, scalar2=None, op0=op0),
                  reads=bufs(r), writes=bufs(w))
        else:
            P.add(eng, lambda e: e.tensor_scalar(out=out, in0=in0, scalar1=s1, scalar2=s2, op0=op0, op1=op1),
                  reads=bufs(r), writes=bufs(w))

    def tt(eng, out, in0, in1, op, r, w):
        P.add(eng, lambda e: e.tensor_tensor(out=out, in0=in0, in1=in1, op=op), reads=bufs(r), writes=bufs(w))

    def stt(out, in0, scalar, in1, op0, op1, r, w, accum=None):
        if accum is None:
            P.add("dve", lambda e: e.scalar_tensor_tensor(out=out, in0=in0, scalar=scalar, in1=in1, op0=op0, op1=op1),
                  reads=bufs(r), writes=bufs(w))
        else:
            P.add("dve", lambda e: e.scalar_tensor_tensor(out=out, in0=in0, scalar=scalar, in1=in1, op0=op0,
                                                          op1=op1, accum_out=accum), reads=bufs(r), writes=bufs(w))

    def cp(eng, out, in_, r, w):
        if eng == "act":
            act(out, in_, AF.Copy, r, w)
        else:
            P.add(eng, lambda e: e.tensor_copy(out=out, in_=in_), reads=bufs(r), writes=bufs(w))

    def recip(out, in_, r, w):
        P.add("dve", lambda e: e.reciprocal(out=out, in_=in_), reads=bufs(r), writes=bufs(w))

    def memset(eng, ap, val, w):
        P.add(eng, lambda e: e.memset(ap, val), writes=bufs(w))

    STQ = "pool"

    def dma(q, out, in_, r, w, key, final=False):
        if q == "pool":
            q = STQ
        P.add(q, lambda e, s: e.dma_start(out=out, in_=in_).then_inc(s, 16), reads=bufs(r), writes=bufs(w),
              dma=True, semkey=key, out=final)

    def dmas(q, pairs, r, w, key, final=False):
        if q == "pool":
            q = STQ
        def fn(e, s):
            for o, i in pairs:
                e.dma_start(out=o, in_=i).then_inc(s, 16)
        P.add(q, fn, reads=bufs(r), writes=bufs(w), dma=True, semkey=key, ndma=len(pairs), out=final)

    def pconst(n, dt, name, src, q="sp"):
        t = AR.alloc(n, dt, name)
        dma(q, t.ap, src, [], [t], "const")
        return t

    ident = pconst(128, BF16, "ident", ident_d[:, :])
    rmat = pconst(128, BF16, "rmat", rmat_d[:, :])
    causal = pconst(128, BF16, "causal", causal_d[:, :])
    tri = pconst(128, F32, "tri", tri_d[:, :])
    invf = pconst(1, F32, "invf", invf_d[:, :])
    sgn = pconst(1, F32, "sgn", sgn_d[:, :])
    keybias = pconst(NKB, F32, "keybias", keybias_d[:, :])
    flag = pconst(1, F32, "flag", flag_d[:, :])
    gmix = pconst(8, F32, "gmix", gmix_d[:, :])
    gffn = pconst(8, F32, "gffn", gffn_d[:, :])
    convw = pconst(32, F32, "convw", convw_d.rearrange("p b j -> p (b j)"))
    convb = pconst(8, F32, "convb", convb_d[:, :])
    gateb = pconst(8, F32, "gateb", gateb_d[0:1, :].partition_broadcast(128))
    mlg = pconst(512, F32, "mlg", mlg_d[0:1, :].partition_broadcast(128))
    fing = pconst(DM, F32, "fing", fing_d[0:1, :].partition_broadcast(128))
    g08 = pconst(128, F32, "g08", subg_d[0:1, :].partition_broadcast(128))
    lamv = pconst(256, F32, "lamv", lam_d[0:1, :].partition_broadcast(128))
    P.barrier()
    ones128 = AR.alloc(128, F32, "ones128")
    memset("dve", ones128.ap, 1.0, [ones128])
    neghalf = AR.alloc(1, F32, "neghalf")
    memset("dve", neghalf.ap, -0.5, [neghalf])
    onecol = AR.alloc(1, F32, "onecol")
    memset("dve", onecol.ap, 1.0, [onecol])
    GP = AR.alloc(NT * 12, F32, "GP")
    GPv = GP.ap.rearrange("p (t c) -> p t c", c=12)
    small = AR.alloc(64, F32, "small")
    lam = T(small.ap[:, 0:1], "lam")
    neglam = T(small.ap[:, 1:2], "neglam")
    s01 = T(small.ap[:, 2:3], "s01")
    s23 = T(small.ap[:, 3:4], "s23")
    junk = AR.alloc(256, F32, "junkc")
    ts("dve", g08.ap, g08.ap, 0.8, None, ALU.mult, None, [g08], [g08])
    stt(junk.ap[:, 0:64], lamv.ap[:, 0:64], 1.0, lamv.ap[:, 64:128], ALU.mult, ALU.mult, [lamv], [junk, s01], accum=s01.ap)
    stt(junk.ap[:, 0:64], lamv.ap[:, 128:192], 1.0, lamv.ap[:, 192:256], ALU.mult, ALU.mult, [lamv], [junk, s23], accum=s23.ap)
    act(s01.ap, s01.ap, AF.Exp, [s01], [s01])
    act(s23.ap, s23.ap, AF.Exp, [s23], [s23])
    tt("dve", lam.ap, s01.ap, s23.ap, ALU.subtract, [s01, s23], [lam])
    ts("dve", lam.ap, lam.ap, 0.2, None, ALU.add, None, [lam], [lam])
    ts("dve", neglam.ap, lam.ap, -1.0, None, ALU.mult, None, [lam], [neglam])
    PERSIST = AR.off
    P.barrier()

    def finish_early():
        dma("sp", y_out[0:128, :], fing.ap, [fing], [], "const", final=True)
        P.emit(st)
        st.close()
        return nc

    if upto == "0":
        return finish_early()
    AR.off = PERSIST
    win = AR.alloc(8 * INW, BF16, "win")
    winv = win.ap.rearrange("p (k c) -> p k c", c=INW)
    TAB = 4096 if max(SP, SO) > 2048 else max(SP, SO)
    Ctab = AR.alloc(max(SP, SO), F32, "Ctab")
    Stab = AR.alloc(max(SP, SO), F32, "Stab")
    ttmp = [AR.alloc(512, F32, "ttmp%d" % i) for i in range(3)]
    tint = AR.alloc(512, I32, "tint")
    A_MARK = AR.off
    wst = [AR.alloc(1796, F32, "wst%d" % i) for i in range(2)]
    n = 0
    for k in range(8):
        for c0 in (0, 1796):
            s = wst[n % 2]
            dma("sp", s.ap, w_in[k * 128:(k + 1) * 128, c0:c0 + 1796], [], [s], s.b.name)
            ts("dve", winv[:, k, c0:c0 + 1796], s.ap, gmix.ap[:, k:k + 1], None, ALU.mult, None,
               [s, gmix], [win])
            n += 1


    def build_tables(t0, ntok):
        for c0 in range(0, ntok, 512):
            cn = min(512, ntok - c0)
            pi_, ang, u = ttmp[0], ttmp[1], ttmp[2]
            dma("sp", tint.ap[:, 0:cn], pos[0:1, t0 + c0:t0 + c0 + cn].partition_broadcast(128), [], [tint], "tint")
            cp("dve", pi_.ap[:, 0:cn], tint.ap[:, 0:cn], [tint], [pi_])
            ts("dve", ang.ap[:, 0:cn], pi_.ap[:, 0:cn], invf.ap, None, ALU.mult, None, [pi_, invf], [ang])
            for tab, shift in ((Stab, 0.0), (Ctab, 0.25)):
                ts("dve", u.ap[:, 0:cn], ang.ap[:, 0:cn], 1.0 / TWO_PI, shift, ALU.mult, ALU.add, [ang], [u])
                cp("dve", tint.ap[:, 0:cn], u.ap[:, 0:cn], [u], [tint])
                cp("dve", pi_.ap[:, 0:cn], tint.ap[:, 0:cn], [tint], [pi_])
                tt("dve", u.ap[:, 0:cn], u.ap[:, 0:cn], pi_.ap[:, 0:cn], ALU.subtract, [u, pi_], [u])
                ts("dve", u.ap[:, 0:cn], u.ap[:, 0:cn], TWO_PI, None, ALU.mult, None, [u], [u])
                ts("dve", u.ap[:, 0:cn], u.ap[:, 0:cn], 3.1415925, -3.1415925, ALU.min, ALU.max, [u], [u])
                act(tab.ap[:, c0:c0 + cn], u.ap[:, 0:cn], AF.Sin, [u], [tab])
            ts("dve", Stab.ap[:, c0:c0 + cn], Stab.ap[:, c0:c0 + cn], sgn.ap, None, ALU.mult, None, [Stab, sgn], [Stab])

    build_tables(0, SP)
    P.barrier()
    if upto == "A0":
        return finish_early()
    AR.off = A_MARK
    xs = [AR.alloc(DM, F32, "xs%d" % i) for i in range(4)]
    hb = [AR.alloc(DM, BF16, "hb%d" % i) for i in range(4)]
    hT = [AR.alloc(8 * 512, BF16, "hT%d" % i) for i in range(2)]
    zc = [AR.alloc(515, F32, "zc%d" % i) for i in range(8)]
    zb = [AR.alloc(512, BF16, "zb%d" % i) for i in range(2)]
    t1 = [AR.alloc(512, F32, "t1_%d" % i) for i in range(2)]
    t2 = [AR.alloc(512, F32, "t2_%d" % i) for i in range(2)]
    yc = [AR.alloc(512, F32, "yc%d" % i) for i in range(2)]
    ost = [AR.alloc(512, BF16, "ost%d" % i) for i in range(3)]
    kst = [AR.alloc(4 * 512, BF16, "kst%d" % i) for i in range(2)]
    vst = [AR.alloc(520, BF16, "vst%d" % i) for i in range(4)]
    ktk = [AR.alloc(512, BF16, "ktk%d" % i) for i in range(2)]
    ef = [AR.alloc(512, F32, "ef%d" % i) for i in range(2)]
    gst = [AR.alloc(512, BF16, "gst%d" % i) for i in range(2)]
    ssA = [AR.alloc(4, F32, "ssA%d" % i) for i in range(4)]
    gts = [AR.alloc(16, F32, "gts%d" % i) for i in range(4)]
    sqj = AR.alloc(DM, BF16, "sqj")
    for v_ in vst:
        memset("dve", v_.ap, 0.0, [v_])
        memset("dve", v_.ap.rearrange("p (h c) -> p h c", c=130)[:, :, 128:129], 1.0, [v_])
    for z_ in zc:
        memset("dve", z_.ap[:, 0:3], 0.0, [z_])
    PS_TOK = [psum[0], psum[1]]
    PS_FM = [psum[2], psum[3]]
    PS_RZ = psum[4]
    PS_TR = psum[5]
    PS_G = psum[6]
    PS_KT = psum[7]
    cnt = {"x": 0, "tok": 0, "fm": 0, "z": 0, "o": 0, "v": 0, "k": 0, "e": 0, "g": 0, "y": 0, "kst": 0}

    def nxt(key, lst):
        i = cnt[key]
        cnt[key] += 1
        return lst[i % len(lst)]

    def lnt_dma(xsrc_rows, slot):
        x_ = xs[slot]
        dma("sp", x_.ap, xsrc_rows, [], [x_], x_.b.name)

    def lnt_compute(slot):
        lnt_c1(slot)
        lnt_c2(slot)

    def lnt_c1(slot):
        x_, ss_ = xs[slot], ssA[slot]
        act(sqj.ap, x_.ap, AF.Square, [x_], [sqj, ss_], accum=ss_.ap[:, 0:1])
        ts("dve", ss_.ap[:, 1:2], ss_.ap[:, 0:1], 1.0 / DM, EPS, ALU.mult, ALU.add, [ss_], [ss_])

    def lnt_c2(slot):
        x_, ss_, h_ = xs[slot], ssA[slot], hb[slot]
        act(ss_.ap[:, 2:3], ss_.ap[:, 1:2], AF.Ln, [ss_], [ss_])
        act(ss_.ap[:, 2:3], ss_.ap[:, 2:3], AF.Exp, [ss_], [ss_], scale=-0.5)
        ts("dve", h_.ap, x_.ap, ss_.ap[:, 2:3], None, ALU.mult, None, [x_, ss_], [h_])

    def lnt_transpose(slot, hT_t, col):
        h_ = hb[slot]
        pst = PS_TR.ap.bitcast(BF16)
        for k in range(8):
            tr(pst[:, k * 128:(k + 1) * 128], h_.ap[:, k * 128:(k + 1) * 128], ident.ap, [h_, ident], [PS_TR])
        hv = hT_t.ap.rearrange("p (k t) -> p k t", t=512)
        cp("act", hv[:, :, col:col + 128], pst.rearrange("p (k t) -> p k t", t=128), [PS_TR], [hT_t])

    def load_norm_transpose(xsrc_rows, hT_t, col):
        lnt_dma(xsrc_rows, 0)
        lnt_compute(0)
        lnt_transpose(0, hT_t, col)

    def tok_matmul(hT_t, col, c0, ncols, ps):
        hv = hT_t.ap.rearrange("p (k t) -> p k t", t=512)
        for k in range(8):
            mm(ps.ap[:, 0:ncols], hv[:, k, col:col + 128], winv[:, k, c0:c0 + ncols], k == 0, k == 7, [hT_t, win], [ps])

    def fm_matmul(hT_t, c0, ps, ntok=512):
        hv = hT_t.ap.rearrange("p (k t) -> p k t", t=512)
        for k in range(8):
            mm(ps.ap[:, 0:ntok], winv[:, k, c0:c0 + 128], hv[:, k, 0:ntok], k == 0, k == 7, [hT_t, win], [ps])

    def rope_block(ps, tab0, dst, dstb):
        z_ = nxt("z", zb)
        i = (cnt["z"] - 1) % 2
        cp("act", z_.ap, ps.ap, [ps], [z_])
        mm(PS_RZ.ap, rmat.ap, z_.ap, True, True, [rmat, z_], [PS_RZ])
        tt("dve", t1[i].ap, ps.ap, Ctab.ap[:, tab0:tab0 + 512], ALU.mult, [ps, Ctab], [t1[i]])
        tt("dve", t2[i].ap, PS_RZ.ap, Stab.ap[:, tab0:tab0 + 512], ALU.mult, [PS_RZ, Stab], [t2[i]])
        o_ = nxt("o", ost)
        tt("dve", o_.ap, t1[i].ap, t2[i].ap, ALU.add, [t1[i], t2[i]], [o_])
        dma("pool", dst, o_.ap, [o_], [dstb], o_.b.name)

    def conv_block(ps, blk, dst_ap):
        z_ = zc[blk]
        cp("act", z_.ap[:, 3:515], ps.ap, [ps], [z_])
        y_ = nxt("y", yc)
        cw = convw.ap.rearrange("p (b j) -> p b j", j=4)
        ts("dve", y_.ap, z_.ap[:, 3:515], cw[:, blk, 3:4], convb.ap[:, blk:blk + 1], ALU.mult, ALU.add, [z_, convw, convb], [y_])
        for j in (2, 1, 0):
            stt(y_.ap, z_.ap[:, j:j + 512], cw[:, blk, j:j + 1], y_.ap, ALU.mult, ALU.add, [z_, convw, y_], [y_])
        cp("dve", z_.ap[:, 0:3], z_.ap[:, 512:515], [z_], [z_])
        return y_

    def gates_part1(ps):
        g_ = nxt("g", gts)
        tt("dve", g_.ap[:, 0:8], ps.ap[:, 0:8], gateb.ap, ALU.add, [ps, gateb], [g_])
        act(g_.ap[:, 8:12], g_.ap[:, 4:8], AF.Exp, [g_], [g_], scale=-1.0)
        act(g_.ap[:, 8:12], g_.ap[:, 8:12], AF.Ln, [g_], [g_], bias=onecol.ap)
        return g_

    def gates_part2(g_, tile_idx):
        mm(PS_G.ap[:, 0:4], tri.ap, g_.ap[:, 8:12], True, True, [tri, g_], [PS_G])
        mm(PS_G.ap[:, 4:8], ones128.ap, g_.ap[:, 8:12], True, True, [ones128, g_], [PS_G])
        tt("dve", g_.ap[:, 12:16], g_.ap[:, 0:4], PS_G.ap[:, 0:4], ALU.add, [g_, PS_G], [g_])
        act(GPv[:, tile_idx, 0:4], g_.ap[:, 12:16], AF.Exp, [g_], [GP])
        act(GPv[:, tile_idx, 4:12], PS_G.ap[:, 0:8], AF.Exp, [PS_G], [GP], scale=-1.0)
        ts("dve", GPv[:, tile_idx, 4:8], GPv[:, tile_idx, 4:8], 128.0 ** -0.5, None, ALU.mult, None, [GP], [GP])


    def xrows(own, g, j):
        xsrc = x_own if own else x_pre
        row0 = (128 if own else 0) + g * 512
        return xsrc[row0 + j * 128:row0 + (j + 1) * 128, :]

    def lnt_group(own, g):
        hT_t = hT[(g + (NGP if own else 0)) % 2]
        for j in range(4):
            lnt_dma(xrows(own, g, j), j)
        for j in range(4):
            lnt_compute(j)
        for j in range(4):
            lnt_transpose(j, hT_t, j * 128)

    pend_k = []

    def phaseA_group(own, g, nxt_grp):
        hT_t = hT[(g + (NGP if own else 0)) % 2]
        if nxt_grp is not None:
            for j in range(4):
                lnt_dma(xrows(*nxt_grp, j), j)
        pend_g = []
        tile0 = (NTP + g * 4) if own else g * 4
        tab0 = g * 512
        for j in range(4):
            tile = tile0 + j
            col = j * 128
            ps = nxt("tok", PS_TOK)
            tok_matmul(hT_t, col, 1024, 512, ps)
            v_ = nxt("v", vst)
            cp("act", v_.ap.rearrange("p (h c) -> p h c", c=130)[:, :, 0:128], ps.ap.rearrange("p (h c) -> p h c", c=128), [ps], [v_])
            dma("pool", V1[:, :, tile, :].rearrange("h p c -> p h c"), v_.ap.rearrange("p (h c) -> p h c", c=130), [v_], [V1b], v_.b.name)
            ps = nxt("tok", PS_TOK)
            tok_matmul(hT_t, col, 2560, 512, ps)
            v_ = nxt("v", vst)
            cp("act", v_.ap.rearrange("p (h c) -> p h c", c=130)[:, :, 0:128], ps.ap.rearrange("p (h c) -> p h c", c=128), [ps], [v_])
            dma("pool", MV1[:, tile, :, :], v_.ap.rearrange("p (h c) -> p h c", c=130), [v_], [MV1b], v_.b.name)
            ps = nxt("tok", PS_TOK)
            tok_matmul(hT_t, col, 3584, 8, ps)
            g_ = gates_part1(ps)
            while pend_g:
                gates_part2(*pend_g.pop(0))
            pend_g.append((g_, tile))
            if nxt_grp is not None and j == 1:
                for jj in range(4):
                    lnt_c1(jj)
            if nxt_grp is not None and j == 2:
                for jj in range(4):
                    lnt_c2(jj)
            if own:
                ps = nxt("tok", PS_TOK)
                tok_matmul(hT_t, col, 3072, 512, ps)
                e_ = nxt("e", ef)
                i = (cnt["e"] - 1) % 2
                act(e_.ap, ps.ap, AF.Exp, [ps], [e_], scale=-1.0)
                ts("dve", e_.ap, e_.ap, 1.0, None, ALU.add, None, [e_], [e_])
                recip(e_.ap, e_.ap, [e_], [e_])
                tt("dve", gst[i].ap, e_.ap, mlg.ap, ALU.mult, [e_, mlg], [gst[i]])
                dma("pool", GSO[:, tile - NTP, :], gst[i].ap, [gst[i]], [GSOb], gst[i].b.name)
        while pend_k:
            pend_k.pop(0)()
        if nxt_grp is not None:
            hT_n = hT[(nxt_grp[1] + (NGP if nxt_grp[0] else 0)) % 2]
            for jj in range(4):
                lnt_transpose(jj, hT_n, jj * 128)
        while pend_g:
            gates_part2(*pend_g.pop(0))
        ks_ = nxt("kst", kst)
        ksv = ks_.ap.rearrange("p (h t) -> p h t", t=512)
        kbase = SP if own else 0

        def mlq_post(ps, h):
            y_ = conv_block(ps, h, None)

            def fin():
                o_ = nxt("o", ost)
                act(o_.ap, y_.ap, AF.Silu, [y_], [o_])
                dma("pool", MQT[h, :, g * 512:(g + 1) * 512], o_.ap, [o_], [MQTb], o_.b.name)
            return fin

        def mlk_post(ps, h):
            y_ = conv_block(ps, 4 + h, None)

            def fin():
                act(ksv[:, h, :], y_.ap, AF.Silu, [y_], [ks_])
            return fin

        blocks = []
        for h in range(4):
            if own:
                blocks.append((h * 128, lambda ps, h=h: rope_block(ps, tab0, QT[h, :, g * 512:(g + 1) * 512], QTb)))
            blocks.append((512 + h * 128, lambda ps, h=h: rope_block(ps, tab0, KT[h, :, kbase + g * 512:kbase + (g + 1) * 512], KTb)))
        if own:
            for h in range(4):
                blocks.append((1536 + h * 128, lambda ps, h=h: mlq_post(ps, h)))
        for h in range(4):
            blocks.append((2048 + h * 128, lambda ps, h=h: mlk_post(ps, h)))
        pend_fin = []
        ps_cur = nxt("fm", PS_FM)
        fm_matmul(hT_t, blocks[0][0], ps_cur)
        for b in range(len(blocks)):
            ps_next = None
            if b + 1 < len(blocks):
                ps_next = nxt("fm", PS_FM)
                fm_matmul(hT_t, blocks[b + 1][0], ps_next)
            fin = blocks[b][1](ps_cur)
            if pend_fin:
                pend_fin.pop(0)()
            if fin is not None:
                pend_fin.append(fin)
            ps_cur = ps_next
        while pend_fin:
            pend_fin.pop(0)()
        if own:
            dma("pool", MKT[:, :, g * 512:(g + 1) * 512].rearrange("h p t -> p h t"), ksv, [ks_], [MKTb], ks_.b.name)
        def ktrans():
            pkt = PS_KT.ap.bitcast(BF16)
            for j in range(4):
                for h in range(4):
                    tr(pkt[:, h * 128:(h + 1) * 128], ksv[:, h, j * 128:(j + 1) * 128], ident.ap, [ks_, ident], [PS_KT])
                k_ = nxt("k", ktk)
                cp("dve", k_.ap, pkt[:, 0:512], [PS_KT], [k_])
                dma("pool", MK[:, tile0 + j, :], k_.ap, [k_], [MKb], k_.b.name)
        pend_k.append(ktrans)

    def halo_step():
        hT_t = hT[0]
        load_norm_transpose(x_own[0:128, :], hT_t, 0)
        for blk in range(8):
            ps = nxt("fm", PS_FM)
            fm_matmul(hT_t, (1536 if blk < 4 else 2048) + (blk % 4) * 128, ps, ntok=128)
            cp("act", zc[blk].ap[:, 0:3], ps.ap[:, 125:128], [ps], [zc[blk]])

    lnt_group(False, 0)
    for g in range(NGP):
        phaseA_group(False, g, (False, g + 1) if g + 1 < NGP else None)
    while pend_k:
        pend_k.pop(0)()
    build_tables(SP, SO)
    halo_step()
    lnt_group(True, 0)
    for g in range(NGO):
        phaseA_group(True, g, (True, g + 1) if g + 1 < NGO else None)
    while pend_k:
        pend_k.pop(0)()
    P.barrier()
    if upto == "A":
        return finish_early()

    AR.off = PERSIST
    masks = AR.alloc(4 * 1024, BF16, "masks")
    dma("sp", masks.ap, masks_d.rearrange("p j c -> p (j c)"), [], [masks], "masks")
    masksv = masks.ap.rearrange("p (j c) -> p j c", c=1024)
    kts = [AR.alloc(SA, BF16, "kts%d" % i) for i in range(2)]
    vs = [AR.alloc(NT * 130, BF16, "vs%d" % i) for i in range(2)]
    qts = [AR.alloc(SO, BF16, "qts%d" % i) for i in range(2)]
    pT = [AR.alloc(1024, BF16, "pT%d" % i) for i in range(3)]
    osb = [AR.alloc(8 * 129, F32, "osb%d" % i) for i in range(2)]
    ob = AR.alloc(128, F32, "ob")
    aob = [AR.alloc(128, BF16, "aob%d" % i) for i in range(2)]
    ast = [AR.alloc(512, BF16, "ast%d" % i) for i in range(2)]
    sc8 = [AR.alloc(8, F32, "sc8_%d" % i) for i in range(2)]
    ST = [T(None, "ST0", True), T(None, "ST1", True)]
    OB = [psum[4], psum[5], psum[6]]

    def oreg(r):
        return OB[r // 3].ap[:, (r % 3) * 129:(r % 3) * 129 + 129], OB[r // 3]

    nB = {"p": 0, "o": 0, "a": 0, "st": 0}
    zer = AR.alloc(128, BF16, "zer")
    memset("dve", zer.ap, 0.0, [zer])
    STap = [psbig[:, 0:1024], psbig[:, 1024:2048]]
    steps = [(h, G, kb) for h in range(4) for G in range(NGO) for kb in range(NTP + 4 * (G + 1))]
    hbuf = {}

    def head_bufs(h):
        if h not in hbuf:
            kt_, v_, qt_ = kts[h % 2], vs[h % 2], qts[h % 2]
            dma("sp", kt_.ap, KT[h, :, :], [KTb], [kt_], kt_.b.name)
            dma("sp", v_.ap, V1[h, :, :, :].rearrange("p t c -> p (t c)"), [V1b], [v_], v_.b.name)
            dma("sp", qt_.ap, QT[h, :, :], [QTb], [qt_], qt_.b.name)
            hbuf[h] = (kt_, v_, qt_)
        return hbuf[h]

    def emit_st(i):
        h, G, kb = steps[i]
        kt_, v_, qt_ = head_bufs(h)
        stb = ST[i % 2]
        sap = STap[i % 2]
        mm(sap[:, 0:512], kt_.ap[0:64, kb * 128:(kb + 1) * 128], qt_.ap[0:64, G * 512:(G + 1) * 512], True, True, [kt_, qt_], [stb])
        mm(sap[:, 512:1024], kt_.ap[64:128, kb * 128:(kb + 1) * 128], qt_.ap[64:128, G * 512:(G + 1) * 512], True, True, [kt_, qt_], [stb])

    def post_part2(h, G, o_, s8):
        ov = o_.ap.rearrange("p (r c) -> p r c", c=129)
        recip(s8.ap, ov[:, :, 128], [o_], [s8])
        ts("dve", s8.ap[:, 4:8], s8.ap[:, 4:8], neglam.ap, None, ALU.mult, None, [s8, neglam], [s8])
        a_ = ast[nB["a"] % 2]
        nB["a"] += 1
        pst = psum[7].ap.bitcast(BF16)
        for qb in range(4):
            ts("dve", ob.ap, ov[:, qb, 0:128], s8.ap[:, qb:qb + 1], None, ALU.mult, None, [o_, s8], [ob])
            stt(ob.ap, ov[:, 4 + qb, 0:128], s8.ap[:, 4 + qb:5 + qb], ob.ap, ALU.mult, ALU.add, [o_, s8, ob], [ob])
            stt(junk.ap[:, 0:128], ob.ap, 1.0, ob.ap, ALU.mult, ALU.mult, [ob], [junk, small], accum=small.ap[:, 8 + qb:9 + qb])
            ts("dve", small.ap[:, 12 + qb:13 + qb], small.ap[:, 8 + qb:9 + qb], 1.0 / 128, EPS, ALU.mult, ALU.add, [small], [small])
            act(small.ap[:, 16 + qb:17 + qb], small.ap[:, 12 + qb:13 + qb], AF.Ln, [small], [small])
            act(small.ap[:, 16 + qb:17 + qb], small.ap[:, 16 + qb:17 + qb], AF.Exp, [small], [small], scale=-0.5)
            ab = aob[qb % 2]
            stt(ab.ap, ob.ap, small.ap[:, 16 + qb:17 + qb], g08.ap, ALU.mult, ALU.mult, [ob, small, g08], [ab])
            tr(pst[:, qb * 128:(qb + 1) * 128], ab.ap, ident.ap, [ab, ident], [psum[7]])
        cp("dve", a_.ap, pst[:, 0:512], [psum[7]], [a_])
        dma("pool", MIXT[h * 128:(h + 1) * 128, G * 512:(G + 1) * 512], a_.ap, [a_], [MIXTb], a_.b.name)

    deferred = []
    emit_st(0)
    if len(steps) > 1:
        emit_st(1)
    for i, (h, G, kb) in enumerate(steps):
        kt_, v_, qt_ = head_bufs(h)
        vv = v_.ap.rearrange("p (t c) -> p t c", c=130)
        while deferred and deferred[0][0] <= i:
            deferred.pop(0)[1]()
        stb = ST[i % 2]
        p_ = pT[nB["p"] % 3]
        nB["p"] += 1
        act(p_.ap, STap[i % 2], AF.Exp, [stb, keybias], [p_], bias=keybias.ap[:, kb:kb + 1], scale=0.125)
        j = kb - (NTP + 4 * G)
        if j >= 0:
            tt("dve", p_.ap, p_.ap, masksv[:, j, :], ALU.mult, [p_, masks], [p_])
        if i + 2 < len(steps):
            emit_st(i + 2)
        if kb == 0:
            for bnk in range(3):
                nreg = 3 if bnk < 2 else 2
                memset("dve", OB[bnk].ap[:, 0:nreg * 129], 0.0, [OB[bnk]])
        for m in range(2):
            for qb in range(4):
                last = NTP + 4 * G + qb
                if kb > last:
                    continue
                oap, obuf = oreg(m * 4 + qb)
                mm(oap, p_.ap[:, m * 512 + qb * 128:m * 512 + (qb + 1) * 128], vv[:, kb, 0:129], False, kb == last, [p_, v_], [obuf], skip=True)
        if kb == NTP + 4 * (G + 1) - 1:
            o_ = osb[nB["o"] % 2]
            s8 = sc8[nB["o"] % 2]
            nB["o"] += 1
            for bnk in range(3):
                nreg = 3 if bnk < 2 else 2
                cp("dve", o_.ap[:, bnk * 387:bnk * 387 + nreg * 129], OB[bnk].ap[:, 0:nreg * 129], [OB[bnk]], [o_])
            deferred.append((i + 6, (lambda h=h, G=G, o_=o_, s8=s8: post_part2(h, G, o_, s8))))
    while deferred:
        deferred.pop(0)[1]()
    P.barrier()
    if upto == "B":
        return finish_early()

    AR.off = PERSIST
    wo = AR.alloc(8 * DM, BF16, "wo")
    wg = AR.alloc(8 * DFF, BF16, "wg")
    wu = AR.alloc(8 * DFF, BF16, "wu")
    WD_OFF = AR.off
    wov = wo.ap.rearrange("p (k c) -> p k c", c=DM)
    wgv = wg.ap.rearrange("p (k c) -> p k c", c=DFF)
    wuv = wu.ap.rearrange("p (k c) -> p k c", c=DFF)
    pf_chunks = [(w_out[k * 128:(k + 1) * 128, :], wov[:, k, :], None) for k in range(8)]
    for k in range(8):
        for c0 in (0, 1408):
            pf_chunks.append((w_gate[k * 128:(k + 1) * 128, c0:c0 + 1408], wgv[:, k, c0:c0 + 1408], k))
            pf_chunks.append((w_up[k * 128:(k + 1) * 128, c0:c0 + 1408], wuv[:, k, c0:c0 + 1408], k))
    mks = [AR.alloc(4 * 512, BF16, "mks%d" % i) for i in range(2)]
    mvs = [AR.alloc(4 * 520, BF16, "mvs%d" % i) for i in range(2)]
    mqs = [AR.alloc(4 * 512, BF16, "mqs%d" % i) for i in range(2)]
    mkts = [AR.alloc(4 * 512, BF16, "mkts%d" % i) for i in range(2)]
    gss = [AR.alloc(4 * 512, BF16, "gss%d" % i) for i in range(2)]
    Cf = AR.alloc(4 * 129, F32, "Cf")
    Cb = AR.alloc(4 * 130, BF16, "Cb")
    kw = [AR.alloc(128, BF16, "kw%d" % i) for i in range(3)]
    scp = [AR.alloc(128, BF16, "scp%d" % i) for i in range(3)]
    hm = [AR.alloc(128, BF16, "hm%d" % i) for i in range(3)]
    hst = [AR.alloc(4 * 512, BF16, "hst%d" % i) for i in range(2)]
    pc = [AR.alloc(32, F32, "pc%d" % i) for i in range(2)]
    wstC = [AR.alloc(1408, F32, "wstC%d" % i) for i in range(4)]
    pf_state = {"dma": 0, "cvt": 0}

    def prefetch_step():
        n = pf_state["dma"]
        if n < len(pf_chunks):
            src, dst, k = pf_chunks[n]
            s_ = wstC[n % 4]
            dma("sp", s_.ap[:, 0:dst.shape[1]], src, [], [s_], s_.b.name)
            pf_state["dma"] += 1
        m = pf_state["cvt"]
        if m < len(pf_chunks) and (pf_state["dma"] - m >= 4 or pf_state["dma"] == len(pf_chunks)):
            src, dst, k = pf_chunks[m]
            s_ = wstC[m % 4]
            ncol = dst.shape[1]
            eng = "dve" if m % 2 == 0 else "act"
            if k is None:
                cp(eng, dst, s_.ap[:, 0:ncol], [s_], [wo])
            elif eng == "act":
                act(dst, s_.ap[:, 0:ncol], AF.Copy, [s_, gffn], [wo], scale=gffn.ap[:, k:k + 1])
            else:
                ts("dve", dst, s_.ap[:, 0:ncol], gffn.ap[:, k:k + 1], None, ALU.mult, None, [s_, gffn], [wo])
            pf_state["cvt"] += 1

    Cfv = Cf.ap.rearrange("p (h c) -> p h c", c=129)
    Cbv = Cb.ap.rearrange("p (h c) -> p h c", c=130)
    PS_SC = [psum[0], psum[1]]
    PS_N = [T(None, "N0", True), T(None, "N1", True)]
    PS_U = psum[6]
    PS_T = psum[7]
    nC = {"kw": 0, "scp": 0, "hm": 0, "sc": 0}

    def state_update(kk, vv4, tile, h, need_cb, mk_t, mv_t, PS_U=psum[6]):
        k_ = kw[nC["kw"] % 3]
        nC["kw"] += 1
        act(k_.ap, kk, AF.Copy, [GP, mk_t], [k_], scale=GPv[:, tile, h:h + 1])
        mm(PS_U.ap[:, 0:129], k_.ap, vv4[:, h, 0:129], True, True, [k_, mv_t], [PS_U])
        act(Cfv[:, h, :], Cfv[:, h, :], AF.Copy, [GP, Cfh[h]], [Cfh[h]], scale=GPv[:, tile, 8 + h:9 + h])
        stt(Cfv[:, h, :], PS_U.ap[:, 0:129], GPv[:, tile, 8 + h:9 + h], Cfv[:, h, :], ALU.mult, ALU.add, [PS_U, GP, Cfh[h]], [Cfh[h]])
        if need_cb:
            cp("act", Cbv[:, h, 0:129], Cfv[:, h, :], [Cfh[h]], [Cbh[h]])

    Cfh = [Buf("Cf%d" % h) for h in range(4)]
    Cbh = [Buf("Cb%d" % h) for h in range(4)]
    memset("dve", Cf.ap, 0.0, Cfh)
    memset("dve", Cb.ap, 0.0, Cbh)
    for grp in range(NGP + NGO):
        own = grp >= NGP
        go = grp - NGP
        mk_, mv_ = mks[grp % 2], mvs[grp % 2]
        dma("sp", mk_.ap.rearrange("p (t c) -> p t c", c=512), MK[:, grp * 4:(grp + 1) * 4, :], [MKb], [mk_], mk_.b.name)
        dma("sp", mv_.ap.rearrange("p (t c) -> p t c", c=520), MV1[:, grp * 4:(grp + 1) * 4, :, :].rearrange("p t h c -> p t (h c)"),
            [MV1b], [mv_], mv_.b.name)
        if own:
            mq_, mkt_, gs_, hs_ = mqs[go % 2], mkts[go % 2], gss[go % 2], hst[go % 2]
            dma("sp", mq_.ap.rearrange("p (h t) -> p h t", t=512), MQT[:, :, go * 512:(go + 1) * 512].rearrange("h p t -> p h t"), [MQTb], [mq_], mq_.b.name)
            dma("sp", mkt_.ap.rearrange("p (h t) -> p h t", t=512), MKT[:, :, go * 512:(go + 1) * 512].rearrange("h p t -> p h t"), [MKTb], [mkt_], mkt_.b.name)
            dma("sp", gs_.ap.rearrange("p (t c) -> p t c", c=512), GSO[:, go * 4:(go + 1) * 4, :], [GSOb], [gs_], gs_.b.name)
            mqv = mq_.ap.rearrange("p (h t) -> p h t", t=512)
            mktv = mkt_.ap.rearrange("p (h t) -> p h t", t=512)
            gsv = gs_.ap.rearrange("p (t c) -> p t c", c=512)
            hsv = hs_.ap.rearrange("p (h t) -> p h t", t=512)
        mkv = mk_.ap.rearrange("p (t c) -> p t c", c=512)
        mvv = mv_.ap.rearrange("p (t h c) -> p t h c", h=4, c=130)
        if own and go == 0:
            P.barrier()
            for h in range(4):
                ts("dve", Cfv[:, h, :], Cfv[:, h, :], flag.ap, None, ALU.mult, None, [Cfh[h], flag], [Cfh[h]])
                cp("act", Cbv[:, h, 0:129], Cfv[:, h, :], [Cfh[h]], [Cbh[h]])
        for j in range(4):
            tile = grp * 4 + j
            if not own:
                prefetch_step()
                prefetch_step()
                for h in range(4):
                    state_update(mkv[:, j, h * 128:(h + 1) * 128], mvv[:, j], tile, h, False, mk_, mv_, psum[(tile * 4 + h) % 7])
                continue
            nb_ = PS_N[tile % 2]
            nbank = 2 + 2 * (tile % 2)
            for h in range(4):
                sc_ = PS_SC[nC["sc"] % 2]
                nC["sc"] += 1
                mm(sc_.ap[:, 0:128], mktv[:, h, j * 128:(j + 1) * 128], mqv[:, h, j * 128:(j + 1) * 128], True, True, [mkt_, mq_], [sc_])
                s_ = scp[nC["scp"] % 3]
                nC["scp"] += 1
                stt(s_.ap, sc_.ap[:, 0:128], GPv[:, tile, h:h + 1], causal.ap, ALU.mult, ALU.mult, [sc_, GP, causal], [s_])
                nreg = psum[nbank + h // 2].ap[:, (h % 2) * 256:(h % 2) * 256 + 129]
                mm(nreg, s_.ap, mvv[:, j, h, 0:129], True, False, [s_, mv_], [nb_])
                mm(nreg, mqv[:, h, j * 128:(j + 1) * 128], Cbv[:, h, 0:129], False, True, [mq_, Cbh[h]], [nb_])
                state_update(mkv[:, j, h * 128:(h + 1) * 128], mvv[:, j], tile, h, True, mk_, mv_)
            p_ = pc[tile % 2]
            for h in range(4):
                nreg = psum[nbank + h // 2].ap[:, (h % 2) * 256:(h % 2) * 256 + 129]
                tt("dve", p_.ap[:, h:h + 1], nreg[:, 128:129], GPv[:, tile, 4 + h:5 + h], ALU.mult, [nb_, GP], [p_])
                act(junk.ap[:, 0:128], nreg[:, 0:128], AF.Square, [nb_], [junk, p_], accum=p_.ap[:, 8 + h:9 + h])
            stt(p_.ap[:, 4:8], p_.ap[:, 0:4], -1.0, p_.ap[:, 0:4], ALU.mult, ALU.max, [p_], [p_])
            ts("dve", p_.ap[:, 4:8], p_.ap[:, 4:8], 1.0, None, ALU.max, None, [p_], [p_])
            recip(p_.ap[:, 4:8], p_.ap[:, 4:8], [p_], [p_])
            tt("dve", p_.ap[:, 4:8], p_.ap[:, 4:8], GPv[:, tile, 4:8], ALU.mult, [p_, GP], [p_])
            tt("dve", p_.ap[:, 8:12], p_.ap[:, 8:12], p_.ap[:, 4:8], ALU.mult, [p_], [p_])
            tt("dve", p_.ap[:, 8:12], p_.ap[:, 8:12], p_.ap[:, 4:8], ALU.mult, [p_], [p_])
            ts("dve", p_.ap[:, 8:12], p_.ap[:, 8:12], 1.0 / 128, EPS, ALU.mult, ALU.add, [p_], [p_])
            act(p_.ap[:, 12:16], p_.ap[:, 8:12], AF.Ln, [p_], [p_])
            act(p_.ap[:, 12:16], p_.ap[:, 12:16], AF.Exp, [p_], [p_], scale=-0.5)
            tt("dve", p_.ap[:, 16:20], p_.ap[:, 12:16], p_.ap[:, 4:8], ALU.mult, [p_], [p_])
            ptr = PS_T.ap.bitcast(BF16)
            for h in range(4):
                nreg = psum[nbank + h // 2].ap[:, (h % 2) * 256:(h % 2) * 256 + 129]
                hm_ = hm[nC["hm"] % 3]
                nC["hm"] += 1
                stt(hm_.ap, nreg[:, 0:128], p_.ap[:, 16 + h:17 + h], gsv[:, j, h * 128:(h + 1) * 128], ALU.mult, ALU.mult, [nb_, p_, gs_], [hm_])
                tr(ptr[:, h * 128:(h + 1) * 128], hm_.ap, ident.ap, [hm_, ident], [PS_T])
            cp("act", hsv[:, :, j * 128:(j + 1) * 128], ptr[:, 0:512].rearrange("p (h t) -> p h t", t=128), [PS_T], [hs_])
        if own:
            dma("pool", MIXT[512:1024, go * 512:(go + 1) * 512].rearrange("(h e) t -> e h t", e=128), hsv, [hs_], [MIXTb], hs_.b.name)
    while pf_state["cvt"] < len(pf_chunks):
        prefetch_step()
    P.barrier()
    if upto == "C":
        return finish_early()

    AR.off = WD_OFF
    wd = AR.alloc(22 * DM, BF16, "wd")
    wdv = wd.ap.rearrange("p (k c) -> p k c", c=DM)
    D_MARK = AR.off
    wst = [AR.alloc(1408, F32, "wstD%d" % i) for i in range(3)]
    n = 0

    def wload(src, dst, scale_col):
        nonlocal n
        s = wst[n % 3]
        eng = ("dve", "act")[n % 2]
        ncol = dst.shape[1]
        dma("sp", s.ap[:, 0:ncol], src, [], [s], s.b.name)
        cp(eng, dst, s.ap[:, 0:ncol], [s], [wo])
        n += 1

    for k in range(22):
        wload(w_down[k * 128:(k + 1) * 128, :], wdv[:, k, :], None)
    P.barrier()
    AR.off = D_MARK
    GT = 256
    x1 = [AR.alloc(2 * DM, F32, "x1_%d" % i) for i in range(2)]
    mxs = [AR.alloc(8 * GT, BF16, "mxs%d" % i) for i in range(2)]
    h2 = AR.alloc(DM, BF16, "h2")
    h2T = AR.alloc(8 * GT, BF16, "h2T")
    aT = AR.alloc(22 * GT, BF16, "aT")
    sg = [AR.alloc(GT, BF16, "sg%d" % i) for i in range(2)]
    sD = [AR.alloc(8, F32, "sD%d" % i) for i in range(2)]
    sqd = AR.alloc(DM, BF16, "sqd")
    h2Tv = h2T.ap.rearrange("p (k t) -> p k t", t=GT)
    aTv = aT.ap.rearrange("p (k t) -> p k t", t=GT)
    PS_ACC = [psum[0], psum[1]]
    PS_GG = [psum[2], psum[3]]
    PS_UU = [psum[4], psum[5]]
    PS_TD = psum[6]
    nD = {"acc": 0, "g": 0}
    wD = [wo]
    for g in range(SO // GT):
        x1_ = x1[g % 2]
        mx_ = mxs[g % 2]
        x1v = x1_.ap.rearrange("p (j c) -> p j c", c=DM)
        mxv = mx_.ap.rearrange("p (k t) -> p k t", t=GT)
        dma("sp", mxv, MIXT[:, g * GT:(g + 1) * GT].rearrange("(k p) t -> p k t", p=128), [MIXTb], [mx_], mx_.b.name)
        dmas("sp", [(x1v[:, j, :], x_own[128 + g * GT + j * 128:128 + g * GT + (j + 1) * 128, :]) for j in range(2)], [], [x1_], x1_.b.name)
        for j in range(2):
            for c in range(2):
                acc = PS_ACC[nD["acc"] % 2]
                nD["acc"] += 1
                for k in range(8):
                    mm(acc.ap, mxv[:, k, j * 128:(j + 1) * 128], wov[:, k, c * 512:(c + 1) * 512], k == 0, k == 7, [mx_] + wD, [acc])
                tt("dve", x1v[:, j, c * 512:(c + 1) * 512], x1v[:, j, c * 512:(c + 1) * 512], acc.ap, ALU.add, [x1_, acc], [x1_])
            s_ = sD[j]
            act(sqd.ap, x1v[:, j, :], AF.Square, [x1_], [sqd, s_], accum=s_.ap[:, 0:1])
            ts("dve", s_.ap[:, 1:2], s_.ap[:, 0:1], 1.0 / DM, EPS, ALU.mult, ALU.add, [s_], [s_])
            act(s_.ap[:, 2:3], s_.ap[:, 1:2], AF.Ln, [s_], [s_])
            act(s_.ap[:, 2:3], s_.ap[:, 2:3], AF.Exp, [s_], [s_], scale=-0.5)
            ts("dve", h2.ap, x1v[:, j, :], s_.ap[:, 2:3], None, ALU.mult, None, [x1_, s_], [h2])
            ptd = PS_TD.ap.bitcast(BF16)
            for k in range(8):
                tr(ptd[:, k * 128:(k + 1) * 128], h2.ap[:, k * 128:(k + 1) * 128], ident.ap, [h2, ident], [PS_TD])
            cp("act", h2Tv[:, :, j * 128:(j + 1) * 128], ptd.rearrange("p (k t) -> p k t", t=128), [PS_TD], [h2T])
        for f in range(22):
            gg = PS_GG[f % 2]
            uu = PS_UU[f % 2]
            for k in range(8):
                mm(gg.ap[:, 0:GT], wgv[:, k, f * 128:(f + 1) * 128], h2Tv[:, k, :], k == 0, k == 7, [h2T] + wD, [gg])
            for k in range(8):
                mm(uu.ap[:, 0:GT], wuv[:, k, f * 128:(f + 1) * 128], h2Tv[:, k, :], k == 0, k == 7, [h2T] + wD, [uu])
            s_ = sg[f % 2]
            act(s_.ap, gg.ap[:, 0:GT], AF.Silu, [gg], [s_])
            tt("dve", aTv[:, f, :], s_.ap, uu.ap[:, 0:GT], ALU.mult, [s_, uu], [aT])
        for j in range(2):
            for c in range(2):
                acc = PS_ACC[nD["acc"] % 2]
                nD["acc"] += 1
                for f in range(22):
                    mm(acc.ap, aTv[:, f, j * 128:(j + 1) * 128], wdv[:, f, c * 512:(c + 1) * 512], f == 0, f == 21, [aT] + wD, [acc])
                tt("dve", x1v[:, j, c * 512:(c + 1) * 512], x1v[:, j, c * 512:(c + 1) * 512], acc.ap, ALU.add, [x1_, acc], [x1_])
            s_ = sD[j]
            act(sqd.ap, x1v[:, j, :], AF.Square, [x1_], [sqd, s_], accum=s_.ap[:, 4:5])
            ts("dve", s_.ap[:, 5:6], s_.ap[:, 4:5], 1.0 / DM, EPS, ALU.mult, ALU.add, [s_], [s_])
            act(s_.ap[:, 6:7], s_.ap[:, 5:6], AF.Ln, [s_], [s_])
            act(s_.ap[:, 6:7], s_.ap[:, 6:7], AF.Exp, [s_], [s_], scale=-0.5)
            stt(x1v[:, j, :], x1v[:, j, :], s_.ap[:, 6:7], fing.ap, ALU.mult, ALU.mult, [x1_, s_, fing], [x1_])
        dmas("pool", [(y_out[g * GT + j * 128:g * GT + (j + 1) * 128, :], x1v[:, j, :]) for j in range(2)], [x1_], [], x1_.b.name, final=True)

    P.emit(st)
    st.close()
    return nc


_INV_FREQ = (np.float32(500000.0) ** (-np.arange(0, 16, 2, dtype=np.float32) / np.float32(16.0))).astype(np.float32)


def _consts():
    bf = ml_dtypes.bfloat16
    c = {}
    c["ident"] = np.eye(128, dtype=np.float32).astype(bf)
    r = np.zeros((128, 128), np.float32)
    invf = np.zeros((128, 1), np.float32)
    sgn = np.zeros((128, 1), np.float32)
    for p in range(128):
        d = p % 64
        if d < 8:
            r[p + 8, p] = 1.0
            invf[p, 0] = _INV_FREQ[d]
            sgn[p, 0] = -1.0
        elif d < 16:
            r[p - 8, p] = 1.0
            invf[p, 0] = _INV_FREQ[d - 8]
            sgn[p, 0] = 1.0
    c["rmat"] = r.astype(bf)
    c["invf"] = invf
    c["sgn"] = sgn
    s = np.arange(128)
    c["causal"] = (s[:, None] <= s[None, :]).astype(np.float32).astype(bf)
    c["tri"] = (s[:, None] <= s[None, :]).astype(np.float32)
    m = np.zeros((128, 4, 1024), np.float32)
    q = np.arange(512)
    for j in range(4):
        kc = (j * 128 + s) // 64
        vis = (kc[:, None] <= (q[None, :] // 64)).astype(np.float32)
        m[:, j, 0:512] = vis
        m[:, j, 512:1024] = vis
    c["masks"] = m.astype(bf)
    return c


_NC_CACHE = {}


def _get_nc(NGP, NGO, dbg):
    key = (NGP, NGO, dbg)
    if key not in _NC_CACHE:
        _NC_CACHE[key] = build(NGP, NGO, dbg)
    return _NC_CACHE[key]


def make_in_maps(inputs, NGP=8, NGO=8, batches=4):
    f32 = np.float32
    x = np.asarray(inputs["x"], f32)
    positions = np.asarray(inputs["positions"], np.int32)
    SP, SO = NGP * 512, NGO * 512
    NT = (NGP + NGO) * 4
    consts = _consts()
    shared = dict(consts)
    shared["w_in"] = np.ascontiguousarray(np.asarray(inputs["w_in"], f32)[0])
    shared["w_out"] = np.ascontiguousarray(np.asarray(inputs["w_out"], f32)[0])
    shared["w_gate"] = np.ascontiguousarray(np.asarray(inputs["w_gate"], f32)[0])
    shared["w_up"] = np.ascontiguousarray(np.asarray(inputs["w_up"], f32)[0])
    shared["w_down"] = np.ascontiguousarray(np.asarray(inputs["w_down"], f32)[0])
    shared["gmix"] = np.ascontiguousarray(np.asarray(inputs["mix_norm_g"], f32)[0].reshape(8, 128).T)
    shared["gffn"] = np.ascontiguousarray(np.asarray(inputs["ffn_norm_g"], f32)[0].reshape(8, 128).T)
    shared["da_lambda"] = np.asarray(inputs["da_lambda"], f32)[0].reshape(1, 256)
    shared["da_subln_g"] = np.asarray(inputs["da_subln_g"], f32)[0].reshape(1, 128)
    cw = np.asarray(inputs["ml_conv_w"], f32)[0]
    shared["convw"] = np.ascontiguousarray(cw.reshape(4, 8, 128).transpose(2, 1, 0))
    shared["convb"] = np.ascontiguousarray(np.asarray(inputs["ml_conv_b"], f32)[0].reshape(8, 128).T)
    shared["ml_gate_b"] = np.asarray(inputs["ml_gate_b"], f32)[0].reshape(1, 8)
    shared["ml_norm_g"] = np.asarray(inputs["ml_norm_g"], f32)[0].reshape(1, 512)
    shared["final_norm_g"] = np.asarray(inputs["final_norm_g"], f32).reshape(1, DM)
    in_maps = []
    for c in range(2 * batches):
        b, g = c // 2, c % 2
        m = dict(shared)
        own0 = g * SO if g == 1 else 0
        if g == 1:
            own0 = SP
            halo = x[b, own0 - 128:own0]
        else:
            halo = np.zeros((128, DM), f32)
        m["x_own"] = np.ascontiguousarray(np.concatenate([halo, x[b, own0:own0 + SO]], axis=0))
        m["x_pre"] = np.ascontiguousarray(x[b, 0:SP])
        m["pos"] = np.ascontiguousarray(np.concatenate([positions[b, 0:SP], positions[b, own0:own0 + SO]])[None, :])
        kbias = np.zeros((128, NT), f32)
        if g == 0:
            kbias[:, 0:NGP * 4] = -30000.0
        m["keybias"] = kbias
        m["flag"] = np.full((128, 1), float(g), f32)
        in_maps.append(m)
    return in_maps


def kernel(**inputs):
    nc = _get_nc(8, 8, False)
    in_maps = make_in_maps(inputs, 8, 8, 4)
    res = run_bass_kernel_spmd(nc, in_maps, core_ids=list(range(8)))
    out = np.empty((4, 8192, DM), np.float32)
    for c in range(8):
        b, g = c // 2, c % 2
        out[b, g * 4096:(g + 1) * 4096] = res.results[c]["y"]
    return out
```

```python
import numpy as np
import ml_dtypes
from contextlib import ExitStack
import concourse.bass as bass
import concourse.mybir as mybir
from concourse.bass_utils import run_bass_kernel_spmd

F32 = mybir.dt.float32
BF16 = mybir.dt.bfloat16
I32 = mybir.dt.int32
AF = mybir.ActivationFunctionType
ALU = mybir.AluOpType

ENGS = ("pe", "act", "dve", "pool", "sp")
EPS = 1e-6
DM = 1024
DFF = 2816
INW = 3592
TWO_PI = 2.0 * np.pi


class Buf:
    __slots__ = ("name", "writers", "readers", "psum")

    def __init__(self, name, psum=False):
        self.name = name
        self.writers = {}
        self.readers = []
        self.psum = psum


class Op:
    __slots__ = ("eng", "fn", "deps", "dma", "semkey", "ndma", "signal", "count", "sem", "gidx")

    def __init__(self, eng, fn, dma, semkey, ndma):
        self.eng = eng
        self.fn = fn
        self.deps = set()
        self.dma = dma
        self.semkey = semkey
        self.ndma = ndma
        self.signal = False
        self.count = None
        self.sem = None


class Prog:
    def __init__(self, nc):
        self.nc = nc
        self.ops = []
        self.barrier_deps = set()
        self.last = {}
        self.pending_dma = []
        self.out_dma = []
        self.phase_map = {}

    def add(self, eng, fn, reads=(), writes=(), dma=False, semkey=None, ndma=1, out=False):
        if dma and semkey != "const":
            semkey = "%s%d" % (eng, self.phase_map.setdefault((eng, semkey), len(self.phase_map)))
        op = Op(eng, fn, dma, semkey, ndma)
        op.gidx = len(self.ops)
        deps = set(self.barrier_deps)
        for b in reads:
            deps.update(b.writers.values())
            if b.psum:
                deps.update(r for r in b.readers if r.eng != eng)
        for b in writes:
            deps.update(b.writers.values())
            deps.update(b.readers)
        for b in reads:
            b.readers.append(op)
        for b in writes:
            if b.readers:
                b.writers = {}
                b.readers = []
            b.writers[("dma", op.gidx) if dma else eng] = op
        deps.discard(op)
        op.deps = deps
        self.ops.append(op)
        self.last[eng] = op
        if dma:
            self.pending_dma.append(op)
            if out:
                self.out_dma.append(op)
        return op

    def barrier(self):
        self.barrier_deps = set(self.last.values()) | set(self.pending_dma)
        self.pending_dma = []
        self.phase_map = {}

    @staticmethod
    def needs_sem(op, d):
        if d.dma or op.dma:
            return True
        if d.eng != op.eng:
            return True
        return op.eng != "pe"

    def emit(self, stack):
        nc = self.nc
        for op in self.ops:
            for d in op.deps:
                if self.needs_sem(op, d):
                    d.signal = True
        for op in self.out_dma:
            op.signal = True
        eng_sem = {e: stack.enter_context(nc.semaphore("s_" + e)) for e in ENGS}
        dma_sem, dma_cnt = {}, {}
        eng_cnt = {e: 0 for e in ENGS}
        for op in self.ops:
            if op.dma:
                if op.semkey not in dma_sem:
                    dma_sem[op.semkey] = stack.enter_context(nc.semaphore("d_%s" % (op.semkey,)))
                    dma_cnt[op.semkey] = 0
                dma_cnt[op.semkey] += 16 * op.ndma
                op.sem = dma_sem[op.semkey]
                op.count = dma_cnt[op.semkey]
            elif op.signal:
                eng_cnt[op.eng] += 1
                op.sem = eng_sem[op.eng]
                op.count = eng_cnt[op.eng]
        for op in self.ops:
            if op.dma and op.semkey == "const":
                op.count = dma_cnt["const"]
        self.nsem = len(dma_sem) + len(ENGS)
        block = stack.enter_context(nc.Block())
        by_eng = {e: [o for o in self.ops if o.eng == e] for e in ENGS}
        final_waits = [(o.sem, o.count) for o in self.out_dma]

        def run(engobj, e):
            waited = {}
            for op in by_eng[e]:
                need = {}
                for d in op.deps:
                    if not self.needs_sem(op, d):
                        continue
                    k = id(d.sem)
                    if k not in need or need[k][1] < d.count:
                        need[k] = (d.sem, d.count)
                for k, (s, c) in need.items():
                    if waited.get(k, 0) < c:
                        engobj.wait_ge(s, c)
                        waited[k] = c
                if op.dma:
                    op.fn(engobj, op.sem)
                else:
                    ins = op.fn(engobj)
                    if op.signal:
                        ins.then_inc(op.sem, 1)
            if e == "sp":
                for s, c in final_waits:
                    if waited.get(id(s), 0) < c:
                        engobj.wait_ge(s, c)
                        waited[id(s)] = c

        @block.tensor
        def _(eng):
            run(eng, "pe")

        @block.scalar
        def _(eng):
            run(eng, "act")

        @block.vector
        def _(eng):
            run(eng, "dve")

        @block.gpsimd
        def _(eng):
            run(eng, "pool")

        @block.sync
        def _(eng):
            run(eng, "sp")


class T:
    __slots__ = ("ap", "b")

    def __init__(self, ap, name, psum=False):
        self.ap = ap
        self.b = Buf(name, psum)


class Arena:
    def __init__(self, t):
        self.t = t
        self.off = 0
        self.N = t.shape[1]
        self.n = 0

    def alloc(self, n_elems, dtype, name=None):
        nbytes = n_elems * (2 if dtype == BF16 else 4)
        n32 = (nbytes + 3) // 4
        n32 = (n32 + 7) // 8 * 8
        o = self.off
        self.off += n32
        assert self.off <= self.N, "arena overflow %d > %d (%s)" % (self.off, self.N, name)
        ap = self.t[:, o:o + n32]
        if dtype != F32:
            ap = ap.bitcast(dtype)
        ap = ap[:, 0:n_elems]
        self.n += 1
        return T(ap, name or ("t%d" % self.n))


def build(NGP=8, NGO=8, dbg=False, upto="D"):
    nc = bass.Bass("TRN2", target_bir_lowering=False)
    NTP, NTO = NGP * 4, NGO * 4
    NT = NTP + NTO
    SP, SO = NGP * 512, NGO * 512
    SA = SP + SO
    NKB = NT

    def din(name, shape, dt=F32):
        return nc.dram_tensor(name, list(shape), dt, kind="ExternalInput").ap()

    def dscr(name, shape, dt=BF16):
        kind = "ExternalOutput" if dbg else "Internal"
        return nc.dram_tensor(name, list(shape), dt, kind=kind).ap()

    x_own = din("x_own", [SO + 128, DM])
    x_pre = din("x_pre", [SP, DM])
    pos = din("pos", [1, SA], I32)
    keybias_d = din("keybias", [128, NKB])
    flag_d = din("flag", [128, 1])
    w_in = din("w_in", [DM, INW])
    w_out = din("w_out", [DM, DM])
    w_gate = din("w_gate", [DM, DFF])
    w_up = din("w_up", [DM, DFF])
    w_down = din("w_down", [DFF, DM])
    gmix_d = din("gmix", [128, 8])
    gffn_d = din("gffn", [128, 8])
    lam_d = din("da_lambda", [1, 256])
    subg_d = din("da_subln_g", [1, 128])
    convw_d = din("convw", [128, 8, 4])
    convb_d = din("convb", [128, 8])
    gateb_d = din("ml_gate_b", [1, 8])
    mlg_d = din("ml_norm_g", [1, 512])
    fing_d = din("final_norm_g", [1, DM])
    ident_d = din("ident", [128, 128], BF16)
    rmat_d = din("rmat", [128, 128], BF16)
    causal_d = din("causal", [128, 128], BF16)
    tri_d = din("tri", [128, 128])
    masks_d = din("masks", [128, 4, 1024], BF16)
    invf_d = din("invf", [128, 1])
    sgn_d = din("sgn", [128, 1])
    y_out = nc.dram_tensor("y", [SO, DM], F32, kind="ExternalOutput").ap()

    QT = dscr("QT", [4, 128, SO])
    KT = dscr("KT", [4, 128, SA])
    V1 = dscr("V1", [4, 128, NT, 130])
    MQT = dscr("MQT", [4, 128, SO])
    MKT = dscr("MKT", [4, 128, SO])
    MK = dscr("MK", [128, NT, 512])
    MV1 = dscr("MV1", [128, NT, 4, 130])
    GSO = dscr("GSO", [128, NTO, 512])
    MIXT = dscr("MIXT", [DM, SO])
    QTb, KTb, V1b, MQTb, MKTb, MKb, MV1b, GSOb, MIXTb = [Buf(n) for n in
                                                          "QT KT V1 MQT MKT MK MV1 GSO MIXT".split()]

    st = ExitStack()
    ARENA_N = 53000
    arena_t = st.enter_context(nc.sbuf_tensor("arena", [128, ARENA_N], F32))
    AR = Arena(arena_t)
    psbig = st.enter_context(nc.psum_tensor("psbig", [128, 4096], F32))
    psum = [T(psbig[:, i * 512:(i + 1) * 512], "ps%d" % i, True) for i in range(8)]
    P = Prog(nc)

    def bufs(ts):
        return [t.b if isinstance(t, T) else t for t in ts]

    def mm(out, lhsT, rhs, start, stop, r, w, skip=False):
        P.add("pe", lambda e: e.matmul(out, lhsT=lhsT, rhs=rhs, start=start, stop=stop, skip_group_check=skip),
              reads=bufs(r), writes=bufs(w))

    def tr(out, in_, ident, r, w):
        P.add("pe", lambda e: e.transpose(out=out, in_=in_, identity=ident), reads=bufs(r), writes=bufs(w))

    def act(out, in_, func, r, w, bias=None, scale=None, accum=None, eng="act"):
        kw = {}
        if bias is not None:
            kw["bias"] = bias
        if scale is not None:
            kw["scale"] = scale
        if accum is not None:
            kw["accum_out"] = accum
        P.add("act", lambda e: e.activation(out=out, in_=in_, func=func, **kw), reads=bufs(r), writes=bufs(w))

    def ts(eng, out, in0, s1, s2, op0, op1, r, w):
        if op1 is None:
            P.add(eng, lambda e: e.tensor_scalar(out=out, in0=in0, scalar1=s1, scalar2=None, op0=op0),
                  reads=bufs(r), writes=bufs(w))
        else:
            P.add(eng, lambda e: e.tensor_scalar(out=out, in0=in0, scalar1=s1, scalar2=s2, op0=op0, op1=op1),
                  reads=bufs(r), writes=bufs(w))

    def tt(eng, out, in0, in1, op, r, w):
        P.add(eng, lambda e: e.tensor_tensor(out=out, in0=in0, in1=in1, op=op), reads=bufs(r), writes=bufs(w))

    def stt(out, in0, scalar, in1, op0, op1, r, w, accum=None):
        if accum is None:
            P.add("dve", lambda e: e.scalar_tensor_tensor(out=out, in0=in0, scalar=scalar, in1=in1, op0=op0, op1=op1),
                  reads=bufs(r), writes=bufs(w))
        else:
            P.add("dve", lambda e: e.scalar_tensor_tensor(out=out, in0=in0, scalar=scalar, in1=in1, op0=op0,
                                                          op1=op1, accum_out=accum), reads=bufs(r), writes=bufs(w))

    def cp(eng, out, in_, r, w):
        if eng == "act":
            act(out, in_, AF.Copy, r, w)
        else:
            P.add(eng, lambda e: e.tensor_copy(out=out, in_=in_), reads=bufs(r), writes=bufs(w))

    def recip(out, in_, r, w):
        P.add("dve", lambda e: e.reciprocal(out=out, in_=in_), reads=bufs(r), writes=bufs(w))

    def memset(eng, ap, val, w):
        P.add(eng, lambda e: e.memset(ap, val), writes=bufs(w))

    STQ = "pool"

    def dma(q, out, in_, r, w, key, final=False):
        if q == "pool":
            q = STQ
        P.add(q, lambda e, s: e.dma_start(out=out, in_=in_).then_inc(s, 16), reads=bufs(r), writes=bufs(w),
              dma=True, semkey=key, out=final)

    def dmas(q, pairs, r, w, key, final=False):
        if q == "pool":
            q = STQ
        def fn(e, s):
            for o, i in pairs:
                e.dma_start(out=o, in_=i).then_inc(s, 16)
        P.add(q, fn, reads=bufs(r), writes=bufs(w), dma=True, semkey=key, ndma=len(pairs), out=final)

    def pconst(n, dt, name, src, q="sp"):
        t = AR.alloc(n, dt, name)
        dma(q, t.ap, src, [], [t], "const")
        return t

    ident = pconst(128, BF16, "ident", ident_d[:, :])
    rmat = pconst(128, BF16, "rmat", rmat_d[:, :])
    causal = pconst(128, BF16, "causal", causal_d[:, :])
    tri = pconst(128, F32, "tri", tri_d[:, :])
    invf = pconst(1, F32, "invf", invf_d[:, :])
    sgn = pconst(1, F32, "sgn", sgn_d[:, :])
    keybias = pconst(NKB, F32, "keybias", keybias_d[:, :])
    flag = pconst(1, F32, "flag", flag_d[:, :])
    gmix = pconst(8, F32, "gmix", gmix_d[:, :])
    gffn = pconst(8, F32, "gffn", gffn_d[:, :])
    convw = pconst(32, F32, "convw", convw_d.rearrange("p b j -> p (b j)"))
    convb = pconst(8, F32, "convb", convb_d[:, :])
    gateb = pconst(8, F32, "gateb", gateb_d[0:1, :].partition_broadcast(128))
    mlg = pconst(512, F32, "mlg", mlg_d[0:1, :].partition_broadcast(128))
    fing = pconst(DM, F32, "fing", fing_d[0:1, :].partition_broadcast(128))
    g08 = pconst(128, F32, "g08", subg_d[0:1, :].partition_broadcast(128))
    lamv = pconst(256, F32, "lamv", lam_d[0:1, :].partition_broadcast(128))
    P.barrier()
    ones128 = AR.alloc(128, F32, "ones128")
    memset("dve", ones128.ap, 1.0, [ones128])
    neghalf = AR.alloc(1, F32, "neghalf")
    memset("dve", neghalf.ap, -0.5, [neghalf])
    onecol = AR.alloc(1, F32, "onecol")
    memset("dve", onecol.ap, 1.0, [onecol])
    GP = AR.alloc(NT * 12, F32, "GP")
    GPv = GP.ap.rearrange("p (t c) -> p t c", c=12)
    small = AR.alloc(64, F32, "small")
    lam = T(small.ap[:, 0:1], "lam")
    neglam = T(small.ap[:, 1:2], "neglam")
    s01 = T(small.ap[:, 2:3], "s01")
    s23 = T(small.ap[:, 3:4], "s23")
    junk = AR.alloc(256, F32, "junkc")
    ts("dve", g08.ap, g08.ap, 0.8, None, ALU.mult, None, [g08], [g08])
    stt(junk.ap[:, 0:64], lamv.ap[:, 0:64], 1.0, lamv.ap[:, 64:128], ALU.mult, ALU.mult, [lamv], [junk, s01], accum=s01.ap)
    stt(junk.ap[:, 0:64], lamv.ap[:, 128:192], 1.0, lamv.ap[:, 192:256], ALU.mult, ALU.mult, [lamv], [junk, s23], accum=s23.ap)
    act(s01.ap, s01.ap, AF.Exp, [s01], [s01])
    act(s23.ap, s23.ap, AF.Exp, [s23], [s23])
    tt("dve", lam.ap, s01.ap, s23.ap, ALU.subtract, [s01, s23], [lam])
    ts("dve", lam.ap, lam.ap, 0.2, None, ALU.add, None, [lam], [lam])
    ts("dve", neglam.ap, lam.ap, -1.0, None, ALU.mult, None, [lam], [neglam])
    PERSIST = AR.off
    P.barrier()

    def finish_early():
        dma("sp", y_out[0:128, :], fing.ap, [fing], [], "const", final=True)
        P.emit(st)
        st.close()
        return nc

    if upto == "0":
        return finish_early()
    AR.off = PERSIST
    win = AR.alloc(8 * INW, BF16, "win")
    winv = win.ap.rearrange("p (k c) -> p k c", c=INW)
    TAB = 4096 if max(SP, SO) > 2048 else max(SP, SO)
    Ctab = AR.alloc(max(SP, SO), F32, "Ctab")
    Stab = AR.alloc(max(SP, SO), F32, "Stab")
    ttmp = [AR.alloc(512, F32, "ttmp%d" % i) for i in range(3)]
    tint = AR.alloc(512, I32, "tint")
    A_MARK = AR.off
    wst = [AR.alloc(1796, F32, "wst%d" % i) for i in range(2)]
    n = 0
    for k in range(8):
        for c0 in (0, 1796):
            s = wst[n % 2]
            dma("sp", s.ap, w_in[k * 128:(k + 1) * 128, c0:c0 + 1796], [], [s], s.b.name)
            ts("dve", winv[:, k, c0:c0 + 1796], s.ap, gmix.ap[:, k:k + 1], None, ALU.mult, None,
               [s, gmix], [win])
            n += 1


    def build_tables(t0, ntok):
        for c0 in range(0, ntok, 512):
            cn = min(512, ntok - c0)
            pi_, ang, u = ttmp[0], ttmp[1], ttmp[2]
            dma("sp", tint.ap[:, 0:cn], pos[0:1, t0 + c0:t0 + c0 + cn].partition_broadcast(128), [], [tint], "tint")
            cp("dve", pi_.ap[:, 0:cn], tint.ap[:, 0:cn], [tint], [pi_])
            ts("dve", ang.ap[:, 0:cn], pi_.ap[:, 0:cn], invf.ap, None, ALU.mult, None, [pi_, invf], [ang])
            for tab, shift in ((Stab, 0.0), (Ctab, 0.25)):
                ts("dve", u.ap[:, 0:cn], ang.ap[:, 0:cn], 1.0 / TWO_PI, shift, ALU.mult, ALU.add, [ang], [u])
                cp("dve", tint.ap[:, 0:cn], u.ap[:, 0:cn], [u], [tint])
                cp("dve", pi_.ap[:, 0:cn], tint.ap[:, 0:cn], [tint], [pi_])
                tt("dve", u.ap[:, 0:cn], u.ap[:, 0:cn], pi_.ap[:, 0:cn], ALU.subtract, [u, pi_], [u])
                ts("dve", u.ap[:, 0:cn], u.ap[:, 0:cn], TWO_PI, None, ALU.mult, None, [u], [u])
                ts("dve", u.ap[:, 0:cn], u.ap[:, 0:cn], 3.1415925, -3.1415925, ALU.min, ALU.max, [u], [u])
                act(tab.ap[:, c0:c0 + cn], u.ap[:, 0:cn], AF.Sin, [u], [tab])
            ts("dve", Stab.ap[:, c0:c0 + cn], Stab.ap[:, c0:c0 + cn], sgn.ap, None, ALU.mult, None, [Stab, sgn], [Stab])

    build_tables(0, SP)
    P.barrier()
    if upto == "A0":
        return finish_early()
    AR.off = A_MARK
    xs = [AR.alloc(DM, F32, "xs%d" % i) for i in range(4)]
    hb = [AR.alloc(DM, BF16, "hb%d" % i) for i in range(4)]
    hT = [AR.alloc(8 * 512, BF16, "hT%d" % i) for i in range(2)]
    zc = [AR.alloc(515, F32, "zc%d" % i) for i in range(8)]
    zb = [AR.alloc(512, BF16, "zb%d" % i) for i in range(2)]
    t1 = [AR.alloc(512, F32, "t1_%d" % i) for i in range(2)]
    t2 = [AR.alloc(512, F32, "t2_%d" % i) for i in range(2)]
    yc = [AR.alloc(512, F32, "yc%d" % i) for i in range(2)]
    ost = [AR.alloc(512, BF16, "ost%d" % i) for i in range(3)]
    kst = [AR.alloc(4 * 512, BF16, "kst%d" % i) for i in range(2)]
    vst = [AR.alloc(520, BF16, "vst%d" % i) for i in range(4)]
    ktk = [AR.alloc(512, BF16, "ktk%d" % i) for i in range(2)]
    ef = [AR.alloc(512, F32, "ef%d" % i) for i in range(2)]
    gst = [AR.alloc(512, BF16, "gst%d" % i) for i in range(2)]
    ssA = [AR.alloc(4, F32, "ssA%d" % i) for i in range(4)]
    gts = [AR.alloc(16, F32, "gts%d" % i) for i in range(4)]
    sqj = AR.alloc(DM, BF16, "sqj")
    for v_ in vst:
        memset("dve", v_.ap, 0.0, [v_])
        memset("dve", v_.ap.rearrange("p (h c) -> p h c", c=130)[:, :, 128:129], 1.0, [v_])
    for z_ in zc:
        memset("dve", z_.ap[:, 0:3], 0.0, [z_])
    PS_TOK = [psum[0], psum[1]]
    PS_FM = [psum[2], psum[3]]
    PS_RZ = psum[4]
    PS_TR = psum[5]
    PS_G = psum[6]
    PS_KT = psum[7]
    cnt = {"x": 0, "tok": 0, "fm": 0, "z": 0, "o": 0, "v": 0, "k": 0, "e": 0, "g": 0, "y": 0, "kst": 0}

    def nxt(key, lst):
        i = cnt[key]
        cnt[key] += 1
        return lst[i % len(lst)]

    def lnt_dma(xsrc_rows, slot):
        x_ = xs[slot]
        dma("sp", x_.ap, xsrc_rows, [], [x_], x_.b.name)

    def lnt_compute(slot):
        lnt_c1(slot)
        lnt_c2(slot)

    def lnt_c1(slot):
        x_, ss_ = xs[slot], ssA[slot]
        act(sqj.ap, x_.ap, AF.Square, [x_], [sqj, ss_], accum=ss_.ap[:, 0:1])
        ts("dve", ss_.ap[:, 1:2], ss_.ap[:, 0:1], 1.0 / DM, EPS, ALU.mult, ALU.add, [ss_], [ss_])

    def lnt_c2(slot):
        x_, ss_, h_ = xs[slot], ssA[slot], hb[slot]
        act(ss_.ap[:, 2:3], ss_.ap[:, 1:2], AF.Ln, [ss_], [ss_])
        act(ss_.ap[:, 2:3], ss_.ap[:, 2:3], AF.Exp, [ss_], [ss_], scale=-0.5)
        ts("dve", h_.ap, x_.ap, ss_.ap[:, 2:3], None, ALU.mult, None, [x_, ss_], [h_])

    def lnt_transpose(slot, hT_t, col):
        h_ = hb[slot]
        pst = PS_TR.ap.bitcast(BF16)
        for k in range(8):
            tr(pst[:, k * 128:(k + 1) * 128], h_.ap[:, k * 128:(k + 1) * 128], ident.ap, [h_, ident], [PS_TR])
        hv = hT_t.ap.rearrange("p (k t) -> p k t", t=512)
        cp("act", hv[:, :, col:col + 128], pst.rearrange("p (k t) -> p k t", t=128), [PS_TR], [hT_t])

    def load_norm_transpose(xsrc_rows, hT_t, col):
        lnt_dma(xsrc_rows, 0)
        lnt_compute(0)
        lnt_transpose(0, hT_t, col)

    def tok_matmul(hT_t, col, c0, ncols, ps):
        hv = hT_t.ap.rearrange("p (k t) -> p k t", t=512)
        for k in range(8):
            mm(ps.ap[:, 0:ncols], hv[:, k, col:col + 128], winv[:, k, c0:c0 + ncols], k == 0, k == 7, [hT_t, win], [ps])

    def fm_matmul(hT_t, c0, ps, ntok=512):
        hv = hT_t.ap.rearrange("p (k t) -> p k t", t=512)
        for k in range(8):
            mm(ps.ap[:, 0:ntok], winv[:, k, c0:c0 + 128], hv[:, k, 0:ntok], k == 0, k == 7, [hT_t, win], [ps])

    def rope_block(ps, tab0, dst, dstb):
        z_ = nxt("z", zb)
        i = (cnt["z"] - 1) % 2
        cp("act", z_.ap, ps.ap, [ps], [z_])
        mm(PS_RZ.ap, rmat.ap, z_.ap, True, True, [rmat, z_], [PS_RZ])
        tt("dve", t1[i].ap, ps.ap, Ctab.ap[:, tab0:tab0 + 512], ALU.mult, [ps, Ctab], [t1[i]])
        tt("dve", t2[i].ap, PS_RZ.ap, Stab.ap[:, tab0:tab0 + 512], ALU.mult, [PS_RZ, Stab], [t2[i]])
        o_ = nxt("o", ost)
        tt("dve", o_.ap, t1[i].ap, t2[i].ap, ALU.add, [t1[i], t2[i]], [o_])
        dma("pool", dst, o_.ap, [o_], [dstb], o_.b.name)

    def conv_block(ps, blk, dst_ap):
        z_ = zc[blk]
        cp("act", z_.ap[:, 3:515], ps.ap, [ps], [z_])
        y_ = nxt("y", yc)
        cw = convw.ap.rearrange("p (b j) -> p b j", j=4)
        ts("dve", y_.ap, z_.ap[:, 3:515], cw[:, blk, 3:4], convb.ap[:, blk:blk + 1], ALU.mult, ALU.add, [z_, convw, convb], [y_])
        for j in (2, 1, 0):
            stt(y_.ap, z_.ap[:, j:j + 512], cw[:, blk, j:j + 1], y_.ap, ALU.mult, ALU.add, [z_, convw, y_], [y_])
        cp("dve", z_.ap[:, 0:3], z_.ap[:, 512:515], [z_], [z_])
        return y_

    def gates_part1(ps):
        g_ = nxt("g", gts)
        tt("dve", g_.ap[:, 0:8], ps.ap[:, 0:8], gateb.ap, ALU.add, [ps, gateb], [g_])
        act(g_.ap[:, 8:12], g_.ap[:, 4:8], AF.Exp, [g_], [g_], scale=-1.0)
        act(g_.ap[:, 8:12], g_.ap[:, 8:12], AF.Ln, [g_], [g_], bias=onecol.ap)
        return g_

    def gates_part2(g_, tile_idx):
        mm(PS_G.ap[:, 0:4], tri.ap, g_.ap[:, 8:12], True, True, [tri, g_], [PS_G])
        mm(PS_G.ap[:, 4:8], ones128.ap, g_.ap[:, 8:12], True, True, [ones128, g_], [PS_G])
        tt("dve", g_.ap[:, 12:16], g_.ap[:, 0:4], PS_G.ap[:, 0:4], ALU.add, [g_, PS_G], [g_])
        act(GPv[:, tile_idx, 0:4], g_.ap[:, 12:16], AF.Exp, [g_], [GP])
        act(GPv[:, tile_idx, 4:12], PS_G.ap[:, 0:8], AF.Exp, [PS_G], [GP], scale=-1.0)
        ts("dve", GPv[:, tile_idx, 4:8], GPv[:, tile_idx, 4:8], 128.0 ** -0.5, None, ALU.mult, None, [GP], [GP])


    def xrows(own, g, j):
        xsrc = x_own if own else x_pre
        row0 = (128 if own else 0) + g * 512
        return xsrc[row0 + j * 128:row0 + (j + 1) * 128, :]

    def lnt_group(own, g):
        hT_t = hT[(g + (NGP if own else 0)) % 2]
        for j in range(4):
            lnt_dma(xrows(own, g, j), j)
        for j in range(4):
            lnt_compute(j)
        for j in range(4):
            lnt_transpose(j, hT_t, j * 128)

    pend_k = []

    def phaseA_group(own, g, nxt_grp):
        hT_t = hT[(g + (NGP if own else 0)) % 2]
        if nxt_grp is not None:
            for j in range(4):
                lnt_dma(xrows(*nxt_grp, j), j)
        pend_g = []
        tile0 = (NTP + g * 4) if own else g * 4
        tab0 = g * 512
        for j in range(4):
            tile = tile0 + j
            col = j * 128
            ps = nxt("tok", PS_TOK)
            tok_matmul(hT_t, col, 1024, 512, ps)
            v_ = nxt("v", vst)
            cp("act", v_.ap.rearrange("p (h c) -> p h c", c=130)[:, :, 0:128], ps.ap.rearrange("p (h c) -> p h c", c=128), [ps], [v_])
            dma("pool", V1[:, :, tile, :].rearrange("h p c -> p h c"), v_.ap.rearrange("p (h c) -> p h c", c=130), [v_], [V1b], v_.b.name)
            ps = nxt("tok", PS_TOK)
            tok_matmul(hT_t, col, 2560, 512, ps)
            v_ = nxt("v", vst)
            cp("act", v_.ap.rearrange("p (h c) -> p h c", c=130)[:, :, 0:128], ps.ap.rearrange("p (h c) -> p h c", c=128), [ps], [v_])
            dma("pool", MV1[:, tile, :, :], v_.ap.rearrange("p (h c) -> p h c", c=130), [v_], [MV1b], v_.b.name)
            ps = nxt("tok", PS_TOK)
            tok_matmul(hT_t, col, 3584, 8, ps)
            g_ = gates_part1(ps)
            while pend_g:
                gates_part2(*pend_g.pop(0))
            pend_g.append((g_, tile))
            if nxt_grp is not None and j == 1:
                for jj in range(4):
                    lnt_c1(jj)
            if nxt_grp is not None and j == 2:
                for jj in range(4):
                    lnt_c2(jj)
            if own:
                ps = nxt("tok", PS_TOK)
                tok_matmul(hT_t, col, 3072, 512, ps)
                e_ = nxt("e", ef)
                i = (cnt["e"] - 1) % 2
                act(e_.ap, ps.ap, AF.Exp, [ps], [e_], scale=-1.0)
                ts("dve", e_.ap, e_.ap, 1.0, None, ALU.add, None, [e_], [e_])
                recip(e_.ap, e_.ap, [e_], [e_])
                tt("dve", gst[i].ap, e_.ap, mlg.ap, ALU.mult, [e_, mlg], [gst[i]])
                dma("pool", GSO[:, tile - NTP, :], gst[i].ap, [gst[i]], [GSOb], gst[i].b.name)
        while pend_k:
            pend_k.pop(0)()
        if nxt_grp is not None:
            hT_n = hT[(nxt_grp[1] + (NGP if nxt_grp[0] else 0)) % 2]
            for jj in range(4):
                lnt_transpose(jj, hT_n, jj * 128)
        while pend_g:
            gates_part2(*pend_g.pop(0))
        ks_ = nxt("kst", kst)
        ksv = ks_.ap.rearrange("p (h t) -> p h t", t=512)
        kbase = SP if own else 0

        def mlq_post(ps, h):
            y_ = conv_block(ps, h, None)

            def fin():
                o_ = nxt("o", ost)
                act(o_.ap, y_.ap, AF.Silu, [y_], [o_])
                dma("pool", MQT[h, :, g * 512:(g + 1) * 512], o_.ap, [o_], [MQTb], o_.b.name)
            return fin

        def mlk_post(ps, h):
            y_ = conv_block(ps, 4 + h, None)

            def fin():
                act(ksv[:, h, :], y_.ap, AF.Silu, [y_], [ks_])
            return fin

        blocks = []
        for h in range(4):
            if own:
                blocks.append((h * 128, lambda ps, h=h: rope_block(ps, tab0, QT[h, :, g * 512:(g + 1) * 512], QTb)))
            blocks.append((512 + h * 128, lambda ps, h=h: rope_block(ps, tab0, KT[h, :, kbase + g * 512:kbase + (g + 1) * 512], KTb)))
        if own:
            for h in range(4):
                blocks.append((1536 + h * 128, lambda ps, h=h: mlq_post(ps, h)))
        for h in range(4):
            blocks.append((2048 + h * 128, lambda ps, h=h: mlk_post(ps, h)))
        pend_fin = []
        ps_cur = nxt("fm", PS_FM)
        fm_matmul(hT_t, blocks[0][0], ps_cur)
        for b in range(len(blocks)):
            ps_next = None
            if b + 1 < len(blocks):
                ps_next = nxt("fm", PS_FM)
                fm_matmul(hT_t, blocks[b + 1][0], ps_next)
            fin = blocks[b][1](ps_cur)
            if pend_fin:
                pend_fin.pop(0)()
            if fin is not None:
                pend_fin.append(fin)
            ps_cur = ps_next
        while pend_fin:
            pend_fin.pop(0)()
        if own:
            dma("pool", MKT[:, :, g * 512:(g + 1) * 512].rearrange("h p t -> p h t"), ksv, [ks_], [MKTb], ks_.b.name)
        def ktrans():
            pkt = PS_KT.ap.bitcast(BF16)
            for j in range(4):
                for h in range(4):
                    tr(pkt[:, h * 128:(h + 1) * 128], ksv[:, h, j * 128:(j + 1) * 128], ident.ap, [ks_, ident], [PS_KT])
                k_ = nxt("k", ktk)
                cp("dve", k_.ap, pkt[:, 0:512], [PS_KT], [k_])
                dma("pool", MK[:, tile0 + j, :], k_.ap, [k_], [MKb], k_.b.name)
        pend_k.append(ktrans)

    def halo_step():
        hT_t = hT[0]
        load_norm_transpose(x_own[0:128, :], hT_t, 0)
        for blk in range(8):
            ps = nxt("fm", PS_FM)
            fm_matmul(hT_t, (1536 if blk < 4 else 2048) + (blk % 4) * 128, ps, ntok=128)
            cp("act", zc[blk].ap[:, 0:3], ps.ap[:, 125:128], [ps], [zc[blk]])

    lnt_group(False, 0)
    for g in range(NGP):
        phaseA_group(False, g, (False, g + 1) if g + 1 < NGP else None)
    while pend_k:
        pend_k.pop(0)()
    build_tables(SP, SO)
    halo_step()
    lnt_group(True, 0)
    for g in range(NGO):
        phaseA_group(True, g, (True, g + 1) if g + 1 < NGO else None)
    while pend_k:
        pend_k.pop(0)()
    P.barrier()
    if upto == "A":
        return finish_early()

    AR.off = PERSIST
    masks = AR.alloc(4 * 1024, BF16, "masks")
    dma("sp", masks.ap, masks_d.rearrange("p j c -> p (j c)"), [], [masks], "masks")
    masksv = masks.ap.rearrange("p (j c) -> p j c", c=1024)
    kts = [AR.alloc(SA, BF16, "kts%d" % i) for i in range(2)]
    vs = [AR.alloc(NT * 130, BF16, "vs%d" % i) for i in range(2)]
    qts = [AR.alloc(SO, BF16, "qts%d" % i) for i in range(2)]
    pT = [AR.alloc(1024, BF16, "pT%d" % i) for i in range(3)]
    osb = [AR.alloc(8 * 129, F32, "osb%d" % i) for i in range(2)]
    ob = AR.alloc(128, F32, "ob")
    aob = [AR.alloc(128, BF16, "aob%d" % i) for i in range(2)]
    ast = [AR.alloc(512, BF16, "ast%d" % i) for i in range(2)]
    sc8 = [AR.alloc(8, F32, "sc8_%d" % i) for i in range(2)]
    ST = [T(None, "ST0", True), T(None, "ST1", True)]
    OB = [psum[4], psum[5], psum[6]]

    def oreg(r):
        return OB[r // 3].ap[:, (r % 3) * 129:(r % 3) * 129 + 129], OB[r // 3]

    nB = {"p": 0, "o": 0, "a": 0, "st": 0}
    zer = AR.alloc(128, BF16, "zer")
    memset("dve", zer.ap, 0.0, [zer])
    STap = [psbig[:, 0:1024], psbig[:, 1024:2048]]
    steps = [(h, G, kb) for h in range(4) for G in range(NGO) for kb in range(NTP + 4 * (G + 1))]
    hbuf = {}

    def head_bufs(h):
        if h not in hbuf:
            kt_, v_, qt_ = kts[h % 2], vs[h % 2], qts[h % 2]
            dma("sp", kt_.ap, KT[h, :, :], [KTb], [kt_], kt_.b.name)
            dma("sp", v_.ap, V1[h, :, :, :].rearrange("p t c -> p (t c)"), [V1b], [v_], v_.b.name)
            dma("sp", qt_.ap, QT[h, :, :], [QTb], [qt_], qt_.b.name)
            hbuf[h] = (kt_, v_, qt_)
        return hbuf[h]

    def emit_st(i):
        h, G, kb = steps[i]
        kt_, v_, qt_ = head_bufs(h)
        stb = ST[i % 2]
        sap = STap[i % 2]
        mm(sap[:, 0:512], kt_.ap[0:64, kb * 128:(kb + 1) * 128], qt_.ap[0:64, G * 512:(G + 1) * 512], True, True, [kt_, qt_], [stb])
        mm(sap[:, 512:1024], kt_.ap[64:128, kb * 128:(kb + 1) * 128], qt_.ap[64:128, G * 512:(G + 1) * 512], True, True, [kt_, qt_], [stb])

    def post_part2(h, G, o_, s8):
        ov = o_.ap.rearrange("p (r c) -> p r c", c=129)
        recip(s8.ap, ov[:, :, 128], [o_], [s8])
        ts("dve", s8.ap[:, 4:8], s8.ap[:, 4:8], neglam.ap, None, ALU.mult, None, [s8, neglam], [s8])
        a_ = ast[nB["a"] % 2]
        nB["a"] += 1
        pst = psum[7].ap.bitcast(BF16)
        for qb in range(4):
            ts("dve", ob.ap, ov[:, qb, 0:128], s8.ap[:, qb:qb + 1], None, ALU.mult, None, [o_, s8], [ob])
            stt(ob.ap, ov[:, 4 + qb, 0:128], s8.ap[:, 4 + qb:5 + qb], ob.ap, ALU.mult, ALU.add, [o_, s8, ob], [ob])
            stt(junk.ap[:, 0:128], ob.ap, 1.0, ob.ap, ALU.mult, ALU.mult, [ob], [junk, small], accum=small.ap[:, 8 + qb:9 + qb])
            ts("dve", small.ap[:, 12 + qb:13 + qb], small.ap[:, 8 + qb:9 + qb], 1.0 / 128, EPS, ALU.mult, ALU.add, [small], [small])
            act(small.ap[:, 16 + qb:17 + qb], small.ap[:, 12 + qb:13 + qb], AF.Ln, [small], [small])
            act(small.ap[:, 16 + qb:17 + qb], small.ap[:, 16 + qb:17 + qb], AF.Exp, [small], [small], scale=-0.5)
            ab = aob[qb % 2]
            stt(ab.ap, ob.ap, small.ap[:, 16 + qb:17 + qb], g08.ap, ALU.mult, ALU.mult, [ob, small, g08], [ab])
            tr(pst[:, qb * 128:(qb + 1) * 128], ab.ap, ident.ap, [ab, ident], [psum[7]])
        cp("dve", a_.ap, pst[:, 0:512], [psum[7]], [a_])
        dma("pool", MIXT[h * 128:(h + 1) * 128, G * 512:(G + 1) * 512], a_.ap, [a_], [MIXTb], a_.b.name)

    deferred = []
    emit_st(0)
    if len(steps) > 1:
        emit_st(1)
    for i, (h, G, kb) in enumerate(steps):
        kt_, v_, qt_ = head_bufs(h)
        vv = v_.ap.rearrange("p (t c) -> p t c", c=130)
        while deferred and deferred[0][0] <= i:
            deferred.pop(0)[1]()
        stb = ST[i % 2]
        p_ = pT[nB["p"] % 3]
        nB["p"] += 1
        j = kb - (NTP + 4 * G)
        if j > 0:
            c0 = 128 * j
            pv = p_.ap.rearrange("p (m c) -> p m c", c=512)[:, :, c0:512]
            sv = STap[i % 2].rearrange("p (m c) -> p m c", c=512)[:, :, c0:512]
            mv = masksv[:, j, :].rearrange("p (m c) -> p m c", c=512)[:, :, c0:512]
            act(pv, sv, AF.Exp, [stb, keybias], [p_], bias=keybias.ap[:, kb:kb + 1], scale=0.125)
            tt("dve", pv, pv, mv, ALU.mult, [p_, masks], [p_])
        else:
            act(p_.ap, STap[i % 2], AF.Exp, [stb, keybias], [p_], bias=keybias.ap[:, kb:kb + 1], scale=0.125)
            if j == 0:
                tt("dve", p_.ap, p_.ap, masksv[:, j, :], ALU.mult, [p_, masks], [p_])
        if i + 2 < len(steps):
            emit_st(i + 2)
        if kb == 0:
            for bnk in range(3):
                nreg = 3 if bnk < 2 else 2
                memset("dve", OB[bnk].ap[:, 0:nreg * 129], 0.0, [OB[bnk]])
        for m in range(2):
            for qb in range(4):
                last = NTP + 4 * G + qb
                if kb > last:
                    continue
                oap, obuf = oreg(m * 4 + qb)
                mm(oap, p_.ap[:, m * 512 + qb * 128:m * 512 + (qb + 1) * 128], vv[:, kb, 0:129], False, kb == last, [p_, v_], [obuf], skip=True)
        if kb == NTP + 4 * (G + 1) - 1:
            o_ = osb[nB["o"] % 2]
            s8 = sc8[nB["o"] % 2]
            nB["o"] += 1
            for bnk in range(3):
                nreg = 3 if bnk < 2 else 2
                cp("dve", o_.ap[:, bnk * 387:bnk * 387 + nreg * 129], OB[bnk].ap[:, 0:nreg * 129], [OB[bnk]], [o_])
            deferred.append((i + 6, (lambda h=h, G=G, o_=o_, s8=s8: post_part2(h, G, o_, s8))))
    while deferred:
        deferred.pop(0)[1]()
    P.barrier()
    if upto == "B":
        return finish_early()

    AR.off = PERSIST
    wo = AR.alloc(8 * DM, BF16, "wo")
    wg = AR.alloc(8 * DFF, BF16, "wg")
    wu = AR.alloc(8 * DFF, BF16, "wu")
    WD_OFF = AR.off
    wov = wo.ap.rearrange("p (k c) -> p k c", c=DM)
    wgv = wg.ap.rearrange("p (k c) -> p k c", c=DFF)
    wuv = wu.ap.rearrange("p (k c) -> p k c", c=DFF)
    pf_chunks = [(w_out[k * 128:(k + 1) * 128, :], wov[:, k, :], None) for k in range(8)]
    for k in range(8):
        for c0 in (0, 1408):
            pf_chunks.append((w_gate[k * 128:(k + 1) * 128, c0:c0 + 1408], wgv[:, k, c0:c0 + 1408], k))
            pf_chunks.append((w_up[k * 128:(k + 1) * 128, c0:c0 + 1408], wuv[:, k, c0:c0 + 1408], k))
    mks = [AR.alloc(4 * 512, BF16, "mks%d" % i) for i in range(2)]
    mvs = [AR.alloc(4 * 520, BF16, "mvs%d" % i) for i in range(2)]
    mqs = [AR.alloc(4 * 512, BF16, "mqs%d" % i) for i in range(2)]
    mkts = [AR.alloc(4 * 512, BF16, "mkts%d" % i) for i in range(2)]
    gss = [AR.alloc(4 * 512, BF16, "gss%d" % i) for i in range(2)]
    Cf = AR.alloc(4 * 129, F32, "Cf")
    Cb = AR.alloc(4 * 130, BF16, "Cb")
    kw = [AR.alloc(128, BF16, "kw%d" % i) for i in range(3)]
    scp = [AR.alloc(128, BF16, "scp%d" % i) for i in range(3)]
    hm = [AR.alloc(128, BF16, "hm%d" % i) for i in range(3)]
    hst = [AR.alloc(4 * 512, BF16, "hst%d" % i) for i in range(2)]
    pc = [AR.alloc(32, F32, "pc%d" % i) for i in range(2)]
    wstC = [AR.alloc(1408, F32, "wstC%d" % i) for i in range(4)]
    pf_state = {"dma": 0, "cvt": 0}

    def prefetch_step():
        n = pf_state["dma"]
        if n < len(pf_chunks):
            src, dst, k = pf_chunks[n]
            s_ = wstC[n % 4]
            dma("sp", s_.ap[:, 0:dst.shape[1]], src, [], [s_], s_.b.name)
            pf_state["dma"] += 1
        m = pf_state["cvt"]
        if m < len(pf_chunks) and (pf_state["dma"] - m >= 4 or pf_state["dma"] == len(pf_chunks)):
            src, dst, k = pf_chunks[m]
            s_ = wstC[m % 4]
            ncol = dst.shape[1]
            eng = "dve" if m % 2 == 0 else "act"
            if k is None:
                cp(eng, dst, s_.ap[:, 0:ncol], [s_], [wo])
            elif eng == "act":
                act(dst, s_.ap[:, 0:ncol], AF.Copy, [s_, gffn], [wo], scale=gffn.ap[:, k:k + 1])
            else:
                ts("dve", dst, s_.ap[:, 0:ncol], gffn.ap[:, k:k + 1], None, ALU.mult, None, [s_, gffn], [wo])
            pf_state["cvt"] += 1

    Cfv = Cf.ap.rearrange("p (h c) -> p h c", c=129)
    Cbv = Cb.ap.rearrange("p (h c) -> p h c", c=130)
    PS_SC = [psum[0], psum[1]]
    PS_N = [T(None, "N0", True), T(None, "N1", True)]
    PS_U = psum[6]
    PS_T = psum[7]
    nC = {"kw": 0, "scp": 0, "hm": 0, "sc": 0}

    def state_update(kk, vv4, tile, h, need_cb, mk_t, mv_t, PS_U=psum[6]):
        k_ = kw[nC["kw"] % 3]
        nC["kw"] += 1
        act(k_.ap, kk, AF.Copy, [GP, mk_t], [k_], scale=GPv[:, tile, h:h + 1])
        mm(PS_U.ap[:, 0:129], k_.ap, vv4[:, h, 0:129], True, True, [k_, mv_t], [PS_U])
        act(Cfv[:, h, :], Cfv[:, h, :], AF.Copy, [GP, Cfh[h]], [Cfh[h]], scale=GPv[:, tile, 8 + h:9 + h])
        stt(Cfv[:, h, :], PS_U.ap[:, 0:129], GPv[:, tile, 8 + h:9 + h], Cfv[:, h, :], ALU.mult, ALU.add, [PS_U, GP, Cfh[h]], [Cfh[h]])
        if need_cb:
            cp("act", Cbv[:, h, 0:129], Cfv[:, h, :], [Cfh[h]], [Cbh[h]])

    Cfh = [Buf("Cf%d" % h) for h in range(4)]
    Cbh = [Buf("Cb%d" % h) for h in range(4)]
    memset("dve", Cf.ap, 0.0, Cfh)
    memset("dve", Cb.ap, 0.0, Cbh)
    for grp in range(NGP + NGO):
        own = grp >= NGP
        go = grp - NGP
        mk_, mv_ = mks[grp % 2], mvs[grp % 2]
        dma("sp", mk_.ap.rearrange("p (t c) -> p t c", c=512), MK[:, grp * 4:(grp + 1) * 4, :], [MKb], [mk_], mk_.b.name)
        dma("sp", mv_.ap.rearrange("p (t c) -> p t c", c=520), MV1[:, grp * 4:(grp + 1) * 4, :, :].rearrange("p t h c -> p t (h c)"),
            [MV1b], [mv_], mv_.b.name)
        if own:
            mq_, mkt_, gs_, hs_ = mqs[go % 2], mkts[go % 2], gss[go % 2], hst[go % 2]
            dma("sp", mq_.ap.rearrange("p (h t) -> p h t", t=512), MQT[:, :, go * 512:(go + 1) * 512].rearrange("h p t -> p h t"), [MQTb], [mq_], mq_.b.name)
            dma("sp", mkt_.ap.rearrange("p (h t) -> p h t", t=512), MKT[:, :, go * 512:(go + 1) * 512].rearrange("h p t -> p h t"), [MKTb], [mkt_], mkt_.b.name)
            dma("sp", gs_.ap.rearrange("p (t c) -> p t c", c=512), GSO[:, go * 4:(go + 1) * 4, :], [GSOb], [gs_], gs_.b.name)
            mqv = mq_.ap.rearrange("p (h t) -> p h t", t=512)
            mktv = mkt_.ap.rearrange("p (h t) -> p h t", t=512)
            gsv = gs_.ap.rearrange("p (t c) -> p t c", c=512)
            hsv = hs_.ap.rearrange("p (h t) -> p h t", t=512)
        mkv = mk_.ap.rearrange("p (t c) -> p t c", c=512)
        mvv = mv_.ap.rearrange("p (t h c) -> p t h c", h=4, c=130)
        if own and go == 0:
            P.barrier()
            for h in range(4):
                ts("dve", Cfv[:, h, :], Cfv[:, h, :], flag.ap, None, ALU.mult, None, [Cfh[h], flag], [Cfh[h]])
                cp("act", Cbv[:, h, 0:129], Cfv[:, h, :], [Cfh[h]], [Cbh[h]])
        for j in range(4):
            tile = grp * 4 + j
            if not own:
                prefetch_step()
                prefetch_step()
                for h in range(4):
                    state_update(mkv[:, j, h * 128:(h + 1) * 128], mvv[:, j], tile, h, False, mk_, mv_, psum[(tile * 4 + h) % 7])
                continue
            nb_ = PS_N[tile % 2]
            nbank = 2 + 2 * (tile % 2)
            for h in range(4):
                sc_ = PS_SC[nC["sc"] % 2]
                nC["sc"] += 1
                mm(sc_.ap[:, 0:128], mktv[:, h, j * 128:(j + 1) * 128], mqv[:, h, j * 128:(j + 1) * 128], True, True, [mkt_, mq_], [sc_])
                s_ = scp[nC["scp"] % 3]
                nC["scp"] += 1
                stt(s_.ap, sc_.ap[:, 0:128], GPv[:, tile, h:h + 1], causal.ap, ALU.mult, ALU.mult, [sc_, GP, causal], [s_])
                nreg = psum[nbank + h // 2].ap[:, (h % 2) * 256:(h % 2) * 256 + 129]
                mm(nreg, s_.ap, mvv[:, j, h, 0:129], True, False, [s_, mv_], [nb_])
                mm(nreg, mqv[:, h, j * 128:(j + 1) * 128], Cbv[:, h, 0:129], False, True, [mq_, Cbh[h]], [nb_])
                state_update(mkv[:, j, h * 128:(h + 1) * 128], mvv[:, j], tile, h, True, mk_, mv_)
            p_ = pc[tile % 2]
            for h in range(4):
                nreg = psum[nbank + h // 2].ap[:, (h % 2) * 256:(h % 2) * 256 + 129]
                tt("dve", p_.ap[:, h:h + 1], nreg[:, 128:129], GPv[:, tile, 4 + h:5 + h], ALU.mult, [nb_, GP], [p_])
                act(junk.ap[:, 0:128], nreg[:, 0:128], AF.Square, [nb_], [junk, p_], accum=p_.ap[:, 8 + h:9 + h])
            stt(p_.ap[:, 4:8], p_.ap[:, 0:4], -1.0, p_.ap[:, 0:4], ALU.mult, ALU.max, [p_], [p_])
            ts("dve", p_.ap[:, 4:8], p_.ap[:, 4:8], 1.0, None, ALU.max, None, [p_], [p_])
            recip(p_.ap[:, 4:8], p_.ap[:, 4:8], [p_], [p_])
            tt("dve", p_.ap[:, 4:8], p_.ap[:, 4:8], GPv[:, tile, 4:8], ALU.mult, [p_, GP], [p_])
            tt("dve", p_.ap[:, 8:12], p_.ap[:, 8:12], p_.ap[:, 4:8], ALU.mult, [p_], [p_])
            tt("dve", p_.ap[:, 8:12], p_.ap[:, 8:12], p_.ap[:, 4:8], ALU.mult, [p_], [p_])
            ts("dve", p_.ap[:, 8:12], p_.ap[:, 8:12], 1.0 / 128, EPS, ALU.mult, ALU.add, [p_], [p_])
            act(p_.ap[:, 12:16], p_.ap[:, 8:12], AF.Ln, [p_], [p_])
            act(p_.ap[:, 12:16], p_.ap[:, 12:16], AF.Exp, [p_], [p_], scale=-0.5)
            tt("dve", p_.ap[:, 16:20], p_.ap[:, 12:16], p_.ap[:, 4:8], ALU.mult, [p_], [p_])
            ptr = PS_T.ap.bitcast(BF16)
            for h in range(4):
                nreg = psum[nbank + h // 2].ap[:, (h % 2) * 256:(h % 2) * 256 + 129]
                hm_ = hm[nC["hm"] % 3]
                nC["hm"] += 1
                stt(hm_.ap, nreg[:, 0:128], p_.ap[:, 16 + h:17 + h], gsv[:, j, h * 128:(h + 1) * 128], ALU.mult, ALU.mult, [nb_, p_, gs_], [hm_])
                tr(ptr[:, h * 128:(h + 1) * 128], hm_.ap, ident.ap, [hm_, ident], [PS_T])
            cp("act", hsv[:, :, j * 128:(j + 1) * 128], ptr[:, 0:512].rearrange("p (h t) -> p h t", t=128), [PS_T], [hs_])
        if own:
            dma("pool", MIXT[512:1024, go * 512:(go + 1) * 512].rearrange("(h e) t -> e h t", e=128), hsv, [hs_], [MIXTb], hs_.b.name)
    while pf_state["cvt"] < len(pf_chunks):
        prefetch_step()
    P.barrier()
    if upto == "C":
        return finish_early()

    AR.off = WD_OFF
    wd = AR.alloc(22 * DM, BF16, "wd")
    wdv = wd.ap.rearrange("p (k c) -> p k c", c=DM)
    D_MARK = AR.off
    wst = [AR.alloc(1408, F32, "wstD%d" % i) for i in range(3)]
    n = 0

    def wload(src, dst, scale_col):
        nonlocal n
        s = wst[n % 3]
        eng = ("dve", "act")[n % 2]
        ncol = dst.shape[1]
        dma("sp", s.ap[:, 0:ncol], src, [], [s], s.b.name)
        cp(eng, dst, s.ap[:, 0:ncol], [s], [wo])
        n += 1

    for k in range(22):
        wload(w_down[k * 128:(k + 1) * 128, :], wdv[:, k, :], None)
    P.barrier()
    AR.off = D_MARK
    GT = 256
    x1 = [AR.alloc(2 * DM, F32, "x1_%d" % i) for i in range(2)]
    mxs = [AR.alloc(8 * GT, BF16, "mxs%d" % i) for i in range(2)]
    h2 = AR.alloc(DM, BF16, "h2")
    h2T = AR.alloc(8 * GT, BF16, "h2T")
    aT = AR.alloc(22 * GT, BF16, "aT")
    sg = [AR.alloc(GT, BF16, "sg%d" % i) for i in range(2)]
    sD = [AR.alloc(8, F32, "sD%d" % i) for i in range(2)]
    sqd = AR.alloc(DM, BF16, "sqd")
    h2Tv = h2T.ap.rearrange("p (k t) -> p k t", t=GT)
    aTv = aT.ap.rearrange("p (k t) -> p k t", t=GT)
    PS_ACC = [psum[0], psum[1]]
    PS_GG = [psum[2], psum[3]]
    PS_UU = [psum[4], psum[5]]
    PS_TD = psum[6]
    nD = {"acc": 0, "g": 0}
    wD = [wo]
    for g in range(SO // GT):
        x1_ = x1[g % 2]
        mx_ = mxs[g % 2]
        x1v = x1_.ap.rearrange("p (j c) -> p j c", c=DM)
        mxv = mx_.ap.rearrange("p (k t) -> p k t", t=GT)
        dma("sp", mxv, MIXT[:, g * GT:(g + 1) * GT].rearrange("(k p) t -> p k t", p=128), [MIXTb], [mx_], mx_.b.name)
        dmas("sp", [(x1v[:, j, :], x_own[128 + g * GT + j * 128:128 + g * GT + (j + 1) * 128, :]) for j in range(2)], [], [x1_], x1_.b.name)
        for j in range(2):
            for c in range(2):
                acc = PS_ACC[nD["acc"] % 2]
                nD["acc"] += 1
                for k in range(8):
                    mm(acc.ap, mxv[:, k, j * 128:(j + 1) * 128], wov[:, k, c * 512:(c + 1) * 512], k == 0, k == 7, [mx_] + wD, [acc])
                tt("dve", x1v[:, j, c * 512:(c + 1) * 512], x1v[:, j, c * 512:(c + 1) * 512], acc.ap, ALU.add, [x1_, acc], [x1_])
            s_ = sD[j]
            act(sqd.ap, x1v[:, j, :], AF.Square, [x1_], [sqd, s_], accum=s_.ap[:, 0:1])
            ts("dve", s_.ap[:, 1:2], s_.ap[:, 0:1], 1.0 / DM, EPS, ALU.mult, ALU.add, [s_], [s_])
            act(s_.ap[:, 2:3], s_.ap[:, 1:2], AF.Ln, [s_], [s_])
            act(s_.ap[:, 2:3], s_.ap[:, 2:3], AF.Exp, [s_], [s_], scale=-0.5)
            ts("dve", h2.ap, x1v[:, j, :], s_.ap[:, 2:3], None, ALU.mult, None, [x1_, s_], [h2])
            ptd = PS_TD.ap.bitcast(BF16)
            for k in range(8):
                tr(ptd[:, k * 128:(k + 1) * 128], h2.ap[:, k * 128:(k + 1) * 128], ident.ap, [h2, ident], [PS_TD])
            cp("act", h2Tv[:, :, j * 128:(j + 1) * 128], ptd.rearrange("p (k t) -> p k t", t=128), [PS_TD], [h2T])
        for f in range(22):
            gg = PS_GG[f % 2]
            uu = PS_UU[f % 2]
            for k in range(8):
                mm(gg.ap[:, 0:GT], wgv[:, k, f * 128:(f + 1) * 128], h2Tv[:, k, :], k == 0, k == 7, [h2T] + wD, [gg])
            for k in range(8):
                mm(uu.ap[:, 0:GT], wuv[:, k, f * 128:(f + 1) * 128], h2Tv[:, k, :], k == 0, k == 7, [h2T] + wD, [uu])
            s_ = sg[f % 2]
            act(s_.ap, gg.ap[:, 0:GT], AF.Silu, [gg], [s_])
            tt("dve", aTv[:, f, :], s_.ap, uu.ap[:, 0:GT], ALU.mult, [s_, uu], [aT])
        for j in range(2):
            for c in range(2):
                acc = PS_ACC[nD["acc"] % 2]
                nD["acc"] += 1
                for f in range(22):
                    mm(acc.ap, aTv[:, f, j * 128:(j + 1) * 128], wdv[:, f, c * 512:(c + 1) * 512], f == 0, f == 21, [aT] + wD, [acc])
                tt("dve", x1v[:, j, c * 512:(c + 1) * 512], x1v[:, j, c * 512:(c + 1) * 512], acc.ap, ALU.add, [x1_, acc], [x1_])
            s_ = sD[j]
            act(sqd.ap, x1v[:, j, :], AF.Square, [x1_], [sqd, s_], accum=s_.ap[:, 4:5])
            ts("dve", s_.ap[:, 5:6], s_.ap[:, 4:5], 1.0 / DM, EPS, ALU.mult, ALU.add, [s_], [s_])
            act(s_.ap[:, 6:7], s_.ap[:, 5:6], AF.Ln, [s_], [s_])
            act(s_.ap[:, 6:7], s_.ap[:, 6:7], AF.Exp, [s_], [s_], scale=-0.5)
            stt(x1v[:, j, :], x1v[:, j, :], s_.ap[:, 6:7], fing.ap, ALU.mult, ALU.mult, [x1_, s_, fing], [x1_])
        dmas("pool", [(y_out[g * GT + j * 128:g * GT + (j + 1) * 128, :], x1v[:, j, :]) for j in range(2)], [x1_], [], x1_.b.name, final=True)

    P.emit(st)
    st.close()
    return nc


_INV_FREQ = (np.float32(500000.0) ** (-np.arange(0, 16, 2, dtype=np.float32) / np.float32(16.0))).astype(np.float32)


def _consts():
    bf = ml_dtypes.bfloat16
    c = {}
    c["ident"] = np.eye(128, dtype=np.float32).astype(bf)
    r = np.zeros((128, 128), np.float32)
    invf = np.zeros((128, 1), np.float32)
    sgn = np.zeros((128, 1), np.float32)
    for p in range(128):
        d = p % 64
        if d < 8:
            r[p + 8, p] = 1.0
            invf[p, 0] = _INV_FREQ[d]
            sgn[p, 0] = -1.0
        elif d < 16:
            r[p - 8, p] = 1.0
            invf[p, 0] = _INV_FREQ[d - 8]
            sgn[p, 0] = 1.0
    c["rmat"] = r.astype(bf)
    c["invf"] = invf
    c["sgn"] = sgn
    s = np.arange(128)
    c["causal"] = (s[:, None] <= s[None, :]).astype(np.float32).astype(bf)
    c["tri"] = (s[:, None] <= s[None, :]).astype(np.float32)
    m = np.zeros((128, 4, 1024), np.float32)
    q = np.arange(512)
    for j in range(4):
        kc = (j * 128 + s) // 64
        vis = (kc[:, None] <= (q[None, :] // 64)).astype(np.float32)
        m[:, j, 0:512] = vis
        m[:, j, 512:1024] = vis
    c["masks"] = m.astype(bf)
    return c


_NC_CACHE = {}


def _get_nc(NGP, NGO, dbg):
    key = (NGP, NGO, dbg)
    if key not in _NC_CACHE:
        _NC_CACHE[key] = build(NGP, NGO, dbg)
    return _NC_CACHE[key]


def make_in_maps(inputs, NGP=8, NGO=8, batches=4):
    f32 = np.float32
    x = np.asarray(inputs["x"], f32)
    positions = np.asarray(inputs["positions"], np.int32)
    SP, SO = NGP * 512, NGO * 512
    NT = (NGP + NGO) * 4
    consts = _consts()
    shared = dict(consts)
    shared["w_in"] = np.ascontiguousarray(np.asarray(inputs["w_in"], f32)[0])
    shared["w_out"] = np.ascontiguousarray(np.asarray(inputs["w_out"], f32)[0])
    shared["w_gate"] = np.ascontiguousarray(np.asarray(inputs["w_gate"], f32)[0])
    shared["w_up"] = np.ascontiguousarray(np.asarray(inputs["w_up"], f32)[0])
    shared["w_down"] = np.ascontiguousarray(np.asarray(inputs["w_down"], f32)[0])
    shared["gmix"] = np.ascontiguousarray(np.asarray(inputs["mix_norm_g"], f32)[0].reshape(8, 128).T)
    shared["gffn"] = np.ascontiguousarray(np.asarray(inputs["ffn_norm_g"], f32)[0].reshape(8, 128).T)
    shared["da_lambda"] = np.asarray(inputs["da_lambda"], f32)[0].reshape(1, 256)
    shared["da_subln_g"] = np.asarray(inputs["da_subln_g"], f32)[0].reshape(1, 128)
    cw = np.asarray(inputs["ml_conv_w"], f32)[0]
    shared["convw"] = np.ascontiguousarray(cw.reshape(4, 8, 128).transpose(2, 1, 0))
    shared["convb"] = np.ascontiguousarray(np.asarray(inputs["ml_conv_b"], f32)[0].reshape(8, 128).T)
    shared["ml_gate_b"] = np.asarray(inputs["ml_gate_b"], f32)[0].reshape(1, 8)
    shared["ml_norm_g"] = np.asarray(inputs["ml_norm_g"], f32)[0].reshape(1, 512)
    shared["final_norm_g"] = np.asarray(inputs["final_norm_g"], f32).reshape(1, DM)
    in_maps = []
    for c in range(2 * batches):
        b, g = c // 2, c % 2
        m = dict(shared)
        own0 = g * SO if g == 1 else 0
        if g == 1:
            own0 = SP
            halo = x[b, own0 - 128:own0]
        else:
            halo = np.zeros((128, DM), f32)
        m["x_own"] = np.ascontiguousarray(np.concatenate([halo, x[b, own0:own0 + SO]], axis=0))
        m["x_pre"] = np.ascontiguousarray(x[b, 0:SP])
        m["pos"] = np.ascontiguousarray(np.concatenate([positions[b, 0:SP], positions[b, own0:own0 + SO]])[None, :])
        kbias = np.zeros((128, NT), f32)
        if g == 0:
            kbias[:, 0:NGP * 4] = -30000.0
        m["keybias"] = kbias
        m["flag"] = np.full((128, 1), float(g), f32)
        in_maps.append(m)
    return in_maps


def kernel(**inputs):
    nc = _get_nc(8, 8, False)
    in_maps = make_in_maps(inputs, 8, 8, 4)
    res = run_bass_kernel_spmd(nc, in_maps, core_ids=list(range(8)))
    out = np.empty((4, 8192, DM), np.float32)
    for c in range(8):
        b, g = c // 2, c % 2
        out[b, g * 4096:(g + 1) * 4096] = res.results[c]["y"]
    return out
```
